# Optimizing a Trainium2 kernel written in Bass

```python
import math
import jax, jax.numpy as jnp
from jax import lax
import numpy as np

D_MODEL = 1024
BATCH = 32
SEQ = 2048
DEPTH = 1

SSD_HEADS = 16
SSD_HEAD_DIM = 64
SSD_WIDTH = SSD_HEADS * SSD_HEAD_DIM
SSD_GROUPS = 2
SSD_STATE = 128
CONV_WIDTH = 4
CHUNK = 128
CONV_CH = SSD_WIDTH + 2 * SSD_GROUPS * SSD_STATE
ATT_HEADS = 16
ATT_HEAD_DIM = 64
ATT_WIDTH = ATT_HEADS * ATT_HEAD_DIM
Q_BLOCK = 128
MIX_WIDTH = SSD_WIDTH + ATT_WIDTH
IN_SIZES = (SSD_WIDTH, CONV_CH, SSD_HEADS, ATT_WIDTH, ATT_WIDTH, ATT_WIDTH, ATT_HEADS)
IN_WIDTH = sum(IN_SIZES)
D_FF = 4 * D_MODEL
EPS = 1e-5

kernel_name = "hymba_ssd_fox_sqrelu_block"


def rmsnorm(x, w):
    xf = x.astype(jnp.float32)
    y = xf * lax.rsqrt(jnp.mean(xf * xf, axis=-1, keepdims=True) + EPS)
    return (y * w.astype(jnp.float32)).astype(x.dtype)


def causal_dwconv(u, w, b):
    out = lax.conv_general_dilated(
        u, w[:, None, :].astype(u.dtype), window_strides=(1,),
        padding=[(CONV_WIDTH - 1, 0)], dimension_numbers=("NWC", "WIO", "NWC"),
        feature_group_count=u.shape[-1])
    return out + b


def segsum(a):
    cum = jnp.cumsum(a, axis=-1)
    diff = cum[..., :, None] - cum[..., None, :]
    l = a.shape[-1]
    mask = jnp.tril(jnp.ones((l, l), dtype=bool))
    return jnp.where(mask, diff, -jnp.inf)


def ssd_chunked(xh, a, bmat, cmat):
    bsz, T = xh.shape[:2]
    nc = T // CHUNK
    r = SSD_HEADS // SSD_GROUPS
    f32 = jnp.float32
    x = xh.astype(f32).reshape(bsz, nc, CHUNK, SSD_GROUPS, r, SSD_HEAD_DIM)
    a = a.astype(f32).reshape(bsz, nc, CHUNK, SSD_GROUPS, r).transpose(0, 3, 4, 1, 2)
    B = bmat.astype(f32).reshape(bsz, nc, CHUNK, SSD_GROUPS, SSD_STATE)
    C = cmat.astype(f32).reshape(bsz, nc, CHUNK, SSD_GROUPS, SSD_STATE)
    a_cum = jnp.cumsum(a, axis=-1)
    ldec = jnp.exp(segsum(a))
    cb = jnp.einsum("bclgn,bcsgn->bcgls", C, B)
    y_diag = jnp.einsum("bcgls,bgrcls,bcsgrp->bclgrp", cb, ldec, x)
    decay_states = jnp.exp(a_cum[..., -1:] - a_cum)
    states = jnp.einsum("bclgn,bgrcl,bclgrp->bcgrpn", B, decay_states, x)
    chunk_decay = jnp.exp(a_cum[..., -1])

    def step(h, inp):
        s_c, d_c = inp
        return h * d_c[..., None, None] + s_c, h

    h0 = jnp.zeros((bsz, SSD_GROUPS, r, SSD_HEAD_DIM, SSD_STATE), f32)
    _, prev = lax.scan(step, h0, (jnp.moveaxis(states, 1, 0), jnp.moveaxis(chunk_decay, -1, 0)))
    y_off = jnp.einsum("bclgn,cbgrpn,bgrcl->bclgrp", C, prev, jnp.exp(a_cum))
    return (y_diag + y_off).reshape(bsz, T, SSD_HEADS, SSD_HEAD_DIM)


def forgetting_attention(q, k, v, log_f):
    T = q.shape[2]
    scale = 1.0 / math.sqrt(ATT_HEAD_DIM)
    c = jnp.cumsum(log_f, axis=-1)
    outs = []
    for i in range(T // Q_BLOCK):
        qs, qe = i * Q_BLOCK, (i + 1) * Q_BLOCK
        s = jnp.einsum("bhqd,bhkd->bhqk", q[:, :, qs:qe], k[:, :, :qe]).astype(jnp.float32) * scale
        s = s + (c[:, :, qs:qe, None] - c[:, :, None, :qe])
        mask = (qs + jnp.arange(Q_BLOCK))[:, None] >= jnp.arange(qe)[None, :]
        p = jax.nn.softmax(jnp.where(mask, s, -jnp.inf), axis=-1)
        outs.append(jnp.einsum("bhqk,bhkd->bhqd", p.astype(v.dtype), v[:, :, :qe]))
    return jnp.concatenate(outs, axis=2)


def hybrid_mixer(h, w_in, conv_w, conv_b, dt_bias, a_log, d_skip, ssd_norm_w, f_bias, w_out):
    bsz, T, _ = h.shape
    proj = jnp.einsum("btd,de->bte", h, w_in)
    idx = np.cumsum(IN_SIZES)[:-1].tolist()
    z, xbc, dt_raw, q, k, v, f_raw = jnp.split(proj, idx, axis=-1)
    xbc = jax.nn.silu(causal_dwconv(xbc, conv_w, conv_b))
    xs, bm, cm = jnp.split(xbc, [SSD_WIDTH, SSD_WIDTH + SSD_GROUPS * SSD_STATE], axis=-1)
    xs = xs.reshape(bsz, T, SSD_HEADS, SSD_HEAD_DIM)
    bm = bm.reshape(bsz, T, SSD_GROUPS, SSD_STATE)
    cm = cm.reshape(bsz, T, SSD_GROUPS, SSD_STATE)
    dt = jax.nn.softplus(dt_raw.astype(jnp.float32) + dt_bias.astype(jnp.float32))
    A = -jnp.exp(a_log.astype(jnp.float32))
    y = ssd_chunked(xs.astype(jnp.float32) * dt[..., None], A * dt, bm, cm)
    y = y + d_skip.astype(jnp.float32)[:, None] * xs.astype(jnp.float32)
    y = y.reshape(bsz, T, SSD_WIDTH) * jax.nn.silu(z.astype(jnp.float32))
    yg = y.reshape(bsz, T, SSD_GROUPS, SSD_WIDTH // SSD_GROUPS)
    yg = yg * lax.rsqrt(jnp.mean(yg * yg, axis=-1, keepdims=True) + EPS)
    y_ssd = (yg.reshape(bsz, T, SSD_WIDTH) * ssd_norm_w.astype(jnp.float32)).astype(h.dtype)
    heads = lambda t: t.reshape(bsz, T, ATT_HEADS, ATT_HEAD_DIM).transpose(0, 2, 1, 3)
    log_f = jax.nn.log_sigmoid(f_raw.astype(jnp.float32) + f_bias.astype(jnp.float32)).transpose(0, 2, 1)
    o = forgetting_attention(heads(q), heads(k), heads(v), log_f)
    y_att = o.transpose(0, 2, 1, 3).reshape(bsz, T, ATT_WIDTH).astype(h.dtype)
    return jnp.einsum("bte,ed->btd", jnp.concatenate([y_ssd, y_att], axis=-1), w_out)


def setup_inputs(seed: int = 0) -> dict:
    key = jax.random.key(seed)
    ks = jax.random.split(key, 16)
    f32 = jnp.float32
    L = DEPTH
    x = jax.random.normal(ks[0], (BATCH, SEQ, D_MODEL), f32)
    norm_mix_w = 1.0 + 0.01 * jax.random.normal(ks[1], (L, D_MODEL), f32)
    w_in = jax.random.normal(ks[2], (L, D_MODEL, IN_WIDTH), f32) * D_MODEL ** -0.5
    conv_w = jax.random.uniform(ks[3], (L, CONV_WIDTH, CONV_CH), f32, -1.0, 1.0) * CONV_WIDTH ** -0.5
    conv_b = 0.01 * jax.random.normal(ks[4], (L, CONV_CH), f32)
    dt0 = jnp.exp(jax.random.uniform(ks[5], (L, SSD_HEADS), f32, math.log(1e-3), math.log(1e-1)))
    dt_bias = dt0 + jnp.log(-jnp.expm1(-dt0))
    a_log = jnp.log(jax.random.uniform(ks[6], (L, SSD_HEADS), f32, 1.0, 16.0))
    d_skip = 1.0 + 0.01 * jax.random.normal(ks[7], (L, SSD_HEADS), f32)
    ssd_norm_w = 1.0 + 0.01 * jax.random.normal(ks[8], (L, SSD_WIDTH), f32)
    f_bias = jax.random.uniform(ks[9], (L, ATT_HEADS), f32, 1.0, 4.0)
    w_out = jax.random.normal(ks[10], (L, MIX_WIDTH, D_MODEL), f32) * MIX_WIDTH ** -0.5
    norm_mlp_w = 1.0 + 0.01 * jax.random.normal(ks[11], (L, D_MODEL), f32)
    w_up = jax.random.normal(ks[12], (L, D_MODEL, D_FF), f32) * D_MODEL ** -0.5
    w_down = jax.random.normal(ks[13], (L, D_FF, D_MODEL), f32) * D_FF ** -0.5
    norm_final_w = 1.0 + 0.01 * jax.random.normal(ks[14], (D_MODEL,), f32)
    return {"x": x, "norm_mix_w": norm_mix_w, "w_in": w_in, "conv_w": conv_w, "conv_b": conv_b,
            "dt_bias": dt_bias, "a_log": a_log, "d_skip": d_skip, "ssd_norm_w": ssd_norm_w,
            "f_bias": f_bias, "w_out": w_out, "norm_mlp_w": norm_mlp_w, "w_up": w_up,
            "w_down": w_down, "norm_final_w": norm_final_w}


def reference(x, norm_mix_w, w_in, conv_w, conv_b, dt_bias, a_log, d_skip, ssd_norm_w,
              f_bias, w_out, norm_mlp_w, w_up, w_down, norm_final_w):
    h = x
    for l in range(DEPTH):
        h = h + hybrid_mixer(rmsnorm(h, norm_mix_w[l]), w_in[l], conv_w[l], conv_b[l], dt_bias[l],
                             a_log[l], d_skip[l], ssd_norm_w[l], f_bias[l], w_out[l])
        u = jnp.square(jax.nn.relu(jnp.einsum("btd,df->btf", rmsnorm(h, norm_mlp_w[l]), w_up[l])))
        h = h + jnp.einsum("btf,fd->btd", u, w_down[l])
    return rmsnorm(h, norm_final_w)
```

```python
import numpy as np
from contextlib import ExitStack
import concourse.bass as bass
import concourse.mybir as mybir
from concourse.bass_utils import run_bass_kernel_spmd
from concourse.ap import AP

F32 = mybir.dt.float32
BF16 = mybir.dt.bfloat16
AF = mybir.ActivationFunctionType
ALU = mybir.AluOpType

NCORES = 8
D = 1024
T = 2048
NT = T // 128
EIN = 5664
EPS = 1e-5
ENGS = ("pe", "act", "dve", "pool", "sp")


class Buf:
    __slots__ = ("name", "writers", "readers", "dsem", "dcount")

    def __init__(self, name):
        self.name = name
        self.writers = []
        self.readers = []
        self.dsem = None
        self.dcount = 0


class Op:
    __slots__ = ("eng", "fn", "deps", "is_dma", "token", "signal", "dmabuf", "idx", "ndma")


class Sched:
    def __init__(self):
        self.ops = []
        self.bufs = []
        self.last = {e: None for e in ENGS}
        self.dma_since_barrier = []

    def buf(self, name):
        b = Buf(name)
        self.bufs.append(b)
        return b

    def record(self):
        self.rec = []

    def stop(self):
        r, self.rec = self.rec, None
        return r

    @staticmethod
    def merge_lists(A, Bl):
        out = []
        na, nb = len(A), len(Bl)
        ia = ib = 0
        while ia < na or ib < nb:
            if ib >= nb or (ia < na and ia * nb <= ib * na):
                out.append(A[ia]); ia += 1
            else:
                out.append(Bl[ib]); ib += 1
        return out

    def replay_merged(self, A, Bl):
        na, nb = len(A), len(Bl)
        ia = ib = 0
        while ia < na or ib < nb:
            if ib >= nb or (ia < na and ia * nb <= ib * na):
                a, kw = A[ia]; ia += 1
            else:
                a, kw = Bl[ib]; ib += 1
            self.add(*a, **kw)

    def add(self, eng, fn, reads=(), writes=(), dma=False, dmabuf=None, ndma=1, same_eng_ok=None,
            partial=False, extra_deps=()):
        if getattr(self, "rec", None) is not None:
            self.rec.append(((eng, fn), dict(reads=list(reads), writes=list(writes), dma=dma, dmabuf=dmabuf, ndma=ndma,
                                             same_eng_ok=same_eng_ok, partial=partial, extra_deps=tuple(extra_deps))))
            return None
        if same_eng_ok is None:
            same_eng_ok = (eng == "pe")
        op = Op()
        op.eng, op.fn, op.is_dma, op.dmabuf, op.ndma = eng, fn, dma, dmabuf, ndma
        op.deps, op.token, op.signal = [], None, False
        op.idx = len(self.ops)
        deps = set(extra_deps)
        for r in reads:
            deps.update(r.writers)
        for w in writes:
            if not partial:
                deps.update(w.writers)
            deps.update(w.readers)
        for r in reads:
            r.readers.append(op.idx)
        for w in writes:
            if partial:
                w.writers.append(op.idx)
            else:
                w.writers = [op.idx]
            w.readers = []
        deps.discard(op.idx)
        for d in sorted(deps):
            dop = self.ops[d]
            if same_eng_ok and dop.eng == eng and not dop.is_dma and not dma:
                continue
            op.deps.append(d)
            dop.signal = True
        self.ops.append(op)
        if fn is not None:
            self.last[eng] = op.idx
        if dma:
            assert dmabuf is not None
            self.dma_since_barrier.append(op.idx)
        return op

    def barrier(self):
        deps = [v for v in self.last.values() if v is not None] + list(self.dma_since_barrier)
        self.dma_since_barrier = []
        for e in ENGS:
            self.add(e, None, extra_deps=deps, same_eng_ok=False)


def run_sched(nc, sched):
    ops = sched.ops
    cnt = {e: 0 for e in ENGS}
    for op in ops:
        if op.is_dma:
            b = op.dmabuf
            b.dcount += 16 * op.ndma
            op.token = ("d", b, b.dcount)
        elif op.signal:
            cnt[op.eng] += 1
            op.token = ("e", op.eng, cnt[op.eng])
    dma_bufs = [b for b in sched.bufs if b.dcount > 0]
    with ExitStack() as es:
        esem = {e: es.enter_context(nc.semaphore(f"s_{e}")) for e in ENGS}
        for b in dma_bufs:
            b.dsem = es.enter_context(nc.semaphore(f"d_{b.name}"))
        block = es.enter_context(nc.Block())
        per_eng = {e: [o for o in ops if o.eng == e] for e in ENGS}

        def emit_engine(e, handle):
            waited = {}
            for op in per_eng[e]:
                need = {}
                for d in op.deps:
                    t = ops[d].token
                    if t[0] == "e":
                        key = ("e", t[1])
                        sem = esem[t[1]]
                    else:
                        key = ("d", id(t[1]))
                        sem = t[1].dsem
                    if need.get(key, (None, 0))[1] < t[2]:
                        need[key] = (sem, t[2])
                for key, (sem, val) in need.items():
                    if waited.get(key, 0) >= val:
                        continue
                    waited[key] = val
                    handle.wait_ge(sem, val)
                if op.fn is None:
                    continue
                ins = op.fn(handle)
                if op.is_dma:
                    if not isinstance(ins, (list, tuple)):
                        ins = [ins]
                    assert len(ins) == op.ndma
                    for i in ins:
                        i.then_inc(op.dmabuf.dsem, 16)
                elif op.signal:
                    ins.then_inc(esem[e], 1)
            last = {}
            for op in per_eng[e]:
                if op.is_dma:
                    last[id(op.dmabuf)] = (op.dmabuf.dsem, op.token[2])
            for key, (sem, val) in last.items():
                if waited.get(("d", key), 0) < val:
                    handle.wait_ge(sem, val)

        @block.sync
        def _(h):
            emit_engine("sp", h)

        @block.tensor
        def _(h):
            emit_engine("pe", h)

        @block.scalar
        def _(h):
            emit_engine("act", h)

        @block.vector
        def _(h):
            emit_engine("dve", h)

        @block.gpsimd
        def _(h):
            emit_engine("pool", h)
    return cnt


class Slot:
    def __init__(self, t, b, sd=None):
        self.t, self.b, self.sd = t, b, sd


class Ring:
    def __init__(self, slots):
        self.slots = slots
        self.i = 0

    def next(self):
        s = self.slots[self.i % len(self.slots)]
        self.i += 1
        return s

    def retarget(self, tensors):
        for s, t in zip(self.slots, tensors):
            s.t = t


def MM(out, lhsT, rhs, start=True, stop=True):
    return lambda h: h.matmul(out, lhsT=lhsT, rhs=rhs, start=start, stop=stop)


def TR(out, in_, ident):
    return lambda h: h.transpose(out=out, in_=in_, identity=ident)


def ACTV(out, in_, func, bias=None, scale=None, accum_out=None):
    kw = {}
    if bias is not None:
        kw["bias"] = bias
    if scale is not None:
        kw["scale"] = scale
    if accum_out is not None:
        kw["accum_out"] = accum_out
    return lambda h: h.activation(out=out, in_=in_, func=func, **kw)


def TT(out, in0, in1, op):
    return lambda h: h.tensor_tensor(out=out, in0=in0, in1=in1, op=op)


def TS(out, in0, s1, s2, op0, op1=None):
    if op1 is None:
        return lambda h: h.tensor_scalar(out=out, in0=in0, scalar1=s1, scalar2=None, op0=op0)
    return lambda h: h.tensor_scalar(out=out, in0=in0, scalar1=s1, scalar2=s2, op0=op0, op1=op1)


def STT(out, in0, scalar, in1, op0, op1):
    return lambda h: h.scalar_tensor_tensor(out=out, in0=in0, scalar=scalar, in1=in1, op0=op0, op1=op1)


def CP(out, in_):
    return lambda h: h.tensor_copy(out=out, in_=in_)


def MS(ap, val):
    return lambda h: h.memset(ap, val)


def DMA(out, in_, slow=False):
    return lambda h: h.dma_start(out=out, in_=in_, allow_slow_non_contiguous=slow)


def RCP(out, in_):
    return lambda h: h.reciprocal(out=out, in_=in_)


def RCPF(out, in_):
    return lambda h: h.reciprocal_approx_fast(out=out, in_=in_)


def SCAN(out, d0, d1, init):
    return lambda h: h.tensor_tensor_scan(out=out, data0=d0, data1=d1, initial=init, op0=ALU.add, op1=ALU.add)


def ASEL(out, in_, pattern, op, fill, base, cm):
    return lambda h: h.affine_select(out=out, in_=in_, pattern=pattern, compare_op=op, fill=fill, base=base, channel_multiplier=cm)


ARENA_BYTES = 100 * 1024


def build_nc(nseq, debug=None):
    nc = bass.Bass("TRN2", target_bir_lowering=False)
    S = Sched()

    def dram_in(name, shape):
        return nc.dram_tensor(name, list(shape), F32, kind="ExternalInput")

    x_h = dram_in("x", [nseq, T, D])
    p_norm_mix = dram_in("norm_mix_w", [1, D])
    p_w_in = dram_in("w_in", [1, D, EIN])
    p_conv_w = dram_in("conv_w", [1, 4, 1536])
    p_conv_b = dram_in("conv_b", [1, 1536])
    p_dt_bias = dram_in("dt_bias", [1, 16])
    p_a_log = dram_in("a_log", [1, 16])
    p_d_skip = dram_in("d_skip", [1, 16])
    p_ssd_norm = dram_in("ssd_norm_w", [1, D])
    p_f_bias = dram_in("f_bias", [1, 16])
    p_w_out = dram_in("w_out", [1, 2048, D])
    p_norm_mlp = dram_in("norm_mlp_w", [1, D])
    p_w_up = dram_in("w_up", [1, D, 4096])
    p_w_down = dram_in("w_down", [1, 4096, D])
    p_norm_final = dram_in("norm_final_w", [D])
    out_h = nc.dram_tensor("out", [nseq, T, D], F32, kind="ExternalOutput")
    x_d = x_h.ap()
    out_d = out_h.ap()
    W1 = nc.dram_tensor("W1s", [D, EIN], BF16, kind="Internal").ap()
    Wo = nc.dram_tensor("Wos", [2048, D], BF16, kind="Internal").ap()
    Wu = nc.dram_tensor("Wus", [D, 4096], BF16, kind="Internal").ap()
    Wd = nc.dram_tensor("Wds", [4096, D], BF16, kind="Internal").ap()
    cs8 = nc.dram_tensor("cs8s", [16, 3, T], BF16, kind="Internal").ap()
    ac6 = nc.dram_tensor("ac6s", [16, 6, T], BF16, kind="Internal").ap()
    dbg_d = None
    if debug:
        dbg_d = {k: nc.dram_tensor("dbg_" + k, list(shp), F32, kind="ExternalOutput").ap() for k, shp in debug.items()}

    def rawap(handle, offset, pat):
        return AP(handle, offset, [list(p) for p in pat])

    with ExitStack() as es:
        def sb(name, shape, dt):
            return es.enter_context(nc.sbuf_tensor(name, list(shape), dt))

        hT = sb("hT", [128, 8, T], BF16)
        yT = sb("yT", [128, 16, T], BF16)
        arena = sb("arena", [128, ARENA_BYTES // 2], BF16)
        ident = sb("ident", [128, 128], BF16)
        onesb = sb("onesb", [128, 128], BF16)
        U1 = sb("U1", [128, 128], BF16)
        U2b = sb("U2b", [128, 128], BF16)
        U2f = sb("U2f", [128, 128], F32)
        mneg = sb("mneg", [128, 128], BF16)
        mneg4 = sb("mneg4", [128, 4, 128], BF16)
        selA = sb("selA", [35, 128], BF16)
        selB = sb("selB", [35, 128], BF16)
        cf = sb("cf", [128, 128], F32)
        wn1 = sb("wn1", [128, 8], F32)
        wn2 = sb("wn2", [128, 8], F32)
        snw = sb("snw", [128, 8], F32)
        dsk = sb("dsk", [128, 8], F32)
        nfw = sb("nfw", [128, D], F32)
        cw = sb("cw", [128, 12, 4], F32)
        cb = sb("cb", [128, 12], F32)
        cbh = sb("cbh", [128, 12], F32)
        dtb_bc = sb("dtb_bc", [128, 16], F32)
        A_bc = sb("A_bc", [128, 16], F32)
        nfb = sb("nfb", [16, 1], F32)
        dtb_p = sb("dtb_p", [16, 1], F32)
        A_p = sb("A_p", [16, 1], F32)
        epsb = sb("epsb", [128, 1], F32)
        Wdtf = sb("Wdtf", [128, 8, 32], BF16)
        stat = sb("stat", [128, 4, 4], F32)
        dt_all = sb("dt_all", [128, 16, 16], F32)
        a_all = sb("a_all", [128, 16, 16], F32)
        ps = [es.enter_context(nc.psum_tensor(f"ps{i}", [128, 512], F32)) for i in range(8)]
        pb = [S.buf(f"pb{i}") for i in range(8)]
        pb4b = S.buf("pb4b")

        def pbs(bank):
            return [pb[4], pb4b] if bank == 4 else [pb[bank]]

        def carve(off, shape, dt):
            esz = 2 if dt == BF16 else 4
            n = int(np.prod(shape))
            assert off % 4 == 0 and off + n * esz <= ARENA_BYTES, (off, shape)
            v = arena[:, off // 2: off // 2 + n * esz // 2]
            if dt != BF16:
                v = v.bitcast(dt)
            if len(shape) == 2:
                v = v.rearrange("p (a b) -> p a b", a=shape[0])
            elif len(shape) == 3:
                v = v.rearrange("p (a b c) -> p a b c", a=shape[0], b=shape[1])
            return v

        B = {}

        def gb(name):
            if name not in B:
                B[name] = S.buf(name)
            return B[name]

        def ring(name, tensors):
            return Ring([Slot(t, gb(f"{name}{i}"), gb(f"{name}{i}_st")) for i, t in enumerate(tensors)])

        hTq = [gb(f"hTq{i}") for i in range(4)]
        yTb = [[gb(f"yT{e}_{q}") for q in range(4)] for e in range(16)]
        outb = [gb(f"outb{i}") for i in range(NT)]

        cst = gb("consts")

        def small_dma(dst, src, nm, slow=False):
            b_ = gb("c_" + nm)
            S.add("sp", DMA(dst, src, slow), writes=[b_], dma=True, dmabuf=b_)
            return b_

        b_wn1 = small_dma(wn1[:], rawap(p_norm_mix, 0, [[1, 128], [128, 8]]), "wn1", True)
        b_wn2 = small_dma(wn2[:], rawap(p_norm_mlp, 0, [[1, 128], [128, 8]]), "wn2", True)
        b_snw = small_dma(snw[:], rawap(p_ssd_norm, 0, [[1, 128], [128, 8]]), "snw", True)
        b_nfw = small_dma(nfw[:], rawap(p_norm_final, 0, [[0, 128], [1, D]]), "nfw")
        b_cw = gb("c_cw")
        for k in range(4):
            S.add("sp", DMA(cw[:, :, k], rawap(p_conv_w, k * 1536, [[1, 128], [128, 12]]), True),
                  writes=[b_cw], dma=True, dmabuf=b_cw, partial=(k > 0))
        b_cb = small_dma(cb[:], rawap(p_conv_b, 0, [[1, 128], [128, 12]]), "cb", True)
        b_cbh = gb("c_cbh")
        S.add("dve", TS(cbh[:], cb[:], 0.5, None, ALU.mult), reads=[b_cb], writes=[b_cbh])
        b_dsk = gb("c_dsk")
        for hh in range(2):
            S.add("sp", DMA(dsk[hh * 64:(hh + 1) * 64, :], rawap(p_d_skip, hh, [[0, 64], [2, 8]]), True),
                  writes=[b_dsk], dma=True, dmabuf=b_dsk, partial=(hh > 0))
        b_dtb = small_dma(dtb_bc[:], rawap(p_dt_bias, 0, [[0, 128], [1, 16]]), "dtb")
        b_A = small_dma(A_bc[:], rawap(p_a_log, 0, [[0, 128], [1, 16]]), "A")
        b_nfb = small_dma(nfb[:], rawap(p_f_bias, 0, [[1, 16], [1, 1]]), "nfb")
        b_dtbp = small_dma(dtb_p[:], rawap(p_dt_bias, 0, [[1, 16], [1, 1]]), "dtbp")
        b_Ap = small_dma(A_p[:], rawap(p_a_log, 0, [[1, 16], [1, 1]]), "Ap")
        S.add("act", ACTV(A_p[:], A_p[:], AF.Exp), reads=[b_Ap], writes=[b_Ap])
        S.add("dve", TS(A_p[:], A_p[:], -1.0, None, ALU.mult), reads=[b_Ap], writes=[b_Ap])
        S.add("act", ACTV(A_bc[:], A_bc[:], AF.Exp), reads=[b_A], writes=[b_A])
        S.add("dve", TS(A_bc[:], A_bc[:], -1.0, None, ALU.mult), reads=[b_A], writes=[b_A])
        S.add("dve", TS(nfb[:], nfb[:], -1.0, None, ALU.mult), reads=[b_nfb], writes=[b_nfb])
        S.add("dve", MS(epsb[:], EPS), writes=[gb("c_eps")])
        S.add("pool", MS(cf[:], 1.0), writes=[cst])
        S.add("dve", CP(onesb[:], cf[:]), reads=[cst], writes=[gb("c_ones")])
        S.add("pool", ASEL(U2f[:], cf[:], [[1, 128]], ALU.is_ge, 0.0, 0, -1), reads=[cst], writes=[gb("c_U2f")])
        S.add("dve", CP(U2b[:], U2f[:]), reads=[gb("c_U2f")], writes=[gb("c_U2b")])
        S.add("pool", ASEL(cf[:], cf[:], [[-1, 128]], ALU.is_gt, 0.0, 0, 1), reads=[cst, gb("c_ones"), gb("c_U2f")], writes=[cst])
        S.add("dve", CP(U1[:], cf[:]), reads=[cst], writes=[gb("c_U1")])
        S.add("dve", TS(mneg[:], cf[:], -30000.0, None, ALU.mult), reads=[cst], writes=[gb("c_mneg")])
        S.add("pool", MS(cf[:], 1.0), reads=[gb("c_U1"), gb("c_mneg")], writes=[cst])
        S.add("pool", ASEL(cf[:], cf[:], [[-1, 128]], ALU.is_equal, 0.0, 0, 1), reads=[cst], writes=[cst])
        S.add("dve", CP(ident[:], cf[:]), reads=[cst], writes=[gb("c_ident")])
        b_mneg4 = gb("c_mneg4")
        S.add("dve", CP(mneg4[:], mneg[:, :].unsqueeze(1).broadcast_to([128, 4, 128])), reads=[gb("c_mneg")], writes=[b_mneg4])
        b_sel = gb("c_sel")
        S.add("pool", MS(selA[:], 0.0), writes=[b_sel])
        S.add("pool", MS(selB[:], 0.0), writes=[b_sel], partial=True)
        S.add("pool", MS(selA[0:3, :], 1.0), reads=[b_sel], writes=[b_sel])
        S.add("pool", MS(selB[32:35, :], 1.0), reads=[b_sel], writes=[b_sel])
        b_eps = gb("c_eps")
        b_ident, b_ones, b_U1, b_U2b, b_U2f, b_mneg = (gb(n) for n in ["c_ident", "c_ones", "c_U1", "c_U2b", "c_U2f", "c_mneg"])

        win = p_w_in.ap()[0]
        segs = []
        for g in range(2):
            base = g * 1280
            segs.append((base, g * 512, 512))
            segs.append((base + 512, 1024 + g * 512, 512))
            segs.append((base + 1024, 2048 + g * 128, 128))
            segs.append((base + 1152, 2304 + g * 128, 128))
        for hp in range(8):
            base = 2560 + hp * 384
            segs.append((base, 2576 + hp * 128, 128))
            segs.append((base + 128, 3600 + hp * 128, 128))
            segs.append((base + 256, 4624 + hp * 128, 128))
        segs.append((5632, 2560, 16))
        segs.append((5648, 5648, 16))
        b_W1g = [gb("W1g0"), gb("W1g1")]
        b_W1hp = [gb(f"W1hp{i}") for i in range(8)]
        b_W1dtf = gb("W1dtf")
        seg_buf = []
        for g in range(2):
            seg_buf += [b_W1g[g]] * 4
        for hp in range(8):
            seg_buf += [b_W1hp[hp]] * 3
        seg_buf += [b_W1dtf, b_W1dtf]
        order_ = [32, 33] + list(range(0, 32))
        seen_ = set()
        for n_ in order_:
            d0, s0, n = segs[n_]
            bb = seg_buf[n_]
            S.add("pool", DMA(W1[:, d0:d0 + n], win[:, s0:s0 + n]), writes=[bb], dma=True, dmabuf=bb, partial=(id(bb) in seen_))
            seen_.add(id(bb))
        b_Wo, b_Wu, b_Wd = gb("Wo"), gb("Wu"), gb("Wd")
        wout = p_w_out.ap()[0]
        wup = p_w_up.ap()[0]
        wdn = p_w_down.ap()[0]
        for i in range(4):
            S.add("pool", DMA(Wo[i * 512:(i + 1) * 512, :], wout[i * 512:(i + 1) * 512, :]), writes=[b_Wo], dma=True, dmabuf=b_Wo,
                  partial=(i > 0))
        for i in range(4):
            S.add("pool", DMA(Wu[i * 256:(i + 1) * 256, :], wup[i * 256:(i + 1) * 256, :]), writes=[b_Wu], dma=True, dmabuf=b_Wu,
                  partial=(i > 0))
        for i in range(4):
            S.add("pool", DMA(Wd[i * 1024:(i + 1) * 1024, :], wdn[i * 1024:(i + 1) * 1024, :]), writes=[b_Wd], dma=True, dmabuf=b_Wd,
                  partial=(i > 0))
        W1v = W1.rearrange("(kc p) e -> p kc e", p=128)
        Wov = Wo.rearrange("(ec p) d -> p ec d", p=128)
        Wuv = Wu.rearrange("(kc p) f -> p kc f", p=128)
        Wdv = Wd.rearrange("(fc p) d -> p fc d", p=128)
        b_Wdtf = gb("Wdtf")
        S.add("sp", DMA(Wdtf[:], W1v[:, :, 5632:5664]), reads=[b_W1dtf], writes=[b_Wdtf], dma=True, dmabuf=b_Wdtf)

        stat_b = [gb(f"stat{i}") for i in range(4)]
        stat_i = [0]
        JUNK = [None]

        def rms_stats(src_ap, src_buf):
            k = stat_i[0] % 4
            stat_i[0] += 1
            sbuf_ = stat_b[k]
            S.add("act", ACTV(JUNK[0], src_ap, AF.Square, accum_out=stat[:, k, 0:1]), reads=[src_buf], writes=[sbuf_])
            S.add("act", ACTV(stat[:, k, 1:2], stat[:, k, 0:1], AF.Ln, bias=epsb[:], scale=1.0 / D), reads=[sbuf_, b_eps], writes=[sbuf_])
            S.add("act", ACTV(stat[:, k, 2:3], stat[:, k, 1:2], AF.Exp, scale=-0.5), reads=[sbuf_], writes=[sbuf_])
            return stat[:, k, 2:3], sbuf_

        def norm_transpose(src_ap, src_buf, i, wn, wn_buf, xn_slot, bank):
            rstd, sbuf_ = rms_stats(src_ap, src_buf)
            S.add("dve", TS(xn_slot.t, src_ap, rstd, None, ALU.mult), reads=[src_buf, sbuf_], writes=[xn_slot.b])
            psT = ps[bank][:].bitcast(BF16)
            for kc in range(8):
                S.add("pe", TR(psT[:, kc * 128:(kc + 1) * 128], xn_slot.t[:, kc * 128:(kc + 1) * 128], ident[:]),
                      reads=[xn_slot.b, b_ident], writes=pbs(bank), partial=(kc > 0))
            S.add("dve", TT(hT[:, :, i * 128:(i + 1) * 128], psT.rearrange("p (k t) -> p k t", k=8),
                            wn.unsqueeze(2).broadcast_to([128, 8, 128]), ALU.mult),
                  reads=pbs(bank) + [wn_buf], writes=[hTq[i // 4]], partial=True)

        def nt_a(src_ap, src_buf, xn_slot):
            rstd, sbuf_ = rms_stats(src_ap, src_buf)
            S.add("dve", TS(xn_slot.t, src_ap, rstd, None, ALU.mult), reads=[src_buf, sbuf_], writes=[xn_slot.b])

        def nt_b(i, wn, wn_buf, xn_slot, bank):
            psT = ps[bank][:].bitcast(BF16)
            for kc in range(8):
                S.add("pe", TR(psT[:, kc * 128:(kc + 1) * 128], xn_slot.t[:, kc * 128:(kc + 1) * 128], ident[:]),
                      reads=[xn_slot.b, b_ident], writes=pbs(bank), partial=(kc > 0))
            S.add("dve", TT(hT[:, :, i * 128:(i + 1) * 128], psT.rearrange("p (k t) -> p k t", k=8),
                            wn.unsqueeze(2).broadcast_to([128, 8, 128]), ALU.mult),
                  reads=pbs(bank) + [wn_buf], writes=[hTq[i // 4]], partial=True)

        def dbg_store(key, src_ap, src_bufs):
            if dbg_d is None or key not in dbg_d:
                return
            b_ = gb("dbgst_" + key)
            S.add("pool", DMA(dbg_d[key], src_ap), reads=list(src_bufs), dma=True, dmabuf=b_)

        for b in range(nseq):
            xr = ring("xr", [carve(i * 4096, [1024], F32) for i in range(3)])
            xnr = ring("xn", [carve(12288 + i * 2048, [1024], BF16) for i in range(2)])
            JUNK[0] = carve(16384, [1024], BF16)
            for i in range(NT):
                xs = xr.next()
                S.add("sp", DMA(xs.t, x_d[b, i * 128:(i + 1) * 128, :]), writes=[xs.b], dma=True, dmabuf=xs.b)
                norm_transpose(xs.t, xs.b, i, wn1[:, :], b_wn1, xnr.next(), i % 2)
            if b == 0:
                dbg_store("hT", hT[:, :, :], hTq)

            OFF0 = 18432
            tmp256 = carve(OFF0, [256], F32)
            fe = carve(OFF0 + 1024, [512], F32)
            csq = [carve(OFF0 + 3072 + i * 2048, [512], F32) for i in range(2)]
            rr = carve(OFF0 + 7168, [512], F32)
            zer = carve(OFF0 + 9216, [512], F32)
            SPq = carve(OFF0 + 11264, [3, 512], BF16)
            b_t256, b_fe, b_rr, b_zer, b_SPq = gb("t256"), gb("fe"), gb("rr"), gb("zer"), gb("SPq")
            b_csq = [gb("csq0"), gb("csq1")]
            b_dt, b_a = gb("dt_all"), gb("a_all")
            b_cs8 = gb("cs8")
            ae = carve(OFF0 + 14336, [512], F32)
            acs = [carve(OFF0 + 16384 + i * 2048, [512], F32) for i in range(2)]
            acm = carve(OFF0 + 20480, [512], F32)
            rr2 = carve(OFF0 + 22528, [512], F32)
            bs4 = carve(OFF0 + 24576, [4], F32)
            SA = carve(OFF0 + 24592, [6, 512], BF16)
            b_ae, b_acm, b_rr2, b_bs4, b_SA = gb("ae"), gb("acm"), gb("rr2"), gb("bs4"), gb("SA")
            b_acs = [gb("acs0"), gb("acs1")]
            b_ac6 = gb("ac6")
            for c in range(NT):
                for kc in range(8):
                    S.add("pe", MM(ps[2][:, c * 16:(c + 1) * 16], hT[:, kc, c * 128:(c + 1) * 128], Wdtf[:, kc, 0:16], kc == 0, kc == 7),
                          reads=[hTq[c // 4], b_Wdtf], writes=[pb[2]], partial=not (c == 0 and kc == 0))
            S.add("dve", TT(tmp256.rearrange("p (c h) -> p c h", c=16), ps[2][:, 0:256].rearrange("p (c h) -> p c h", c=16),
                            dtb_bc[:, :].unsqueeze(1).broadcast_to([128, 16, 16]), ALU.add),
                  reads=[pb[2], b_dtb], writes=[b_t256])
            S.add("act", ACTV(tmp256, tmp256, AF.Exp), reads=[b_t256], writes=[b_t256])
            S.add("act", ACTV(dt_all[:].rearrange("p c h -> p (c h)"), tmp256, AF.Ln, bias=1.0), reads=[b_t256], writes=[b_dt])
            S.add("dve", TT(a_all[:], dt_all[:], A_bc[:, :].unsqueeze(1).broadcast_to([128, 16, 16]), ALU.mult),
                  reads=[b_dt, b_A], writes=[b_a])
            S.add("dve", MS(zer[0:16, :], 0.0), writes=[b_zer])
            for tq in range(4):
                bank = 3 + tq
                for kc in range(8):
                    S.add("pe", MM(ps[bank][0:16, :], Wdtf[:, kc, 16:32], hT[:, kc, tq * 512:(tq + 1) * 512], kc == 0, kc == 7),
                          reads=[hTq[tq], b_Wdtf], writes=pbs(bank), partial=(kc > 0))
                S.add("act", ACTV(fe[0:16, :], ps[bank][0:16, :], AF.Exp, bias=nfb[:, 0:1], scale=-1.0),
                      reads=pbs(bank) + [b_nfb], writes=[b_fe])
                S.add("act", ACTV(fe[0:16, :], fe[0:16, :], AF.Ln, bias=1.0), reads=[b_fe], writes=[b_fe])
                cur, prv = csq[tq % 2], csq[(tq + 1) % 2]
                bcur, bprv = b_csq[tq % 2], b_csq[(tq + 1) % 2]
                if tq == 0:
                    S.add("dve", SCAN(cur[0:16, :], fe[0:16, :], zer[0:16, :], 0.0), reads=[b_fe, b_zer], writes=[bcur])
                else:
                    S.add("dve", SCAN(cur[0:16, :], fe[0:16, :], zer[0:16, :], prv[0:16, 511:512]),
                          reads=[b_fe, b_zer, bprv], writes=[bcur])
                S.add("dve", TS(SPq[0:16, 0, :], cur[0:16, :], 8.0, None, ALU.mult), reads=[bcur], writes=[b_SPq])
                S.add("dve", STT(rr[0:16, :], cur[0:16, :], 8.0, SPq[0:16, 0, :], ALU.mult, ALU.subtract),
                      reads=[bcur, b_SPq], writes=[b_rr])
                S.add("dve", CP(SPq[0:16, 1, :], rr[0:16, :]), reads=[b_rr], writes=[b_SPq], partial=True)
                S.add("dve", TT(rr[0:16, :], rr[0:16, :], SPq[0:16, 1, :], ALU.subtract), reads=[b_rr, b_SPq], writes=[b_rr])
                S.add("dve", CP(SPq[0:16, 2, :], rr[0:16, :]), reads=[b_rr], writes=[b_SPq], partial=True)
                S.add("sp", DMA(cs8[:, :, tq * 512:(tq + 1) * 512], SPq[0:16, :, :]),
                      reads=[b_SPq], writes=[b_cs8], dma=True, dmabuf=gb("SPq_st"), partial=(tq > 0))
                for kc in range(8):
                    S.add("pe", MM(ps[7][0:16, :], Wdtf[:, kc, 0:16], hT[:, kc, tq * 512:(tq + 1) * 512], kc == 0, kc == 7),
                          reads=[hTq[tq], b_Wdtf], writes=[pb[7]], partial=(kc > 0))
                S.add("act", ACTV(ae[0:16, :], ps[7][0:16, :], AF.Exp, bias=dtb_p[:, 0:1]), reads=[pb[7], b_dtbp], writes=[b_ae])
                S.add("act", ACTV(ae[0:16, :], ae[0:16, :], AF.Ln, bias=1.0), reads=[b_ae], writes=[b_ae])
                S.add("dve", TS(ae[0:16, :], ae[0:16, :], A_p[:, 0:1], None, ALU.mult), reads=[b_ae, b_Ap], writes=[b_ae])
                acur, aprv = acs[tq % 2], acs[(tq + 1) % 2]
                bacur, baprv = b_acs[tq % 2], b_acs[(tq + 1) % 2]
                if tq == 0:
                    S.add("dve", SCAN(acur[0:16, :], ae[0:16, :], zer[0:16, :], 0.0), reads=[b_ae, b_zer], writes=[bacur])
                    S.add("dve", MS(bs4[0:16, 0:1], 0.0), writes=[b_bs4])
                else:
                    S.add("dve", SCAN(acur[0:16, :], ae[0:16, :], zer[0:16, :], aprv[0:16, 511:512]),
                          reads=[b_ae, b_zer, baprv], writes=[bacur])
                    S.add("dve", CP(bs4[0:16, 0:1], aprv[0:16, 511:512]), reads=[baprv], writes=[b_bs4])
                S.add("dve", CP(bs4[0:16, 1:4], acur[0:16, :].rearrange("p (c l) -> p c l", c=4)[:, 0:3, 127]),
                      reads=[bacur], writes=[b_bs4], partial=True)
                S.add("dve", TT(acm[0:16, :].rearrange("p (c l) -> p c l", c=4), acur[0:16, :].rearrange("p (c l) -> p c l", c=4),
                                bs4[0:16, 0:4].unsqueeze(2).broadcast_to([16, 4, 128]), ALU.subtract),
                      reads=[bacur, b_bs4], writes=[b_acm])
                S.add("dve", CP(SA[0:16, 0, :], acm[0:16, :]), reads=[b_acm], writes=[b_SA])
                S.add("dve", TT(rr2[0:16, :], acm[0:16, :], SA[0:16, 0, :], ALU.subtract), reads=[b_acm, b_SA], writes=[b_rr2])
                S.add("dve", CP(SA[0:16, 1, :], rr2[0:16, :]), reads=[b_rr2], writes=[b_SA], partial=True)
                S.add("dve", TT(rr2[0:16, :], rr2[0:16, :], SA[0:16, 1, :], ALU.subtract), reads=[b_rr2, b_SA], writes=[b_rr2])
                S.add("dve", CP(SA[0:16, 2, :], rr2[0:16, :]), reads=[b_rr2], writes=[b_SA], partial=True)
                S.add("dve", TS(SA[0:16, 3:6, :], SA[0:16, 0:3, :], -1.0, None, ALU.mult), reads=[b_SA], writes=[b_SA], partial=True)
                S.add("sp", DMA(ac6[:, :, tq * 512:(tq + 1) * 512], SA[0:16, :, :]),
                      reads=[b_SA], writes=[b_ac6], dma=True, dmabuf=gb("SA_st"), partial=(tq > 0))
            S.barrier()

            yTu = yT[:, 8:16, :].rearrange("p a b -> p (a b)")

            def carve2(off, shape, dt):
                esz = 2 if dt == BF16 else 4
                n = int(np.prod(shape))
                assert off % 4 == 0 and off + n * esz <= 32768, (off, shape)
                v = yTu[:, off // 2: off // 2 + n * esz // 2]
                if dt != BF16:
                    v = v.bitcast(dt)
                if len(shape) == 2:
                    v = v.rearrange("p (a b) -> p a b", a=shape[0])
                return v

            O = 18432
            Wg = carve(O, [8, 1280], BF16); O += 20480
            gz_r = [carve(O + i * 4096, [4, 512], BF16) for i in range(2)]; O += 8192
            pexb = [carve(O + i * 1040, [520], BF16) for i in range(2)]; O += 2080
            xbc_r = [carve(O + i * 6144, [6, 512], BF16) for i in range(2)]; O += 12288
            hlb = carve(O, [6, 4], BF16); O += 64
            Dg = carve(O, [24, 128], BF16); O += 6144
            prevF = carve(O, [512], F32); O += 2048
            prevB = carve(O, [512], BF16); O += 1024
            yg = carve(O, [4, 512], F32); O += 8192
            sq = carve(O, [4, 512], BF16); O += 4096
            rstd_bc = carve(O, [512], F32); O += 2048
            lnv_bc = carve(O, [512], F32); O += 2048
            LT_r = [carve(O + i * 2048, [8, 128], BF16) for i in range(2)]; O += 4096
            MT_r = [carve(O + i * 2048, [8, 128], BF16) for i in range(2)]; O += 4096
            tnb = [carve(O + i * 1024, [512], BF16) for i in range(2)]; O += 2048
            hvb = [carve(O + i * 1024, [512], BF16) for i in range(2)]; O += 2048
            b_tnb, b_hvb = [gb("tnb0"), gb("tnb1")], [gb("hvb0"), gb("hvb1")]
            Dd = carve(O, [8, 128], BF16); O += 2048
            dsp = carve(O, [16], BF16); O += 32
            dsr = carve(O, [8], F32); O += 32
            assert O <= ARENA_BYTES, O
            O2 = 0
            Ebc_r = [carve2(O2 + i * 4096, [8, 128], F32) for i in range(2)]; O2 += 8192
            CpT_r = [carve2(O2 + i * 2048, [8, 128], BF16) for i in range(2)]; O2 += 4096
            xdt_r = [carve2(O2 + i * 1024, [512], BF16) for i in range(2)]; O2 += 2048
            xdtD_r = [carve2(O2 + i * 1024, [512], BF16) for i in range(2)]; O2 += 2048
            Btok_r = [carve2(O2 + i * 256, [128], BF16) for i in range(2)]; O2 += 512
            CBm_r = [carve2(O2 + i * 256, [128], BF16) for i in range(2)]; O2 += 512
            Rr = carve2(O2, [8, 512], BF16); O2 += 8192
            assert O2 <= 32768
            b_Rr = gb("Rr")
            b_Wg, b_hl, b_Dg = gb("Wg"), gb("hl"), gb("Dg")
            b_gz = [gb("gz0"), gb("gz1")]
            b_xbc = [gb("xbcT0"), gb("xbcT1")]
            b_pex = [gb("pex0"), gb("pex1")]
            b_prevF, b_prevB, b_yg, b_sq, b_rstd, b_lnv = (gb(n) for n in ["prevF", "prevB", "yg", "sq", "rstd_bc", "lnv_bc"])

            def r2(nm):
                return [gb(nm + "0"), gb(nm + "1")]
            b_LT, b_MT, b_Ebc, b_CpT, b_xdt, b_xdtD, b_Btok, b_CBm = (
                r2(n) for n in ["LT", "MT", "Ebc", "CpT", "xdt", "xdtD", "Btok", "CBm"])
            pexi = 0
            tni = 0
            pbk = 0
            h8 = "p (h q) -> p h q"
            for g in range(2):
                S.add("sp", DMA(Wg, W1v[:, :, g * 1280:(g + 1) * 1280]), reads=[b_W1g[g]], writes=[b_Wg], dma=True, dmabuf=b_Wg)
                S.add("dve", MS(hlb, 0.0), writes=[b_hl])
                S.add("dve", MS(prevF, 0.0), writes=[b_prevF])
                S.add("dve", MS(prevB, 0.0), writes=[b_prevB])
                for ci in range(6):
                    cc = (g * 4 + ci) if ci < 4 else (8 + g if ci == 4 else 10 + g)
                    for k in range(4):
                        S.add("dve", TS(Dg[:, ci * 4 + k, :], ident[:, :], cw[:, cc, k:k + 1], None, ALU.mult),
                              reads=[b_ident, b_cw], writes=[b_Dg], partial=not (ci == 0 and k == 0))
                b_dsp, b_Dd = gb("dsp"), gb("Dd")
                S.add("dve", CP(dsp[:, 0:4], dsk[:, g * 4:(g + 1) * 4]), reads=[b_dsk], writes=[b_dsp])
                S.add("dve", TT(dsr[:, 0:4], dsk[:, g * 4:(g + 1) * 4], dsp[:, 0:4], ALU.subtract), reads=[b_dsk, b_dsp], writes=[gb("dsr")])
                S.add("dve", CP(dsp[:, 4:8], dsr[:, 0:4]), reads=[gb("dsr")], writes=[b_dsp], partial=True)
                for ec in range(4):
                    for j in range(2):
                        S.add("dve", TS(Dd[:, ec * 2 + j, :], ident[:, :], dsp[:, j * 4 + ec:j * 4 + ec + 1], None, ALU.mult),
                              reads=[b_ident, b_dsp], writes=[b_Dd], partial=not (ec == 0 and j == 0))

                def emit_inproj(tq, lo=0, hi=10):
                    nonlocal pexi, pbk, tni
                    q2 = tq % 2
                    tsl = slice(tq * 512, (tq + 1) * 512)
                    xbcT, gz_t = xbc_r[q2], gz_r[q2]
                    order = [(4 + j, j) for j in range(6)] + [(j, None) for j in range(4)]
                    deferred = []
                    for blk, ci in order[lo:hi]:
                        bank = pbk % 3
                        pbk += 1
                        for kc in range(8):
                            S.add("pe", MM(ps[bank][:, :], Wg[:, kc, blk * 128:(blk + 1) * 128], hT[:, kc, tsl], kc == 0, kc == 7),
                                  reads=[b_Wg, hTq[tq]], writes=[pb[bank]], partial=(kc > 0))
                        ti = tni % 2
                        tni += 1
                        if ci is None:
                            S.add("act", ACTV(tnb[ti], ps[bank][:, :], AF.Tanh, scale=0.5), reads=[pb[bank]], writes=[b_tnb[ti]])
                            S.add("act", ACTV(hvb[ti], ps[bank][:, :], AF.Copy, scale=0.5), reads=[pb[bank]], writes=[b_hvb[ti]])
                            for f_ in deferred:
                                f_()
                            deferred = [lambda ti=ti, blk=blk: S.add("dve", STT(gz_t[:, blk, :], tnb[ti], 1.0, hvb[ti], ALU.add, ALU.mult),
                                                                   reads=[b_tnb[ti], b_hvb[ti]], writes=[b_gz[q2]], partial=(blk > 0))]
                            continue
                        cc = (g * 4 + ci) if ci < 4 else (8 + g if ci == 4 else 10 + g)
                        pi = pexi % 2
                        pexi += 1
                        px, bpx = pexb[pi], b_pex[pi]
                        S.add("dve", CP(px[:, 0:3], hlb[:, ci, 0:3]), reads=[b_hl], writes=[bpx])
                        S.add("act", ACTV(px[:, 3:515], ps[bank][:, :], AF.Copy), reads=[pb[bank]], writes=[bpx], partial=True)
                        bank2 = pbk % 3
                        pbk += 1
                        for k in range(4):
                            S.add("pe", MM(ps[bank2][:, :], Dg[:, ci * 4 + k, :], px[:, k:k + 512], k == 0, k == 3),
                                  reads=[bpx, b_Dg], writes=[pb[bank2]], partial=(k > 0))
                        S.add("act", ACTV(tnb[ti], ps[bank2][:, :], AF.Tanh, bias=cbh[:, cc:cc + 1], scale=0.5),
                              reads=[pb[bank2], b_cbh], writes=[b_tnb[ti]])
                        S.add("act", ACTV(hvb[ti], ps[bank2][:, :], AF.Identity, bias=cbh[:, cc:cc + 1], scale=0.5),
                              reads=[pb[bank2], b_cbh], writes=[b_hvb[ti]])
                        for f_ in deferred:
                            f_()
                        deferred = [
                            lambda px=px, bpx=bpx, ci=ci: S.add("dve", CP(hlb[:, ci, 0:3], px[:, 512:515]), reads=[bpx], writes=[b_hl], partial=True),
                            lambda ti=ti, ci=ci: S.add("dve", STT(xbcT[:, ci, :], tnb[ti], 1.0, hvb[ti], ALU.add, ALU.mult),
                                                       reads=[b_tnb[ti], b_hvb[ti]], writes=[b_xbc[q2]], partial=(ci > 0))]
                    for f_ in deferred:
                        f_()

                def emit_rows(tq):
                    tsl = slice(tq * 512, (tq + 1) * 512)
                    import os as _os3
                    if _os3.environ.get("KDBG_NOROWDMA") == "1":
                        return
                    if tq == 0:
                        S.add("pool", MS(Rr[0:35, :, :], 0.0), writes=[b_Rr])
                    S.add("sp", DMA(Rr[0:3, :, :], ac6[g * 8:(g + 1) * 8, 0:3, tsl].rearrange("h k t -> k h t")),
                          reads=[b_ac6], writes=[b_Rr], dma=True, dmabuf=b_Rr)
                    S.add("sp", DMA(Rr[32:35, :, :], ac6[g * 8:(g + 1) * 8, 3:6, tsl].rearrange("h k t -> k h t")),
                          reads=[b_ac6], writes=[b_Rr], dma=True, dmabuf=b_Rr)

                def emit_S1(c):
                    tq, cl, k = c // 4, c % 4, c % 2
                    q2 = tq % 2
                    xbcT = xbc_r[q2]
                    csl = slice(cl * 128, (cl + 1) * 128)
                    dt_c = dt_all[:, c, g * 8:(g + 1) * 8]
                    LT, MT, Ebc, CpT = LT_r[k], MT_r[k], Ebc_r[k], CpT_r[k]
                    xdt, xdtD, Btok, CBm = xdt_r[k], xdtD_r[k], Btok_r[k], CBm_r[k]
                    if cl == 0:
                        emit_rows(tq)
                    for hh in range(2):
                        rv = Rr[0:35, hh * 4:(hh + 1) * 4, csl]
                        p3v = ps[3][:, :].rearrange("p (a b) -> p a b", a=4)
                        p4v = ps[4][:, :].rearrange("p (a b) -> p a b", a=4)
                        S.add("pe", MM(p3v, selA[0:35, :], rv, True, False), reads=[b_Rr, b_sel], writes=[pb[3]])
                        S.add("pe", MM(p4v, selA[0:35, :], rv, True, True), reads=[b_Rr, b_sel], writes=pbs(4))
                        for h4 in range(4):
                            hd = hh * 4 + h4
                            osl = slice(h4 * 128, (h4 + 1) * 128)
                            S.add("pe", MM(ps[3][:, osl], Rr[0:35, hd, csl], selB[0:35, :], False, False),
                                  reads=[b_Rr, b_sel], writes=[pb[3]], partial=True)
                        S.add("pe", MM(p3v, ident[:, :], mneg4[:, :, :], False, True), reads=[b_ident, b_mneg4], writes=[pb[3]], partial=True)
                        S.add("act", ACTV(LT[:, hh * 4:(hh + 1) * 4, :].rearrange("p a b -> p (a b)"), ps[3][:, :], AF.Exp),
                              reads=[pb[3]], writes=[b_LT[k]], partial=(hh > 0))
                        S.add("act", ACTV(Ebc[:, hh * 4:(hh + 1) * 4, :].rearrange("p a b -> p (a b)"), ps[4][:, :], AF.Exp),
                              reads=pbs(4), writes=[b_Ebc[k]], partial=(hh > 0))
                    psT = ps[5][:].bitcast(BF16)
                    for xi in range(5):
                        S.add("pe", TR(psT[:, xi * 128:(xi + 1) * 128], xbcT[:, xi, csl], ident[:]),
                              reads=[b_xbc[q2], b_ident], writes=[pb[5]], partial=(xi > 0))
                    S.add("pe", MM(ps[5][:, 384:512], xbcT[:, 4, csl], xbcT[:, 5, csl]), reads=[b_xbc[q2]], writes=[pb[5]], partial=True)
                    S.add("dve", TT(xdt.rearrange(h8, h=8), psT[:, 0:512].rearrange(h8, h=8),
                                    dt_c.unsqueeze(2).broadcast_to([128, 8, 64]), ALU.mult),
                          reads=[pb[5], b_dt], writes=[b_xdt[k]])
                    S.add("dve", CP(Btok, psT[:, 512:640]), reads=[pb[5]], writes=[b_Btok[k]])
                    S.add("dve", TT(CBm, ps[5][:, 384:512], U2f[:, :], ALU.mult), reads=[pb[5], b_U2f], writes=[b_CBm[k]])
                    S.add("dve", TT(xdtD.rearrange(h8, h=8), xdt.rearrange(h8, h=8),
                                    LT[:, :, 127:128].broadcast_to([128, 8, 64]), ALU.mult),
                          reads=[b_xdt[k], b_LT[k]], writes=[b_xdtD[k]])
                    S.add("dve", TT(MT, LT, CBm.unsqueeze(1).broadcast_to([128, 8, 128]), ALU.mult),
                          reads=[b_LT[k], b_CBm[k]], writes=[b_MT[k]])
                    S.add("dve", TT(CpT, Ebc, xbcT[:, 5, csl].unsqueeze(1).broadcast_to([128, 8, 128]), ALU.mult),
                          reads=[b_Ebc[k], b_xbc[q2]], writes=[b_CpT[k]])

                def emit_S2(c):
                    tq, cl, k = c // 4, c % 4, c % 2
                    q2 = tq % 2
                    xbcT, gz_t = xbc_r[q2], gz_r[q2]
                    csl = slice(cl * 128, (cl + 1) * 128)
                    MT, Ebc, CpT = MT_r[k], Ebc_r[k], CpT_r[k]
                    xdt, xdtD, Btok = xdt_r[k], xdtD_r[k], Btok_r[k]
                    S.add("pe", MM(ps[7][:, :], Btok, xdtD), reads=[b_Btok[k], b_xdtD[k]], writes=[pb[7]])
                    for pr in range(4):
                        psl = slice(pr * 128, (pr + 1) * 128)
                        for j in range(2):
                            S.add("pe", MM(ps[6][:, psl], Dd[:, pr * 2 + j, :], xbcT[:, pr, csl], j == 0, False),
                                  reads=[b_xbc[q2], b_Dd], writes=[pb[6]], partial=not (pr == 0 and j == 0))
                        for hx in range(2):
                            hd, r0 = pr * 2 + hx, hx * 64
                            S.add("pe", MM(ps[6][r0:r0 + 64, psl], xdt[:, hd * 64:(hd + 1) * 64], MT[:, hd, :], False, False),
                                  reads=[b_xdt[k], b_MT[k]], writes=[pb[6]], partial=True)
                            S.add("pe", MM(ps[6][r0:r0 + 64, psl], prevB[:, hd * 64:(hd + 1) * 64], CpT[:, hd, :], False, True),
                                  reads=[b_prevB, b_CpT[k]], writes=[pb[6]], partial=True)
                    S.add("dve", TT(prevF.rearrange(h8, h=8), prevF.rearrange(h8, h=8),
                                    Ebc[:, :, 127:128].broadcast_to([128, 8, 64]), ALU.mult),
                          reads=[b_prevF, b_Ebc[k]], writes=[b_prevF])
                    S.add("dve", TT(prevF, prevF, ps[7][:, :], ALU.add), reads=[b_prevF, pb[7]], writes=[b_prevF])
                    S.add("pool", CP(prevB, prevF), reads=[b_prevF], writes=[b_prevB])
                    S.add("dve", TT(yg[:, :, csl], ps[6][:, :].rearrange("p (a b) -> p a b", a=4), gz_t[:, :, csl], ALU.mult),
                          reads=[pb[6], b_gz[q2]], writes=[b_yg], partial=True)

                def emit_norm(tq):
                    tsl = slice(tq * 512, (tq + 1) * 512)
                    for ec in range(4):
                        S.add("act", ACTV(sq[:, ec, :], yg[:, ec, :], AF.Square), reads=[b_yg], writes=[b_sq], partial=(ec > 0))
                    for ec in range(4):
                        S.add("pe", MM(ps[7][:, :], onesb[:, :], sq[:, ec, :], ec == 0, ec == 3),
                              reads=[b_sq, b_ones], writes=[pb[7]], partial=(ec > 0))
                    S.add("act", ACTV(lnv_bc, ps[7][:, :], AF.Ln, bias=epsb[:], scale=1.0 / 512), reads=[pb[7], b_eps], writes=[b_lnv])
                    S.add("act", ACTV(rstd_bc, lnv_bc, AF.Exp, scale=-0.5), reads=[b_lnv], writes=[b_rstd])
                    for ec in range(4):
                        e_ = g * 4 + ec
                        S.add("dve", STT(yT[:, e_, tsl], yg[:, ec, :], snw[:, e_:e_ + 1], rstd_bc, ALU.mult, ALU.mult),
                              reads=[b_yg, b_rstd, b_snw], writes=[yTb[e_][tq]])

                import os as _os
                _nointer = _os.environ.get("KDBG_NOINTER") == "1"
                emit_inproj(0)
                pieces = [(0, 3), (3, 6), (6, 8), (8, 10)]
                for c in range(NT + 1):
                    S.record()
                    if c < NT:
                        emit_S1(c)
                    strA1 = S.stop()
                    S.record()
                    if c >= 1:
                        emit_S2(c - 1)
                        if (c - 1) % 4 == 3:
                            emit_norm((c - 1) // 4)
                    strA2 = S.stop()
                    strA = strA1 + strA2
                    S.record()
                    if c < NT and c // 4 + 1 < 4:
                        lo_, hi_ = pieces[c % 4]
                        emit_inproj(c // 4 + 1, lo_, hi_)
                    strB = S.stop()
                    if c % 4 == 0:
                        S.replay_merged(strA, [])
                        S.replay_merged([], strB)
                    else:
                        S.replay_merged(strA, strB)
            if b == 0:
                dbg_store("yssd", yT[:, 0:8, :], [yTb[e][q] for e in range(8) for q in range(4)])
            S.barrier()

            O = 0
            Whp = [carve(O + i * 6144, [8, 384], BF16) for i in range(2)]; O += 12288
            Qa = [[carve(O + (s * 2 + hd) * 4096, [T], BF16) for hd in range(2)] for s in range(2)]; O += 16384
            Ka = [[carve(O + (s * 2 + hd) * 4096, [T], BF16) for hd in range(2)] for s in range(2)]; O += 16384
            Va = [carve(O + s * 8192, [16, 2, 128], BF16) for s in range(2)]
            Va3 = [carve(O + s * 8192, [32, 128], BF16) for s in range(2)]; O += 16384
            PT = [carve(O + i * 1024, [512], BF16) for i in range(3)]; O += 3072
            rec = [carve(O + i * 2048, [512], F32) for i in range(2)]; O += 4096
            WoutT = carve(O, [16, 1024], BF16); O += 32768
            assert O <= ARENA_BYTES
            b_Whp = [gb("Whp0"), gb("Whp1")]
            b_Qa = [[gb(f"Qa{s}{hd}") for hd in range(2)] for s in range(2)]
            b_Ka = [[gb(f"Ka{s}{hd}") for hd in range(2)] for s in range(2)]
            b_Va = [gb("Va0"), gb("Va1")]
            b_PT = [gb(f"PT{i}") for i in range(3)]
            b_rec = [gb("rec0"), gb("rec1")]
            b_WoutT = gb("WoutT")
            S.add("sp", DMA(WoutT, Wov), reads=[b_Wo], writes=[b_WoutT], dma=True, dmabuf=b_WoutT)
            pti = 0
            poi = 0
            def c_inproj(hp):
                s = hp % 2
                S.add("sp", DMA(Whp[s], W1v[:, :, 2560 + hp * 384: 2560 + (hp + 1) * 384]),
                      reads=[b_W1hp[hp]], writes=[b_Whp[s]], dma=True, dmabuf=b_Whp[s])
                for hd in range(2):
                    head = 2 * hp + hd
                    S.add("pool", MS(Ka[s][hd][64:70, :], -1.0), writes=[b_Ka[s][hd]])
                    S.add("pool", MS(Qa[s][hd][64:70, :], 1.0), writes=[b_Qa[s][hd]])
                    S.add("sp", DMA(Ka[s][hd][64:67, :], cs8[head:head + 1, :, :]),
                          reads=[b_cs8], writes=[b_Ka[s][hd]], dma=True, dmabuf=b_Ka[s][hd])
                    S.add("sp", DMA(Qa[s][hd][67:70, :], cs8[head:head + 1, :, :]),
                          reads=[b_cs8], writes=[b_Qa[s][hd]], dma=True, dmabuf=b_Qa[s][hd])
                S.add("pool", MS(Va3[s][:, :, 64:128], 1.0), writes=[b_Va[s]])
                nb = 0
                for which, dst, bdst in ((0, Qa[s], b_Qa[s]), (1, Ka[s], b_Ka[s])):
                    for tq in range(4):
                        bank = nb % 2
                        nb += 1
                        tsl = slice(tq * 512, (tq + 1) * 512)
                        for kc in range(8):
                            S.add("pe", MM(ps[bank][:, :], Whp[s][:, kc, which * 128:(which + 1) * 128], hT[:, kc, tsl], kc == 0, kc == 7),
                                  reads=[b_Whp[s], hTq[tq]], writes=[pb[bank]], partial=(kc > 0))
                        for hd in range(2):
                            S.add("dve", CP(dst[hd][0:64, tsl], ps[bank][hd * 64:(hd + 1) * 64, :]),
                                  reads=[pb[bank]], writes=[bdst[hd]], partial=True)
                vT = yT[:, 8 + hp, :]
                for tq in range(4):
                    bank = nb % 2
                    nb += 1
                    tsl = slice(tq * 512, (tq + 1) * 512)
                    for kc in range(8):
                        S.add("pe", MM(ps[bank][:, :], Whp[s][:, kc, 256:384], hT[:, kc, tsl], kc == 0, kc == 7),
                              reads=[b_Whp[s], hTq[tq]], writes=[pb[bank]], partial=(kc > 0))
                    S.add("dve", CP(vT[:, tsl], ps[bank][:, :]), reads=[pb[bank]], writes=[yTb[8 + hp][tq]])
                psTv = ps[2][:].bitcast(BF16)
                for i0 in range(0, NT, 8):
                    for ii in range(8):
                        i = i0 + ii
                        S.add("pe", TR(psTv[:, ii * 128:(ii + 1) * 128], vT[:, i * 128:(i + 1) * 128], ident[:]),
                              reads=[yTb[8 + hp][i // 4], b_ident], writes=[pb[2]], partial=(ii > 0))
                    for hd in range(2):
                        S.add("dve", CP(Va[s][:, i0:i0 + 8, hd, 0:64],
                                        psTv.rearrange("p (a c d) -> p a c d", a=8, c=2)[:, :, hd, :]),
                              reads=[pb[2]], writes=[b_Va[s]], partial=True)

            def c_attn(hp):
                nonlocal pti, poi
                s = hp % 2
                steps = [(hd, J, kb) for hd in range(2) for J in range(4) for kb in range(4 * J + 4)]
                infos = {}
                LOOK = 2
                for n_ in range(len(steps) + LOOK):
                    if n_ < len(steps):
                        hd, J, kb = steps[n_]
                        pi = pti % 3
                        pti += 1
                        bank = 3 + pi
                        r = kb - 4 * J
                        c0 = max(r, 0) * 128
                        K_ = Ka[s][hd][0:70, kb * 128:(kb + 1) * 128]
                        Qt = Qa[s][hd]
                        rd = [b_Ka[s][hd], b_Qa[s][hd]]
                        if r < 0:
                            S.add("pe", MM(ps[bank][:, 0:512], K_, Qt[0:70, J * 512:(J + 1) * 512]), reads=rd, writes=pbs(bank))
                        else:
                            q0 = J * 512 + c0
                            S.add("pe", MM(ps[bank][:, c0:c0 + 128], K_, Qt[0:70, q0:q0 + 128], True, False), reads=rd, writes=pbs(bank))
                            S.add("pe", MM(ps[bank][:, c0:c0 + 128], ident[:, :], mneg[:, :], False, True),
                                  reads=[b_ident, b_mneg], writes=pbs(bank), partial=True)
                            if c0 + 128 < 512:
                                S.add("pe", MM(ps[bank][:, c0 + 128:512], K_, Qt[0:70, q0 + 128:J * 512 + 512]),
                                      reads=rd, writes=pbs(bank), partial=True)
                        S.add("act", ACTV(PT[pi][:, c0:512], ps[bank][:, c0:512], AF.Exp, scale=0.125), reads=pbs(bank), writes=[b_PT[pi]])
                        infos[n_] = (pi, c0)
                    if n_ >= LOOK:
                        hd, J, kb = steps[n_ - LOOK]
                        pi, c0 = infos[n_ - LOOK]
                        if kb == 0:
                            poi += 1
                        ob = 6 + (poi % 2)
                        last = (kb == 4 * J + 3)
                        S.add("pe", MM(ps[ob][:, c0:512], Va[s][:, kb, hd, :], PT[pi][:, c0:512], kb == 0, last),
                              reads=[b_Va[s], b_PT[pi]], writes=[pb[ob]], partial=(kb > 0))
                        if last:
                            ri = poi % 2
                            r0 = hd * 64
                            S.add("dve", RCP(rec[ri][64:128, :], ps[ob][64:128, :]), reads=[pb[ob]], writes=[b_rec[ri]])
                            S.add("dve", TT(yT[r0:r0 + 64, 8 + hp, J * 512:(J + 1) * 512], ps[ob][0:64, :], rec[ri][64:128, :], ALU.mult),
                                  reads=[pb[ob], b_rec[ri]], writes=[yTb[8 + hp][J]], partial=True)

            c_inproj(0)
            for hp in range(8):
                S.record()
                c_attn(hp)
                strA = S.stop()
                S.record()
                if hp < 7:
                    c_inproj(hp + 1)
                strB = S.stop()
                S.replay_merged(strA, strB)
            if b == 0:
                dbg_store("yatt", yT[:, 8:16, :], [yTb[e][q] for e in range(8, 16) for q in range(4)])
            S.barrier()

            xr = ring("xr", [carve(i * 4096, [1024], F32) for i in range(3)])
            xnr = ring("xn", [carve(12288 + i * 2048, [1024], BF16) for i in range(2)])
            JUNK[0] = carve(16384, [1024], BF16)
            h1r = ring("h1t", [carve(18432 + i * 4096, [1024], F32) for i in range(3)])
            pend_nt = None
            for i in range(NT):
                xs = xr.next()
                S.add("sp", DMA(xs.t, x_d[b, i * 128:(i + 1) * 128, :]), writes=[xs.b], dma=True, dmabuf=xs.b)
                hs = h1r.next()
                if pend_nt is not None:
                    nt_a(pend_nt[0], pend_nt[1], pend_nt[3])
                for half in range(2):
                    bank = (i % 2) * 2 + half
                    hsl = slice(half * 512, (half + 1) * 512)
                    for ec in range(16):
                        S.add("pe", MM(ps[bank][:, :], yT[:, ec, i * 128:(i + 1) * 128], WoutT[:, ec, hsl], ec == 0, ec == 15),
                              reads=[yTb[ec][i // 4], b_WoutT], writes=[pb[bank]], partial=(ec > 0))
                    S.add("dve", TT(hs.t[:, hsl], ps[bank][:, :], xs.t[:, hsl], ALU.add),
                          reads=[pb[bank], xs.b], writes=[hs.b], partial=(half > 0))
                if pend_nt is not None:
                    nt_b(pend_nt[2], wn2[:, :], b_wn2, pend_nt[3], pend_nt[4])
                S.add("act", DMA(out_d[b, i * 128:(i + 1) * 128, :], hs.t), reads=[hs.b], writes=[outb[i]], dma=True, dmabuf=hs.sd)
                pend_nt = (hs.t, hs.b, i, xnr.next(), 4 + (i % 2))
            nt_a(pend_nt[0], pend_nt[1], pend_nt[3])
            nt_b(pend_nt[2], wn2[:, :], b_wn2, pend_nt[3], pend_nt[4])
            S.barrier()

            O = 0
            Wupr = ring("Wup", [carve(O + i * 8192, [8, 512], BF16) for i in range(2)]); O += 16384
            Wdnr = ring("Wdn", [carve(O + i * 8192, [4, 1024], BF16) for i in range(3)]); O += 24576
            uT = carve(O, [32, 512], BF16); O += 32768
            h1l = ring("h1l", [carve(O + i * 4096, [1024], F32) for i in range(4)]); O += 16384
            otr = ring("ot", [carve(O + i * 4096, [1024], F32) for i in range(2)]); O += 8192
            rtr = ring("rt", [carve(O + i * 1024, [512], BF16) for i in range(2)]); O += 2048
            JUNK[0] = carve(O, [1024], BF16); O += 2048
            assert O <= ARENA_BYTES
            b_uT = [gb(f"uT{i}") for i in range(32)]
            for tg in range(4):
                tsl = slice(tg * 512, (tg + 1) * 512)
                nb = 0
                h1s = []
                for ti in range(4):
                    i = tg * 4 + ti
                    hs = h1l.next()
                    h1s.append(hs)
                    S.add("act", DMA(hs.t, out_d[b, i * 128:(i + 1) * 128, :]), reads=[outb[i]], writes=[hs.b], dma=True, dmabuf=hs.b)
                for fg in range(8):
                    ws = Wupr.next()
                    S.add("sp", DMA(ws.t, Wuv[:, :, fg * 512:(fg + 1) * 512]), reads=[b_Wu], writes=[ws.b], dma=True, dmabuf=ws.b)
                    for fj in range(4):
                        fc = fg * 4 + fj
                        bank = nb % 8
                        nb += 1
                        for kc in range(8):
                            S.add("pe", MM(ps[bank][:, :], ws.t[:, kc, fj * 128:(fj + 1) * 128], hT[:, kc, tsl], kc == 0, kc == 7),
                                  reads=[ws.b, hTq[tg]], writes=pbs(bank), partial=(kc > 0))
                        rs = rtr.next()
                        S.add("act", ACTV(rs.t, ps[bank][:, :], AF.Relu), reads=pbs(bank), writes=[rs.b])
                        S.add("pool", TT(uT[:, fc, :], rs.t, rs.t, ALU.mult), reads=[rs.b], writes=[b_uT[fc]])
                for fg in range(8):
                    ws = Wdnr.next()
                    S.add("sp", DMA(ws.t, Wdv[:, fg * 4:(fg + 1) * 4, :]), reads=[b_Wd], writes=[ws.b], dma=True, dmabuf=ws.b)
                    for ti in range(4):
                        for half in range(2):
                            bank = ti * 2 + half
                            for fj in range(4):
                                fc = fg * 4 + fj
                                S.add("pe", MM(ps[bank][:, :], uT[:, fc, ti * 128:(ti + 1) * 128], ws.t[:, fj, half * 512:(half + 1) * 512],
                                               fc == 0, fc == 31),
                                      reads=[ws.b, b_uT[fc]], writes=pbs(bank), partial=(fc > 0))
                for ti in range(4):
                    i = tg * 4 + ti
                    hs = h1s[ti]
                    for half in range(2):
                        bank = ti * 2 + half
                        hsl = slice(half * 512, (half + 1) * 512)
                        S.add("dve", TT(hs.t[:, hsl], ps[bank][:, :], hs.t[:, hsl], ALU.add), reads=pbs(bank) + [hs.b], writes=[hs.b])
                    rstd, sbuf_ = rms_stats(hs.t, hs.b)
                    os_ = otr.next()
                    S.add("dve", STT(os_.t, hs.t, rstd, nfw[:, :], ALU.mult, ALU.mult), reads=[hs.b, sbuf_, b_nfw], writes=[os_.b])
                    S.add("act", DMA(out_d[b, i * 128:(i + 1) * 128, :], os_.t), reads=[os_.b, outb[i]], writes=[outb[i]],
                          dma=True, dmabuf=os_.sd)
            S.barrier()

        run_sched(nc, S)
    return nc


_NC_CACHE = {}


def kernel(**inputs):
    x = np.ascontiguousarray(inputs["x"], dtype=np.float32)
    nb = x.shape[0]
    per = nb // NCORES
    if per not in _NC_CACHE:
        _NC_CACHE[per] = build_nc(per)
    nc = _NC_CACHE[per]
    names = ["norm_mix_w", "w_in", "conv_w", "conv_b", "dt_bias", "a_log", "d_skip", "ssd_norm_w", "f_bias",
             "w_out", "norm_mlp_w", "w_up", "w_down", "norm_final_w"]
    shared = {n: np.ascontiguousarray(inputs[n], dtype=np.float32) for n in names}
    in_maps = []
    for c in range(NCORES):
        m = dict(shared)
        m["x"] = np.ascontiguousarray(x[c * per:(c + 1) * per])
        in_maps.append(m)
    res = run_bass_kernel_spmd(nc, in_maps, core_ids=list(range(NCORES)))
    return np.concatenate([r["out"] for r in res.results], axis=0).astype(np.float32)
```

```python
import numpy as np
from contextlib import ExitStack
import concourse.bass as bass
import concourse.mybir as mybir
from concourse.bass_utils import run_bass_kernel_spmd
from concourse.ap import AP

F32 = mybir.dt.float32
BF16 = mybir.dt.bfloat16
AF = mybir.ActivationFunctionType
ALU = mybir.AluOpType

NCORES = 8
D = 1024
T = 2048
NT = T // 128
EIN = 5664
EPS = 1e-5
ENGS = ("pe", "act", "dve", "pool", "sp")


class Buf:
    __slots__ = ("name", "writers", "readers", "dsem", "dcount")

    def __init__(self, name):
        self.name = name
        self.writers = []
        self.readers = []
        self.dsem = None
        self.dcount = 0


class Op:
    __slots__ = ("eng", "fn", "deps", "is_dma", "token", "signal", "dmabuf", "idx", "ndma")


class Sched:
    def __init__(self):
        self.ops = []
        self.bufs = []
        self.last = {e: None for e in ENGS}
        self.dma_since_barrier = []

    def buf(self, name):
        b = Buf(name)
        self.bufs.append(b)
        return b

    def record(self):
        self.rec = []

    def stop(self):
        r, self.rec = self.rec, None
        return r

    @staticmethod
    def merge_lists(A, Bl):
        out = []
        na, nb = len(A), len(Bl)
        ia = ib = 0
        while ia < na or ib < nb:
            if ib >= nb or (ia < na and ia * nb <= ib * na):
                out.append(A[ia]); ia += 1
            else:
                out.append(Bl[ib]); ib += 1
        return out

    def replay_merged(self, A, Bl):
        na, nb = len(A), len(Bl)
        ia = ib = 0
        while ia < na or ib < nb:
            if ib >= nb or (ia < na and ia * nb <= ib * na):
                a, kw = A[ia]; ia += 1
            else:
                a, kw = Bl[ib]; ib += 1
            self.add(*a, **kw)

    def add(self, eng, fn, reads=(), writes=(), dma=False, dmabuf=None, ndma=1, same_eng_ok=None,
            partial=False, extra_deps=()):
        if getattr(self, "rec", None) is not None:
            self.rec.append(((eng, fn), dict(reads=list(reads), writes=list(writes), dma=dma, dmabuf=dmabuf, ndma=ndma,
                                             same_eng_ok=same_eng_ok, partial=partial, extra_deps=tuple(extra_deps))))
            return None
        if same_eng_ok is None:
            same_eng_ok = (eng == "pe")
        op = Op()
        op.eng, op.fn, op.is_dma, op.dmabuf, op.ndma = eng, fn, dma, dmabuf, ndma
        op.deps, op.token, op.signal = [], None, False
        op.idx = len(self.ops)
        deps = set(extra_deps)
        for r in reads:
            deps.update(r.writers)
        for w in writes:
            if not partial:
                deps.update(w.writers)
            deps.update(w.readers)
        for r in reads:
            r.readers.append(op.idx)
        for w in writes:
            if partial:
                w.writers.append(op.idx)
            else:
                w.writers = [op.idx]
            w.readers = []
        deps.discard(op.idx)
        for d in sorted(deps):
            dop = self.ops[d]
            if same_eng_ok and dop.eng == eng and not dop.is_dma and not dma:
                continue
            op.deps.append(d)
            dop.signal = True
        self.ops.append(op)
        if fn is not None:
            self.last[eng] = op.idx
        if dma:
            assert dmabuf is not None
            self.dma_since_barrier.append(op.idx)
        return op

    def barrier(self):
        deps = [v for v in self.last.values() if v is not None] + list(self.dma_since_barrier)
        self.dma_since_barrier = []
        for e in ENGS:
            self.add(e, None, extra_deps=deps, same_eng_ok=False)


def run_sched(nc, sched):
    ops = sched.ops
    cnt = {e: 0 for e in ENGS}
    for op in ops:
        if op.is_dma:
            b = op.dmabuf
            b.dcount += 16 * op.ndma
            op.token = ("d", b, b.dcount)
        elif op.signal:
            cnt[op.eng] += 1
            op.token = ("e", op.eng, cnt[op.eng])
    dma_bufs = [b for b in sched.bufs if b.dcount > 0]
    with ExitStack() as es:
        esem = {e: es.enter_context(nc.semaphore(f"s_{e}")) for e in ENGS}
        for b in dma_bufs:
            b.dsem = es.enter_context(nc.semaphore(f"d_{b.name}"))
        block = es.enter_context(nc.Block())
        per_eng = {e: [o for o in ops if o.eng == e] for e in ENGS}

        def emit_engine(e, handle):
            waited = {}
            for op in per_eng[e]:
                need = {}
                for d in op.deps:
                    t = ops[d].token
                    if t[0] == "e":
                        key = ("e", t[1])
                        sem = esem[t[1]]
                    else:
                        key = ("d", id(t[1]))
                        sem = t[1].dsem
                    if need.get(key, (None, 0))[1] < t[2]:
                        need[key] = (sem, t[2])
                for key, (sem, val) in need.items():
                    if waited.get(key, 0) >= val:
                        continue
                    waited[key] = val
                    handle.wait_ge(sem, val)
                if op.fn is None:
                    continue
                ins = op.fn(handle)
                if op.is_dma:
                    if not isinstance(ins, (list, tuple)):
                        ins = [ins]
                    assert len(ins) == op.ndma
                    for i in ins:
                        i.then_inc(op.dmabuf.dsem, 16)
                elif op.signal:
                    ins.then_inc(esem[e], 1)
            last = {}
            for op in per_eng[e]:
                if op.is_dma:
                    last[id(op.dmabuf)] = (op.dmabuf.dsem, op.token[2])
            for key, (sem, val) in last.items():
                if waited.get(("d", key), 0) < val:
                    handle.wait_ge(sem, val)

        @block.sync
        def _(h):
            emit_engine("sp", h)

        @block.tensor
        def _(h):
            emit_engine("pe", h)

        @block.scalar
        def _(h):
            emit_engine("act", h)

        @block.vector
        def _(h):
            emit_engine("dve", h)

        @block.gpsimd
        def _(h):
            emit_engine("pool", h)
    return cnt


class Slot:
    def __init__(self, t, b, sd=None):
        self.t, self.b, self.sd = t, b, sd


class Ring:
    def __init__(self, slots):
        self.slots = slots
        self.i = 0

    def next(self):
        s = self.slots[self.i % len(self.slots)]
        self.i += 1
        return s

    def retarget(self, tensors):
        for s, t in zip(self.slots, tensors):
            s.t = t


def MM(out, lhsT, rhs, start=True, stop=True):
    return lambda h: h.matmul(out, lhsT=lhsT, rhs=rhs, start=start, stop=stop)


def TR(out, in_, ident):
    return lambda h: h.transpose(out=out, in_=in_, identity=ident)


def ACTV(out, in_, func, bias=None, scale=None, accum_out=None):
    kw = {}
    if bias is not None:
        kw["bias"] = bias
    if scale is not None:
        kw["scale"] = scale
    if accum_out is not None:
        kw["accum_out"] = accum_out
    return lambda h: h.activation(out=out, in_=in_, func=func, **kw)


def TT(out, in0, in1, op):
    return lambda h: h.tensor_tensor(out=out, in0=in0, in1=in1, op=op)


def TS(out, in0, s1, s2, op0, op1=None):
    if op1 is None:
        return lambda h: h.tensor_scalar(out=out, in0=in0, scalar1=s1, scalar2=None, op0=op0)
    return lambda h: h.tensor_scalar(out=out, in0=in0, scalar1=s1, scalar2=s2, op0=op0, op1=op1)


def STT(out, in0, scalar, in1, op0, op1):
    return lambda h: h.scalar_tensor_tensor(out=out, in0=in0, scalar=scalar, in1=in1, op0=op0, op1=op1)


def CP(out, in_):
    return lambda h: h.tensor_copy(out=out, in_=in_)


def MS(ap, val):
    return lambda h: h.memset(ap, val)


def DMA(out, in_, slow=False):
    return lambda h: h.dma_start(out=out, in_=in_, allow_slow_non_contiguous=slow)


def RCP(out, in_):
    return lambda h: h.reciprocal(out=out, in_=in_)


def RCPF(out, in_):
    return lambda h: h.reciprocal_approx_fast(out=out, in_=in_)


def SCAN(out, d0, d1, init):
    return lambda h: h.tensor_tensor_scan(out=out, data0=d0, data1=d1, initial=init, op0=ALU.add, op1=ALU.add)


def ASEL(out, in_, pattern, op, fill, base, cm):
    return lambda h: h.affine_select(out=out, in_=in_, pattern=pattern, compare_op=op, fill=fill, base=base, channel_multiplier=cm)


ARENA_BYTES = 100 * 1024


def build_nc(nseq, debug=None):
    nc = bass.Bass("TRN2", target_bir_lowering=False)
    S = Sched()

    def dram_in(name, shape):
        return nc.dram_tensor(name, list(shape), F32, kind="ExternalInput")

    x_h = dram_in("x", [nseq, T, D])
    p_norm_mix = dram_in("norm_mix_w", [1, D])
    p_w_in = dram_in("w_in", [1, D, EIN])
    p_conv_w = dram_in("conv_w", [1, 4, 1536])
    p_conv_b = dram_in("conv_b", [1, 1536])
    p_dt_bias = dram_in("dt_bias", [1, 16])
    p_a_log = dram_in("a_log", [1, 16])
    p_d_skip = dram_in("d_skip", [1, 16])
    p_ssd_norm = dram_in("ssd_norm_w", [1, D])
    p_f_bias = dram_in("f_bias", [1, 16])
    p_w_out = dram_in("w_out", [1, 2048, D])
    p_norm_mlp = dram_in("norm_mlp_w", [1, D])
    p_w_up = dram_in("w_up", [1, D, 4096])
    p_w_down = dram_in("w_down", [1, 4096, D])
    p_norm_final = dram_in("norm_final_w", [D])
    out_h = nc.dram_tensor("out", [nseq, T, D], F32, kind="ExternalOutput")
    x_d = x_h.ap()
    out_d = out_h.ap()
    W1 = nc.dram_tensor("W1s", [D, EIN], BF16, kind="Internal").ap()
    Wo = nc.dram_tensor("Wos", [2048, D], BF16, kind="Internal").ap()
    Wu = nc.dram_tensor("Wus", [D, 4096], BF16, kind="Internal").ap()
    Wd = nc.dram_tensor("Wds", [4096, D], BF16, kind="Internal").ap()
    cs8 = nc.dram_tensor("cs8s", [16, 3, T], BF16, kind="Internal").ap()
    ac6 = nc.dram_tensor("ac6s", [16, 6, T], BF16, kind="Internal").ap()
    dbg_d = None
    if debug:
        dbg_d = {k: nc.dram_tensor("dbg_" + k, list(shp), F32, kind="ExternalOutput").ap() for k, shp in debug.items()}

    def rawap(handle, offset, pat):
        return AP(handle, offset, [list(p) for p in pat])

    with ExitStack() as es:
        def sb(name, shape, dt):
            return es.enter_context(nc.sbuf_tensor(name, list(shape), dt))

        hT = sb("hT", [128, 8, T], BF16)
        yT = sb("yT", [128, 16, T], BF16)
        arena = sb("arena", [128, ARENA_BYTES // 2], BF16)
        ident = sb("ident", [128, 128], BF16)
        onesb = sb("onesb", [128, 128], BF16)
        U1 = sb("U1", [128, 128], BF16)
        U2b = sb("U2b", [128, 128], BF16)
        U2f = sb("U2f", [128, 128], F32)
        mneg = sb("mneg", [128, 128], BF16)
        mneg4 = sb("mneg4", [128, 4, 128], BF16)
        selA = sb("selA", [35, 128], BF16)
        selB = sb("selB", [35, 128], BF16)
        cf = sb("cf", [128, 128], F32)
        wn1 = sb("wn1", [128, 8], F32)
        wn2 = sb("wn2", [128, 8], F32)
        snw = sb("snw", [128, 8], F32)
        dsk = sb("dsk", [128, 8], F32)
        nfw = sb("nfw", [128, D], F32)
        cw = sb("cw", [128, 12, 4], F32)
        cb = sb("cb", [128, 12], F32)
        cbh = sb("cbh", [128, 12], F32)
        dtb_bc = sb("dtb_bc", [128, 16], F32)
        A_bc = sb("A_bc", [128, 16], F32)
        nfb = sb("nfb", [16, 1], F32)
        dtb_p = sb("dtb_p", [16, 1], F32)
        A_p = sb("A_p", [16, 1], F32)
        epsb = sb("epsb", [128, 1], F32)
        Wdtf = sb("Wdtf", [128, 8, 32], BF16)
        stat = sb("stat", [128, 4, 4], F32)
        dt_all = sb("dt_all", [128, 16, 16], F32)
        a_all = sb("a_all", [128, 16, 16], F32)
        ps = [es.enter_context(nc.psum_tensor(f"ps{i}", [128, 512], F32)) for i in range(8)]
        pb = [S.buf(f"pb{i}") for i in range(8)]
        pb4b = S.buf("pb4b")

        def pbs(bank):
            return [pb[4], pb4b] if bank == 4 else [pb[bank]]

        def carve(off, shape, dt):
            esz = 2 if dt == BF16 else 4
            n = int(np.prod(shape))
            assert off % 4 == 0 and off + n * esz <= ARENA_BYTES, (off, shape)
            v = arena[:, off // 2: off // 2 + n * esz // 2]
            if dt != BF16:
                v = v.bitcast(dt)
            if len(shape) == 2:
                v = v.rearrange("p (a b) -> p a b", a=shape[0])
            elif len(shape) == 3:
                v = v.rearrange("p (a b c) -> p a b c", a=shape[0], b=shape[1])
            return v

        B = {}

        def gb(name):
            if name not in B:
                B[name] = S.buf(name)
            return B[name]

        def ring(name, tensors):
            return Ring([Slot(t, gb(f"{name}{i}"), gb(f"{name}{i}_st")) for i, t in enumerate(tensors)])

        hTq = [gb(f"hTq{i}") for i in range(4)]
        yTb = [[gb(f"yT{e}_{q}") for q in range(4)] for e in range(16)]
        outb = [gb(f"outb{i}") for i in range(NT)]

        cst = gb("consts")

        def small_dma(dst, src, nm, slow=False):
            b_ = gb("c_" + nm)
            S.add("sp", DMA(dst, src, slow), writes=[b_], dma=True, dmabuf=b_)
            return b_

        b_wn1 = small_dma(wn1[:], rawap(p_norm_mix, 0, [[1, 128], [128, 8]]), "wn1", True)
        b_wn2 = small_dma(wn2[:], rawap(p_norm_mlp, 0, [[1, 128], [128, 8]]), "wn2", True)
        b_snw = small_dma(snw[:], rawap(p_ssd_norm, 0, [[1, 128], [128, 8]]), "snw", True)
        b_nfw = small_dma(nfw[:], rawap(p_norm_final, 0, [[0, 128], [1, D]]), "nfw")
        b_cw = gb("c_cw")
        for k in range(4):
            S.add("sp", DMA(cw[:, :, k], rawap(p_conv_w, k * 1536, [[1, 128], [128, 12]]), True),
                  writes=[b_cw], dma=True, dmabuf=b_cw, partial=(k > 0))
        b_cb = small_dma(cb[:], rawap(p_conv_b, 0, [[1, 128], [128, 12]]), "cb", True)
        b_cbh = gb("c_cbh")
        S.add("dve", TS(cbh[:], cb[:], 0.5, None, ALU.mult), reads=[b_cb], writes=[b_cbh])
        b_dsk = gb("c_dsk")
        for hh in range(2):
            S.add("sp", DMA(dsk[hh * 64:(hh + 1) * 64, :], rawap(p_d_skip, hh, [[0, 64], [2, 8]]), True),
                  writes=[b_dsk], dma=True, dmabuf=b_dsk, partial=(hh > 0))
        b_dtb = small_dma(dtb_bc[:], rawap(p_dt_bias, 0, [[0, 128], [1, 16]]), "dtb")
        b_A = small_dma(A_bc[:], rawap(p_a_log, 0, [[0, 128], [1, 16]]), "A")
        b_nfb = small_dma(nfb[:], rawap(p_f_bias, 0, [[1, 16], [1, 1]]), "nfb")
        b_dtbp = small_dma(dtb_p[:], rawap(p_dt_bias, 0, [[1, 16], [1, 1]]), "dtbp")
        b_Ap = small_dma(A_p[:], rawap(p_a_log, 0, [[1, 16], [1, 1]]), "Ap")
        S.add("act", ACTV(A_p[:], A_p[:], AF.Exp), reads=[b_Ap], writes=[b_Ap])
        S.add("dve", TS(A_p[:], A_p[:], -1.0, None, ALU.mult), reads=[b_Ap], writes=[b_Ap])
        S.add("act", ACTV(A_bc[:], A_bc[:], AF.Exp), reads=[b_A], writes=[b_A])
        S.add("dve", TS(A_bc[:], A_bc[:], -1.0, None, ALU.mult), reads=[b_A], writes=[b_A])
        S.add("dve", TS(nfb[:], nfb[:], -1.0, None, ALU.mult), reads=[b_nfb], writes=[b_nfb])
        S.add("dve", MS(epsb[:], EPS), writes=[gb("c_eps")])
        S.add("pool", MS(cf[:], 1.0), writes=[cst])
        S.add("dve", CP(onesb[:], cf[:]), reads=[cst], writes=[gb("c_ones")])
        S.add("pool", ASEL(U2f[:], cf[:], [[1, 128]], ALU.is_ge, 0.0, 0, -1), reads=[cst], writes=[gb("c_U2f")])
        S.add("dve", CP(U2b[:], U2f[:]), reads=[gb("c_U2f")], writes=[gb("c_U2b")])
        S.add("pool", ASEL(cf[:], cf[:], [[-1, 128]], ALU.is_gt, 0.0, 0, 1), reads=[cst, gb("c_ones"), gb("c_U2f")], writes=[cst])
        S.add("dve", CP(U1[:], cf[:]), reads=[cst], writes=[gb("c_U1")])
        S.add("dve", TS(mneg[:], cf[:], -30000.0, None, ALU.mult), reads=[cst], writes=[gb("c_mneg")])
        S.add("pool", MS(cf[:], 1.0), reads=[gb("c_U1"), gb("c_mneg")], writes=[cst])
        S.add("pool", ASEL(cf[:], cf[:], [[-1, 128]], ALU.is_equal, 0.0, 0, 1), reads=[cst], writes=[cst])
        S.add("dve", CP(ident[:], cf[:]), reads=[cst], writes=[gb("c_ident")])
        b_mneg4 = gb("c_mneg4")
        S.add("dve", CP(mneg4[:], mneg[:, :].unsqueeze(1).broadcast_to([128, 4, 128])), reads=[gb("c_mneg")], writes=[b_mneg4])
        b_sel = gb("c_sel")
        S.add("pool", MS(selA[:], 0.0), writes=[b_sel])
        S.add("pool", MS(selB[:], 0.0), writes=[b_sel], partial=True)
        S.add("pool", MS(selA[0:3, :], 1.0), reads=[b_sel], writes=[b_sel])
        S.add("pool", MS(selB[32:35, :], 1.0), reads=[b_sel], writes=[b_sel])
        b_eps = gb("c_eps")
        b_ident, b_ones, b_U1, b_U2b, b_U2f, b_mneg = (gb(n) for n in ["c_ident", "c_ones", "c_U1", "c_U2b", "c_U2f", "c_mneg"])

        win = p_w_in.ap()[0]
        segs = []
        for g in range(2):
            base = g * 1280
            segs.append((base, g * 512, 512))
            segs.append((base + 512, 1024 + g * 512, 512))
            segs.append((base + 1024, 2048 + g * 128, 128))
            segs.append((base + 1152, 2304 + g * 128, 128))
        for hp in range(8):
            base = 2560 + hp * 384
            segs.append((base, 2576 + hp * 128, 128))
            segs.append((base + 128, 3600 + hp * 128, 128))
            segs.append((base + 256, 4624 + hp * 128, 128))
        segs.append((5632, 2560, 16))
        segs.append((5648, 5648, 16))
        b_W1g = [gb("W1g0"), gb("W1g1")]
        b_W1hp = [gb(f"W1hp{i}") for i in range(8)]
        b_W1dtf = gb("W1dtf")
        seg_buf = []
        for g in range(2):
            seg_buf += [b_W1g[g]] * 4
        for hp in range(8):
            seg_buf += [b_W1hp[hp]] * 3
        seg_buf += [b_W1dtf, b_W1dtf]
        order_ = [32, 33] + list(range(0, 32))
        seen_ = set()
        for n_ in order_:
            d0, s0, n = segs[n_]
            bb = seg_buf[n_]
            S.add("pool", DMA(W1[:, d0:d0 + n], win[:, s0:s0 + n]), writes=[bb], dma=True, dmabuf=bb, partial=(id(bb) in seen_))
            seen_.add(id(bb))
        b_Wo, b_Wu, b_Wd = gb("Wo"), gb("Wu"), gb("Wd")
        wout = p_w_out.ap()[0]
        wup = p_w_up.ap()[0]
        wdn = p_w_down.ap()[0]
        for i in range(4):
            S.add("pool", DMA(Wo[i * 512:(i + 1) * 512, :], wout[i * 512:(i + 1) * 512, :]), writes=[b_Wo], dma=True, dmabuf=b_Wo,
                  partial=(i > 0))
        for i in range(4):
            S.add("pool", DMA(Wu[i * 256:(i + 1) * 256, :], wup[i * 256:(i + 1) * 256, :]), writes=[b_Wu], dma=True, dmabuf=b_Wu,
                  partial=(i > 0))
        for i in range(4):
            S.add("pool", DMA(Wd[i * 1024:(i + 1) * 1024, :], wdn[i * 1024:(i + 1) * 1024, :]), writes=[b_Wd], dma=True, dmabuf=b_Wd,
                  partial=(i > 0))
        W1v = W1.rearrange("(kc p) e -> p kc e", p=128)
        Wov = Wo.rearrange("(ec p) d -> p ec d", p=128)
        Wuv = Wu.rearrange("(kc p) f -> p kc f", p=128)
        Wdv = Wd.rearrange("(fc p) d -> p fc d", p=128)
        b_Wdtf = gb("Wdtf")
        S.add("sp", DMA(Wdtf[:], W1v[:, :, 5632:5664]), reads=[b_W1dtf], writes=[b_Wdtf], dma=True, dmabuf=b_Wdtf)

        stat_b = [gb(f"stat{i}") for i in range(4)]
        stat_i = [0]
        JUNK = [None]

        def rms_stats(src_ap, src_buf):
            k = stat_i[0] % 4
            stat_i[0] += 1
            sbuf_ = stat_b[k]
            S.add("act", ACTV(JUNK[0], src_ap, AF.Square, accum_out=stat[:, k, 0:1]), reads=[src_buf], writes=[sbuf_])
            S.add("act", ACTV(stat[:, k, 1:2], stat[:, k, 0:1], AF.Ln, bias=epsb[:], scale=1.0 / D), reads=[sbuf_, b_eps], writes=[sbuf_])
            S.add("act", ACTV(stat[:, k, 2:3], stat[:, k, 1:2], AF.Exp, scale=-0.5), reads=[sbuf_], writes=[sbuf_])
            return stat[:, k, 2:3], sbuf_

        def norm_transpose(src_ap, src_buf, i, wn, wn_buf, xn_slot, bank):
            rstd, sbuf_ = rms_stats(src_ap, src_buf)
            S.add("dve", TS(xn_slot.t, src_ap, rstd, None, ALU.mult), reads=[src_buf, sbuf_], writes=[xn_slot.b])
            psT = ps[bank][:].bitcast(BF16)
            for kc in range(8):
                S.add("pe", TR(psT[:, kc * 128:(kc + 1) * 128], xn_slot.t[:, kc * 128:(kc + 1) * 128], ident[:]),
                      reads=[xn_slot.b, b_ident], writes=pbs(bank), partial=(kc > 0))
            S.add("dve", TT(hT[:, :, i * 128:(i + 1) * 128], psT.rearrange("p (k t) -> p k t", k=8),
                            wn.unsqueeze(2).broadcast_to([128, 8, 128]), ALU.mult),
                  reads=pbs(bank) + [wn_buf], writes=[hTq[i // 4]], partial=True)

        def nt_a(src_ap, src_buf, xn_slot):
            rstd, sbuf_ = rms_stats(src_ap, src_buf)
            S.add("dve", TS(xn_slot.t, src_ap, rstd, None, ALU.mult), reads=[src_buf, sbuf_], writes=[xn_slot.b])

        def nt_b(i, wn, wn_buf, xn_slot, bank):
            psT = ps[bank][:].bitcast(BF16)
            for kc in range(8):
                S.add("pe", TR(psT[:, kc * 128:(kc + 1) * 128], xn_slot.t[:, kc * 128:(kc + 1) * 128], ident[:]),
                      reads=[xn_slot.b, b_ident], writes=pbs(bank), partial=(kc > 0))
            S.add("dve", TT(hT[:, :, i * 128:(i + 1) * 128], psT.rearrange("p (k t) -> p k t", k=8),
                            wn.unsqueeze(2).broadcast_to([128, 8, 128]), ALU.mult),
                  reads=pbs(bank) + [wn_buf], writes=[hTq[i // 4]], partial=True)

        def dbg_store(key, src_ap, src_bufs):
            if dbg_d is None or key not in dbg_d:
                return
            b_ = gb("dbgst_" + key)
            S.add("pool", DMA(dbg_d[key], src_ap), reads=list(src_bufs), dma=True, dmabuf=b_)

        for b in range(nseq):
            xr = ring("xr", [carve(i * 4096, [1024], F32) for i in range(3)])
            xnr = ring("xn", [carve(12288 + i * 2048, [1024], BF16) for i in range(2)])
            JUNK[0] = carve(16384, [1024], BF16)
            for i in range(NT):
                xs = xr.next()
                S.add("sp", DMA(xs.t, x_d[b, i * 128:(i + 1) * 128, :]), writes=[xs.b], dma=True, dmabuf=xs.b)
                norm_transpose(xs.t, xs.b, i, wn1[:, :], b_wn1, xnr.next(), i % 2)
            if b == 0:
                dbg_store("hT", hT[:, :, :], hTq)

            OFF0 = 18432
            tmp256 = carve(OFF0, [256], F32)
            fe = carve(OFF0 + 1024, [512], F32)
            csq = [carve(OFF0 + 3072 + i * 2048, [512], F32) for i in range(2)]
            rr = carve(OFF0 + 7168, [512], F32)
            zer = carve(OFF0 + 9216, [512], F32)
            SPq = carve(OFF0 + 11264, [3, 512], BF16)
            b_t256, b_fe, b_rr, b_zer, b_SPq = gb("t256"), gb("fe"), gb("rr"), gb("zer"), gb("SPq")
            b_csq = [gb("csq0"), gb("csq1")]
            b_dt, b_a = gb("dt_all"), gb("a_all")
            b_cs8 = gb("cs8")
            ae = carve(OFF0 + 14336, [512], F32)
            acs = [carve(OFF0 + 16384 + i * 2048, [512], F32) for i in range(2)]
            acm = carve(OFF0 + 20480, [512], F32)
            rr2 = carve(OFF0 + 22528, [512], F32)
            bs4 = carve(OFF0 + 24576, [4], F32)
            SA = carve(OFF0 + 24592, [6, 512], BF16)
            b_ae, b_acm, b_rr2, b_bs4, b_SA = gb("ae"), gb("acm"), gb("rr2"), gb("bs4"), gb("SA")
            b_acs = [gb("acs0"), gb("acs1")]
            b_ac6 = gb("ac6")
            for c in range(NT):
                for kc in range(8):
                    S.add("pe", MM(ps[2][:, c * 16:(c + 1) * 16], hT[:, kc, c * 128:(c + 1) * 128], Wdtf[:, kc, 0:16], kc == 0, kc == 7),
                          reads=[hTq[c // 4], b_Wdtf], writes=[pb[2]], partial=not (c == 0 and kc == 0))
            S.add("dve", TT(tmp256.rearrange("p (c h) -> p c h", c=16), ps[2][:, 0:256].rearrange("p (c h) -> p c h", c=16),
                            dtb_bc[:, :].unsqueeze(1).broadcast_to([128, 16, 16]), ALU.add),
                  reads=[pb[2], b_dtb], writes=[b_t256])
            S.add("act", ACTV(tmp256, tmp256, AF.Exp), reads=[b_t256], writes=[b_t256])
            S.add("act", ACTV(dt_all[:].rearrange("p c h -> p (c h)"), tmp256, AF.Ln, bias=1.0), reads=[b_t256], writes=[b_dt])
            S.add("dve", TT(a_all[:], dt_all[:], A_bc[:, :].unsqueeze(1).broadcast_to([128, 16, 16]), ALU.mult),
                  reads=[b_dt, b_A], writes=[b_a])
            S.add("dve", MS(zer[0:16, :], 0.0), writes=[b_zer])
            for tq in range(4):
                bank = 3 + tq
                for kc in range(8):
                    S.add("pe", MM(ps[bank][0:16, :], Wdtf[:, kc, 16:32], hT[:, kc, tq * 512:(tq + 1) * 512], kc == 0, kc == 7),
                          reads=[hTq[tq], b_Wdtf], writes=pbs(bank), partial=(kc > 0))
                S.add("act", ACTV(fe[0:16, :], ps[bank][0:16, :], AF.Exp, bias=nfb[:, 0:1], scale=-1.0),
                      reads=pbs(bank) + [b_nfb], writes=[b_fe])
                S.add("act", ACTV(fe[0:16, :], fe[0:16, :], AF.Ln, bias=1.0), reads=[b_fe], writes=[b_fe])
                cur, prv = csq[tq % 2], csq[(tq + 1) % 2]
                bcur, bprv = b_csq[tq % 2], b_csq[(tq + 1) % 2]
                if tq == 0:
                    S.add("dve", SCAN(cur[0:16, :], fe[0:16, :], zer[0:16, :], 0.0), reads=[b_fe, b_zer], writes=[bcur])
                else:
                    S.add("dve", SCAN(cur[0:16, :], fe[0:16, :], zer[0:16, :], prv[0:16, 511:512]),
                          reads=[b_fe, b_zer, bprv], writes=[bcur])
                S.add("dve", TS(SPq[0:16, 0, :], cur[0:16, :], 8.0, None, ALU.mult), reads=[bcur], writes=[b_SPq])
                S.add("dve", STT(rr[0:16, :], cur[0:16, :], 8.0, SPq[0:16, 0, :], ALU.mult, ALU.subtract),
                      reads=[bcur, b_SPq], writes=[b_rr])
                S.add("dve", CP(SPq[0:16, 1, :], rr[0:16, :]), reads=[b_rr], writes=[b_SPq], partial=True)
                S.add("dve", TT(rr[0:16, :], rr[0:16, :], SPq[0:16, 1, :], ALU.subtract), reads=[b_rr, b_SPq], writes=[b_rr])
                S.add("dve", CP(SPq[0:16, 2, :], rr[0:16, :]), reads=[b_rr], writes=[b_SPq], partial=True)
                S.add("sp", DMA(cs8[:, :, tq * 512:(tq + 1) * 512], SPq[0:16, :, :]),
                      reads=[b_SPq], writes=[b_cs8], dma=True, dmabuf=gb("SPq_st"), partial=(tq > 0))
                for kc in range(8):
                    S.add("pe", MM(ps[7][0:16, :], Wdtf[:, kc, 0:16], hT[:, kc, tq * 512:(tq + 1) * 512], kc == 0, kc == 7),
                          reads=[hTq[tq], b_Wdtf], writes=[pb[7]], partial=(kc > 0))
                S.add("act", ACTV(ae[0:16, :], ps[7][0:16, :], AF.Exp, bias=dtb_p[:, 0:1]), reads=[pb[7], b_dtbp], writes=[b_ae])
                S.add("act", ACTV(ae[0:16, :], ae[0:16, :], AF.Ln, bias=1.0), reads=[b_ae], writes=[b_ae])
                S.add("dve", TS(ae[0:16, :], ae[0:16, :], A_p[:, 0:1], None, ALU.mult), reads=[b_ae, b_Ap], writes=[b_ae])
                acur, aprv = acs[tq % 2], acs[(tq + 1) % 2]
                bacur, baprv = b_acs[tq % 2], b_acs[(tq + 1) % 2]
                if tq == 0:
                    S.add("dve", SCAN(acur[0:16, :], ae[0:16, :], zer[0:16, :], 0.0), reads=[b_ae, b_zer], writes=[bacur])
                    S.add("dve", MS(bs4[0:16, 0:1], 0.0), writes=[b_bs4])
                else:
                    S.add("dve", SCAN(acur[0:16, :], ae[0:16, :], zer[0:16, :], aprv[0:16, 511:512]),
                          reads=[b_ae, b_zer, baprv], writes=[bacur])
                    S.add("dve", CP(bs4[0:16, 0:1], aprv[0:16, 511:512]), reads=[baprv], writes=[b_bs4])
                S.add("dve", CP(bs4[0:16, 1:4], acur[0:16, :].rearrange("p (c l) -> p c l", c=4)[:, 0:3, 127]),
                      reads=[bacur], writes=[b_bs4], partial=True)
                S.add("dve", TT(acm[0:16, :].rearrange("p (c l) -> p c l", c=4), acur[0:16, :].rearrange("p (c l) -> p c l", c=4),
                                bs4[0:16, 0:4].unsqueeze(2).broadcast_to([16, 4, 128]), ALU.subtract),
                      reads=[bacur, b_bs4], writes=[b_acm])
                S.add("dve", CP(SA[0:16, 0, :], acm[0:16, :]), reads=[b_acm], writes=[b_SA])
                S.add("dve", TT(rr2[0:16, :], acm[0:16, :], SA[0:16, 0, :], ALU.subtract), reads=[b_acm, b_SA], writes=[b_rr2])
                S.add("dve", CP(SA[0:16, 1, :], rr2[0:16, :]), reads=[b_rr2], writes=[b_SA], partial=True)
                S.add("dve", TT(rr2[0:16, :], rr2[0:16, :], SA[0:16, 1, :], ALU.subtract), reads=[b_rr2, b_SA], writes=[b_rr2])
                S.add("dve", CP(SA[0:16, 2, :], rr2[0:16, :]), reads=[b_rr2], writes=[b_SA], partial=True)
                S.add("dve", TS(SA[0:16, 3:6, :], SA[0:16, 0:3, :], -1.0, None, ALU.mult), reads=[b_SA], writes=[b_SA], partial=True)
                S.add("sp", DMA(ac6[:, :, tq * 512:(tq + 1) * 512], SA[0:16, :, :]),
                      reads=[b_SA], writes=[b_ac6], dma=True, dmabuf=gb("SA_st"), partial=(tq > 0))
            S.barrier()

            yTu = yT[:, 8:16, :].rearrange("p a b -> p (a b)")

            def carve2(off, shape, dt):
                esz = 2 if dt == BF16 else 4
                n = int(np.prod(shape))
                assert off % 4 == 0 and off + n * esz <= 32768, (off, shape)
                v = yTu[:, off // 2: off // 2 + n * esz // 2]
                if dt != BF16:
                    v = v.bitcast(dt)
                if len(shape) == 2:
                    v = v.rearrange("p (a b) -> p a b", a=shape[0])
                return v

            O = 18432
            Wg = carve(O, [8, 1280], BF16); O += 20480
            gz_r = [carve(O + i * 4096, [4, 512], BF16) for i in range(2)]; O += 8192
            pexb = [carve(O + i * 1040, [520], BF16) for i in range(2)]; O += 2080
            xbc_r = [carve(O + i * 6144, [6, 512], BF16) for i in range(2)]; O += 12288
            hlb = carve(O, [6, 4], BF16); O += 64
            Dg = carve(O, [24, 128], BF16); O += 6144
            prevF = carve(O, [512], F32); O += 2048
            prevB = carve(O, [512], BF16); O += 1024
            yg = carve(O, [4, 512], F32); O += 8192
            sq = carve(O, [4, 512], BF16); O += 4096
            rstd_bc = carve(O, [512], F32); O += 2048
            lnv_bc = carve(O, [512], F32); O += 2048
            LT_r = [carve(O + i * 2048, [8, 128], BF16) for i in range(2)]; O += 4096
            MT_r = [carve(O + i * 2048, [8, 128], BF16) for i in range(2)]; O += 4096
            tnb = [carve(O + i * 1024, [512], BF16) for i in range(2)]; O += 2048
            hvb = [carve(O + i * 1024, [512], BF16) for i in range(2)]; O += 2048
            b_tnb, b_hvb = [gb("tnb0"), gb("tnb1")], [gb("hvb0"), gb("hvb1")]
            Dd = carve(O, [8, 128], BF16); O += 2048
            dsp = carve(O, [16], BF16); O += 32
            dsr = carve(O, [8], F32); O += 32
            assert O <= ARENA_BYTES, O
            O2 = 0
            Ebc_r = [carve2(O2 + i * 4096, [8, 128], F32) for i in range(2)]; O2 += 8192
            CpT_r = [carve2(O2 + i * 2048, [8, 128], BF16) for i in range(2)]; O2 += 4096
            xdt_r = [carve2(O2 + i * 1024, [512], BF16) for i in range(2)]; O2 += 2048
            xdtD_r = [carve2(O2 + i * 1024, [512], BF16) for i in range(2)]; O2 += 2048
            Btok_r = [carve2(O2 + i * 256, [128], BF16) for i in range(2)]; O2 += 512
            CBm_r = [carve2(O2 + i * 256, [128], BF16) for i in range(2)]; O2 += 512
            Rr = carve2(O2, [8, 512], BF16); O2 += 8192
            assert O2 <= 32768
            b_Rr = gb("Rr")
            b_Wg, b_hl, b_Dg = gb("Wg"), gb("hl"), gb("Dg")
            b_gz = [gb("gz0"), gb("gz1")]
            b_xbc = [gb("xbcT0"), gb("xbcT1")]
            b_pex = [gb("pex0"), gb("pex1")]
            b_prevF, b_prevB, b_yg, b_sq, b_rstd, b_lnv = (gb(n) for n in ["prevF", "prevB", "yg", "sq", "rstd_bc", "lnv_bc"])

            def r2(nm):
                return [gb(nm + "0"), gb(nm + "1")]
            b_LT, b_MT, b_Ebc, b_CpT, b_xdt, b_xdtD, b_Btok, b_CBm = (
                r2(n) for n in ["LT", "MT", "Ebc", "CpT", "xdt", "xdtD", "Btok", "CBm"])
            pexi = 0
            tni = 0
            pbk = 0
            h8 = "p (h q) -> p h q"
            for g in range(2):
                S.add("sp", DMA(Wg, W1v[:, :, g * 1280:(g + 1) * 1280]), reads=[b_W1g[g]], writes=[b_Wg], dma=True, dmabuf=b_Wg)
                S.add("dve", MS(hlb, 0.0), writes=[b_hl])
                S.add("dve", MS(prevF, 0.0), writes=[b_prevF])
                S.add("dve", MS(prevB, 0.0), writes=[b_prevB])
                for ci in range(6):
                    cc = (g * 4 + ci) if ci < 4 else (8 + g if ci == 4 else 10 + g)
                    for k in range(4):
                        S.add("dve", TS(Dg[:, ci * 4 + k, :], ident[:, :], cw[:, cc, k:k + 1], None, ALU.mult),
                              reads=[b_ident, b_cw], writes=[b_Dg], partial=not (ci == 0 and k == 0))
                b_dsp, b_Dd = gb("dsp"), gb("Dd")
                S.add("dve", CP(dsp[:, 0:4], dsk[:, g * 4:(g + 1) * 4]), reads=[b_dsk], writes=[b_dsp])
                S.add("dve", TT(dsr[:, 0:4], dsk[:, g * 4:(g + 1) * 4], dsp[:, 0:4], ALU.subtract), reads=[b_dsk, b_dsp], writes=[gb("dsr")])
                S.add("dve", CP(dsp[:, 4:8], dsr[:, 0:4]), reads=[gb("dsr")], writes=[b_dsp], partial=True)
                for ec in range(4):
                    for j in range(2):
                        S.add("dve", TS(Dd[:, ec * 2 + j, :], ident[:, :], dsp[:, j * 4 + ec:j * 4 + ec + 1], None, ALU.mult),
                              reads=[b_ident, b_dsp], writes=[b_Dd], partial=not (ec == 0 and j == 0))

                def emit_inproj(tq, lo=0, hi=10):
                    nonlocal pexi, pbk, tni
                    q2 = tq % 2
                    tsl = slice(tq * 512, (tq + 1) * 512)
                    xbcT, gz_t = xbc_r[q2], gz_r[q2]
                    order = [(4 + j, j) for j in range(6)] + [(j, None) for j in range(4)]
                    deferred = []
                    for blk, ci in order[lo:hi]:
                        bank = pbk % 3
                        pbk += 1
                        for kc in range(8):
                            S.add("pe", MM(ps[bank][:, :], Wg[:, kc, blk * 128:(blk + 1) * 128], hT[:, kc, tsl], kc == 0, kc == 7),
                                  reads=[b_Wg, hTq[tq]], writes=[pb[bank]], partial=(kc > 0))
                        ti = tni % 2
                        tni += 1
                        if ci is None:
                            S.add("act", ACTV(tnb[ti], ps[bank][:, :], AF.Tanh, scale=0.5), reads=[pb[bank]], writes=[b_tnb[ti]])
                            S.add("act", ACTV(hvb[ti], ps[bank][:, :], AF.Copy, scale=0.5), reads=[pb[bank]], writes=[b_hvb[ti]])
                            for f_ in deferred:
                                f_()
                            deferred = [lambda ti=ti, blk=blk: S.add("dve", STT(gz_t[:, blk, :], tnb[ti], 1.0, hvb[ti], ALU.add, ALU.mult),
                                                                   reads=[b_tnb[ti], b_hvb[ti]], writes=[b_gz[q2]], partial=(blk > 0))]
                            continue
                        cc = (g * 4 + ci) if ci < 4 else (8 + g if ci == 4 else 10 + g)
                        pi = pexi % 2
                        pexi += 1
                        px, bpx = pexb[pi], b_pex[pi]
                        S.add("dve", CP(px[:, 0:3], hlb[:, ci, 0:3]), reads=[b_hl], writes=[bpx])
                        S.add("act", ACTV(px[:, 3:515], ps[bank][:, :], AF.Copy), reads=[pb[bank]], writes=[bpx], partial=True)
                        bank2 = pbk % 3
                        pbk += 1
                        for k in range(4):
                            S.add("pe", MM(ps[bank2][:, :], Dg[:, ci * 4 + k, :], px[:, k:k + 512], k == 0, k == 3),
                                  reads=[bpx, b_Dg], writes=[pb[bank2]], partial=(k > 0))
                        S.add("act", ACTV(tnb[ti], ps[bank2][:, :], AF.Tanh, bias=cbh[:, cc:cc + 1], scale=0.5),
                              reads=[pb[bank2], b_cbh], writes=[b_tnb[ti]])
                        S.add("act", ACTV(hvb[ti], ps[bank2][:, :], AF.Identity, bias=cbh[:, cc:cc + 1], scale=0.5),
                              reads=[pb[bank2], b_cbh], writes=[b_hvb[ti]])
                        for f_ in deferred:
                            f_()
                        deferred = [
                            lambda px=px, bpx=bpx, ci=ci: S.add("dve", CP(hlb[:, ci, 0:3], px[:, 512:515]), reads=[bpx], writes=[b_hl], partial=True),
                            lambda ti=ti, ci=ci: S.add("dve", STT(xbcT[:, ci, :], tnb[ti], 1.0, hvb[ti], ALU.add, ALU.mult),
                                                       reads=[b_tnb[ti], b_hvb[ti]], writes=[b_xbc[q2]], partial=(ci > 0))]
                    for f_ in deferred:
                        f_()

                def emit_rows(tq):
                    tsl = slice(tq * 512, (tq + 1) * 512)
                    import os as _os3
                    if _os3.environ.get("KDBG_NOROWDMA") == "1":
                        return
                    if tq == 0:
                        S.add("pool", MS(Rr[0:35, :, :], 0.0), writes=[b_Rr])
                    S.add("sp", DMA(Rr[0:3, :, :], ac6[g * 8:(g + 1) * 8, 0:3, tsl].rearrange("h k t -> k h t")),
                          reads=[b_ac6], writes=[b_Rr], dma=True, dmabuf=b_Rr)
                    S.add("sp", DMA(Rr[32:35, :, :], ac6[g * 8:(g + 1) * 8, 3:6, tsl].rearrange("h k t -> k h t")),
                          reads=[b_ac6], writes=[b_Rr], dma=True, dmabuf=b_Rr)

                def emit_S1(c):
                    tq, cl, k = c // 4, c % 4, c % 2
                    q2 = tq % 2
                    xbcT = xbc_r[q2]
                    csl = slice(cl * 128, (cl + 1) * 128)
                    dt_c = dt_all[:, c, g * 8:(g + 1) * 8]
                    LT, MT, Ebc, CpT = LT_r[k], MT_r[k], Ebc_r[k], CpT_r[k]
                    xdt, xdtD, Btok, CBm = xdt_r[k], xdtD_r[k], Btok_r[k], CBm_r[k]
                    if cl == 0:
                        emit_rows(tq)
                    psT = ps[5][:].bitcast(BF16)
                    for xi in range(5):
                        S.add("pe", TR(psT[:, xi * 128:(xi + 1) * 128], xbcT[:, xi, csl], ident[:]),
                              reads=[b_xbc[q2], b_ident], writes=[pb[5]], partial=(xi > 0))
                    S.add("pe", MM(ps[5][:, 384:512], xbcT[:, 4, csl], xbcT[:, 5, csl]), reads=[b_xbc[q2]], writes=[pb[5]], partial=True)
                    for hh in range(2):
                        rv = Rr[0:35, hh * 4:(hh + 1) * 4, csl]
                        p3v = ps[3][:, :].rearrange("p (a b) -> p a b", a=4)
                        p4v = ps[4][:, :].rearrange("p (a b) -> p a b", a=4)
                        S.add("pe", MM(p3v, selA[0:35, :], rv, True, False), reads=[b_Rr, b_sel], writes=[pb[3]])
                        S.add("pe", MM(p4v, selA[0:35, :], rv, True, True), reads=[b_Rr, b_sel], writes=pbs(4))
                        for h4 in range(4):
                            hd = hh * 4 + h4
                            osl = slice(h4 * 128, (h4 + 1) * 128)
                            S.add("pe", MM(ps[3][:, osl], Rr[0:35, hd, csl], selB[0:35, :], False, False),
                                  reads=[b_Rr, b_sel], writes=[pb[3]], partial=True)
                        S.add("pe", MM(p3v, ident[:, :], mneg4[:, :, :], False, True), reads=[b_ident, b_mneg4], writes=[pb[3]], partial=True)
                        S.add("act", ACTV(LT[:, hh * 4:(hh + 1) * 4, :].rearrange("p a b -> p (a b)"), ps[3][:, :], AF.Exp),
                              reads=[pb[3]], writes=[b_LT[k]], partial=(hh > 0))
                        S.add("act", ACTV(Ebc[:, hh * 4:(hh + 1) * 4, :].rearrange("p a b -> p (a b)"), ps[4][:, :], AF.Exp),
                              reads=pbs(4), writes=[b_Ebc[k]], partial=(hh > 0))
                    S.add("dve", TT(xdt.rearrange(h8, h=8), psT[:, 0:512].rearrange(h8, h=8),
                                    dt_c.unsqueeze(2).broadcast_to([128, 8, 64]), ALU.mult),
                          reads=[pb[5], b_dt], writes=[b_xdt[k]])
                    S.add("dve", CP(Btok, psT[:, 512:640]), reads=[pb[5]], writes=[b_Btok[k]])
                    S.add("dve", TT(CBm, ps[5][:, 384:512], U2f[:, :], ALU.mult), reads=[pb[5], b_U2f], writes=[b_CBm[k]])
                    S.add("dve", TT(xdtD.rearrange(h8, h=8), xdt.rearrange(h8, h=8),
                                    LT[:, :, 127:128].broadcast_to([128, 8, 64]), ALU.mult),
                          reads=[b_xdt[k], b_LT[k]], writes=[b_xdtD[k]])
                    S.add("dve", TT(MT, LT, CBm.unsqueeze(1).broadcast_to([128, 8, 128]), ALU.mult),
                          reads=[b_LT[k], b_CBm[k]], writes=[b_MT[k]])
                    S.add("dve", TT(CpT, Ebc, xbcT[:, 5, csl].unsqueeze(1).broadcast_to([128, 8, 128]), ALU.mult),
                          reads=[b_Ebc[k], b_xbc[q2]], writes=[b_CpT[k]])

                def emit_S2(c):
                    tq, cl, k = c // 4, c % 4, c % 2
                    q2 = tq % 2
                    xbcT, gz_t = xbc_r[q2], gz_r[q2]
                    csl = slice(cl * 128, (cl + 1) * 128)
                    MT, Ebc, CpT = MT_r[k], Ebc_r[k], CpT_r[k]
                    xdt, xdtD, Btok = xdt_r[k], xdtD_r[k], Btok_r[k]
                    S.add("pe", MM(ps[7][:, :], Btok, xdtD), reads=[b_Btok[k], b_xdtD[k]], writes=[pb[7]])
                    for pr in range(4):
                        psl = slice(pr * 128, (pr + 1) * 128)
                        for j in range(2):
                            S.add("pe", MM(ps[6][:, psl], Dd[:, pr * 2 + j, :], xbcT[:, pr, csl], j == 0, False),
                                  reads=[b_xbc[q2], b_Dd], writes=[pb[6]], partial=not (pr == 0 and j == 0))
                        for hx in range(2):
                            hd, r0 = pr * 2 + hx, hx * 64
                            S.add("pe", MM(ps[6][r0:r0 + 64, psl], xdt[:, hd * 64:(hd + 1) * 64], MT[:, hd, :], False, False),
                                  reads=[b_xdt[k], b_MT[k]], writes=[pb[6]], partial=True)
                            S.add("pe", MM(ps[6][r0:r0 + 64, psl], prevB[:, hd * 64:(hd + 1) * 64], CpT[:, hd, :], False, True),
                                  reads=[b_prevB, b_CpT[k]], writes=[pb[6]], partial=True)
                    S.add("dve", TT(prevF.rearrange(h8, h=8), prevF.rearrange(h8, h=8),
                                    Ebc[:, :, 127:128].broadcast_to([128, 8, 64]), ALU.mult),
                          reads=[b_prevF, b_Ebc[k]], writes=[b_prevF])
                    S.add("dve", TT(prevF, prevF, ps[7][:, :], ALU.add), reads=[b_prevF, pb[7]], writes=[b_prevF])
                    S.add("pool", CP(prevB, prevF), reads=[b_prevF], writes=[b_prevB])
                    S.add("dve", TT(yg[:, :, csl], ps[6][:, :].rearrange("p (a b) -> p a b", a=4), gz_t[:, :, csl], ALU.mult),
                          reads=[pb[6], b_gz[q2]], writes=[b_yg], partial=True)

                def emit_norm(tq):
                    tsl = slice(tq * 512, (tq + 1) * 512)
                    for ec in range(4):
                        S.add("act", ACTV(sq[:, ec, :], yg[:, ec, :], AF.Square), reads=[b_yg], writes=[b_sq], partial=(ec > 0))
                    for ec in range(4):
                        S.add("pe", MM(ps[7][:, :], onesb[:, :], sq[:, ec, :], ec == 0, ec == 3),
                              reads=[b_sq, b_ones], writes=[pb[7]], partial=(ec > 0))
                    S.add("act", ACTV(lnv_bc, ps[7][:, :], AF.Ln, bias=epsb[:], scale=1.0 / 512), reads=[pb[7], b_eps], writes=[b_lnv])
                    S.add("act", ACTV(rstd_bc, lnv_bc, AF.Exp, scale=-0.5), reads=[b_lnv], writes=[b_rstd])
                    for ec in range(4):
                        e_ = g * 4 + ec
                        S.add("dve", STT(yT[:, e_, tsl], yg[:, ec, :], snw[:, e_:e_ + 1], rstd_bc, ALU.mult, ALU.mult),
                              reads=[b_yg, b_rstd, b_snw], writes=[yTb[e_][tq]])

                import os as _os
                _nointer = _os.environ.get("KDBG_NOINTER") == "1"
                emit_inproj(0)
                pieces = [(0, 3), (3, 6), (6, 8), (8, 10)]
                for c in range(NT + 1):
                    S.record()
                    if c < NT:
                        emit_S1(c)
                    strA1 = S.stop()
                    S.record()
                    if c >= 1:
                        emit_S2(c - 1)
                        if (c - 1) % 4 == 3:
                            emit_norm((c - 1) // 4)
                    strA2 = S.stop()
                    strA = strA1 + strA2
                    S.record()
                    if c < NT and c // 4 + 1 < 4:
                        lo_, hi_ = pieces[c % 4]
                        emit_inproj(c // 4 + 1, lo_, hi_)
                    strB = S.stop()
                    if c % 4 == 0:
                        S.replay_merged(strA, [])
                        S.replay_merged([], strB)
                    else:
                        S.replay_merged(strA, strB)
            if b == 0:
                dbg_store("yssd", yT[:, 0:8, :], [yTb[e][q] for e in range(8) for q in range(4)])
            S.barrier()

            O = 0
            Whp = [carve(O + i * 6144, [8, 384], BF16) for i in range(2)]; O += 12288
            Qa = [[carve(O + (s * 2 + hd) * 4096, [T], BF16) for hd in range(2)] for s in range(2)]; O += 16384
            Ka = [[carve(O + (s * 2 + hd) * 4096, [T], BF16) for hd in range(2)] for s in range(2)]; O += 16384
            Va = [carve(O + s * 8192, [16, 2, 128], BF16) for s in range(2)]
            Va3 = [carve(O + s * 8192, [32, 128], BF16) for s in range(2)]; O += 16384
            PT = [carve(O + i * 1024, [512], BF16) for i in range(3)]; O += 3072
            rec = [carve(O + i * 2048, [512], F32) for i in range(2)]; O += 4096
            WoutT = carve(O, [16, 1024], BF16); O += 32768
            assert O <= ARENA_BYTES
            b_Whp = [gb("Whp0"), gb("Whp1")]
            b_Qa = [[gb(f"Qa{s}{hd}") for hd in range(2)] for s in range(2)]
            b_Ka = [[gb(f"Ka{s}{hd}") for hd in range(2)] for s in range(2)]
            b_Va = [gb("Va0"), gb("Va1")]
            b_PT = [gb(f"PT{i}") for i in range(3)]
            b_rec = [gb("rec0"), gb("rec1")]
            b_WoutT = gb("WoutT")
            S.add("sp", DMA(WoutT, Wov), reads=[b_Wo], writes=[b_WoutT], dma=True, dmabuf=b_WoutT)
            pti = 0
            poi = 0
            def c_inproj(hp):
                s = hp % 2
                S.add("sp", DMA(Whp[s], W1v[:, :, 2560 + hp * 384: 2560 + (hp + 1) * 384]),
                      reads=[b_W1hp[hp]], writes=[b_Whp[s]], dma=True, dmabuf=b_Whp[s])
                for hd in range(2):
                    head = 2 * hp + hd
                    S.add("pool", MS(Ka[s][hd][64:70, :], -1.0), writes=[b_Ka[s][hd]])
                    S.add("pool", MS(Qa[s][hd][64:70, :], 1.0), writes=[b_Qa[s][hd]])
                    S.add("sp", DMA(Ka[s][hd][64:67, :], cs8[head:head + 1, :, :]),
                          reads=[b_cs8], writes=[b_Ka[s][hd]], dma=True, dmabuf=b_Ka[s][hd])
                    S.add("sp", DMA(Qa[s][hd][67:70, :], cs8[head:head + 1, :, :]),
                          reads=[b_cs8], writes=[b_Qa[s][hd]], dma=True, dmabuf=b_Qa[s][hd])
                S.add("pool", MS(Va3[s][:, :, 64:128], 1.0), writes=[b_Va[s]])
                nb = 0
                for which, dst, bdst in ((0, Qa[s], b_Qa[s]), (1, Ka[s], b_Ka[s])):
                    for tq in range(4):
                        bank = nb % 2
                        nb += 1
                        tsl = slice(tq * 512, (tq + 1) * 512)
                        for kc in range(8):
                            S.add("pe", MM(ps[bank][:, :], Whp[s][:, kc, which * 128:(which + 1) * 128], hT[:, kc, tsl], kc == 0, kc == 7),
                                  reads=[b_Whp[s], hTq[tq]], writes=[pb[bank]], partial=(kc > 0))
                        for hd in range(2):
                            S.add("dve", CP(dst[hd][0:64, tsl], ps[bank][hd * 64:(hd + 1) * 64, :]),
                                  reads=[pb[bank]], writes=[bdst[hd]], partial=True)
                vT = yT[:, 8 + hp, :]
                for tq in range(4):
                    bank = nb % 2
                    nb += 1
                    tsl = slice(tq * 512, (tq + 1) * 512)
                    for kc in range(8):
                        S.add("pe", MM(ps[bank][:, :], Whp[s][:, kc, 256:384], hT[:, kc, tsl], kc == 0, kc == 7),
                              reads=[b_Whp[s], hTq[tq]], writes=[pb[bank]], partial=(kc > 0))
                    S.add("dve", CP(vT[:, tsl], ps[bank][:, :]), reads=[pb[bank]], writes=[yTb[8 + hp][tq]])
                psTv = ps[2][:].bitcast(BF16)
                for i0 in range(0, NT, 8):
                    for ii in range(8):
                        i = i0 + ii
                        S.add("pe", TR(psTv[:, ii * 128:(ii + 1) * 128], vT[:, i * 128:(i + 1) * 128], ident[:]),
                              reads=[yTb[8 + hp][i // 4], b_ident], writes=[pb[2]], partial=(ii > 0))
                    for hd in range(2):
                        S.add("dve", CP(Va[s][:, i0:i0 + 8, hd, 0:64],
                                        psTv.rearrange("p (a c d) -> p a c d", a=8, c=2)[:, :, hd, :]),
                              reads=[pb[2]], writes=[b_Va[s]], partial=True)

            def c_attn(hp):
                nonlocal pti, poi
                s = hp % 2
                steps = [(hd, J, kb) for hd in range(2) for J in range(4) for kb in range(4 * J + 4)]
                infos = {}
                LOOK = 2
                for n_ in range(len(steps) + LOOK):
                    if n_ < len(steps):
                        hd, J, kb = steps[n_]
                        pi = pti % 3
                        pti += 1
                        bank = 3 + pi
                        r = kb - 4 * J
                        c0 = max(r, 0) * 128
                        K_ = Ka[s][hd][0:70, kb * 128:(kb + 1) * 128]
                        Qt = Qa[s][hd]
                        rd = [b_Ka[s][hd], b_Qa[s][hd]]
                        if r < 0:
                            S.add("pe", MM(ps[bank][:, 0:512], K_, Qt[0:70, J * 512:(J + 1) * 512]), reads=rd, writes=pbs(bank))
                        else:
                            q0 = J * 512 + c0
                            S.add("pe", MM(ps[bank][:, c0:c0 + 128], K_, Qt[0:70, q0:q0 + 128], True, False), reads=rd, writes=pbs(bank))
                            S.add("pe", MM(ps[bank][:, c0:c0 + 128], ident[:, :], mneg[:, :], False, True),
                                  reads=[b_ident, b_mneg], writes=pbs(bank), partial=True)
                            if c0 + 128 < 512:
                                S.add("pe", MM(ps[bank][:, c0 + 128:512], K_, Qt[0:70, q0 + 128:J * 512 + 512]),
                                      reads=rd, writes=pbs(bank), partial=True)
                        S.add("act", ACTV(PT[pi][:, c0:512], ps[bank][:, c0:512], AF.Exp, scale=0.125), reads=pbs(bank), writes=[b_PT[pi]])
                        infos[n_] = (pi, c0)
                    if n_ >= LOOK:
                        hd, J, kb = steps[n_ - LOOK]
                        pi, c0 = infos[n_ - LOOK]
                        if kb == 0:
                            poi += 1
                        ob = 6 + (poi % 2)
                        last = (kb == 4 * J + 3)
                        S.add("pe", MM(ps[ob][:, c0:512], Va[s][:, kb, hd, :], PT[pi][:, c0:512], kb == 0, last),
                              reads=[b_Va[s], b_PT[pi]], writes=[pb[ob]], partial=(kb > 0))
                        if last:
                            ri = poi % 2
                            r0 = hd * 64
                            S.add("dve", RCP(rec[ri][64:128, :], ps[ob][64:128, :]), reads=[pb[ob]], writes=[b_rec[ri]])
                            S.add("dve", TT(yT[r0:r0 + 64, 8 + hp, J * 512:(J + 1) * 512], ps[ob][0:64, :], rec[ri][64:128, :], ALU.mult),
                                  reads=[pb[ob], b_rec[ri]], writes=[yTb[8 + hp][J]], partial=True)

            c_inproj(0)
            for hp in range(8):
                S.record()
                c_attn(hp)
                strA = S.stop()
                S.record()
                if hp < 7:
                    c_inproj(hp + 1)
                strB = S.stop()
                S.replay_merged(strA, strB)
            if b == 0:
                dbg_store("yatt", yT[:, 8:16, :], [yTb[e][q] for e in range(8, 16) for q in range(4)])
            S.barrier()

            xr = ring("xr", [carve(i * 4096, [1024], F32) for i in range(3)])
            xnr = ring("xn", [carve(12288 + i * 2048, [1024], BF16) for i in range(2)])
            JUNK[0] = carve(16384, [1024], BF16)
            h1r = ring("h1t", [carve(18432 + i * 4096, [1024], F32) for i in range(3)])
            pend_nt = None
            for i in range(NT):
                xs = xr.next()
                S.add("sp", DMA(xs.t, x_d[b, i * 128:(i + 1) * 128, :]), writes=[xs.b], dma=True, dmabuf=xs.b)
                hs = h1r.next()
                if pend_nt is not None:
                    nt_a(pend_nt[0], pend_nt[1], pend_nt[3])
                for half in range(2):
                    bank = (i % 2) * 2 + half
                    hsl = slice(half * 512, (half + 1) * 512)
                    for ec in range(16):
                        S.add("pe", MM(ps[bank][:, :], yT[:, ec, i * 128:(i + 1) * 128], WoutT[:, ec, hsl], ec == 0, ec == 15),
                              reads=[yTb[ec][i // 4], b_WoutT], writes=[pb[bank]], partial=(ec > 0))
                    S.add("dve", TT(hs.t[:, hsl], ps[bank][:, :], xs.t[:, hsl], ALU.add),
                          reads=[pb[bank], xs.b], writes=[hs.b], partial=(half > 0))
                if pend_nt is not None:
                    nt_b(pend_nt[2], wn2[:, :], b_wn2, pend_nt[3], pend_nt[4])
                S.add("act", DMA(out_d[b, i * 128:(i + 1) * 128, :], hs.t), reads=[hs.b], writes=[outb[i]], dma=True, dmabuf=hs.sd)
                pend_nt = (hs.t, hs.b, i, xnr.next(), 4 + (i % 2))
            nt_a(pend_nt[0], pend_nt[1], pend_nt[3])
            nt_b(pend_nt[2], wn2[:, :], b_wn2, pend_nt[3], pend_nt[4])
            S.barrier()

            O = 0
            Wupr = ring("Wup", [carve(O + i * 8192, [8, 512], BF16) for i in range(2)]); O += 16384
            Wdnr = ring("Wdn", [carve(O + i * 8192, [4, 1024], BF16) for i in range(3)]); O += 24576
            uT = carve(O, [32, 512], BF16); O += 32768
            h1l = ring("h1l", [carve(O + i * 4096, [1024], F32) for i in range(4)]); O += 16384
            otr = ring("ot", [carve(O + i * 4096, [1024], F32) for i in range(2)]); O += 8192
            rtr = ring("rt", [carve(O + i * 1024, [512], BF16) for i in range(2)]); O += 2048
            JUNK[0] = carve(O, [1024], BF16); O += 2048
            assert O <= ARENA_BYTES
            b_uT = [gb(f"uT{i}") for i in range(32)]
            for tg in range(4):
                tsl = slice(tg * 512, (tg + 1) * 512)
                nb = 0
                h1s = []
                for ti in range(4):
                    i = tg * 4 + ti
                    hs = h1l.next()
                    h1s.append(hs)
                    S.add("act", DMA(hs.t, out_d[b, i * 128:(i + 1) * 128, :]), reads=[outb[i]], writes=[hs.b], dma=True, dmabuf=hs.b)
                for fg in range(8):
                    ws = Wupr.next()
                    S.add("sp", DMA(ws.t, Wuv[:, :, fg * 512:(fg + 1) * 512]), reads=[b_Wu], writes=[ws.b], dma=True, dmabuf=ws.b)
                    for fj in range(4):
                        fc = fg * 4 + fj
                        bank = nb % 8
                        nb += 1
                        for kc in range(8):
                            S.add("pe", MM(ps[bank][:, :], ws.t[:, kc, fj * 128:(fj + 1) * 128], hT[:, kc, tsl], kc == 0, kc == 7),
                                  reads=[ws.b, hTq[tg]], writes=pbs(bank), partial=(kc > 0))
                        rs = rtr.next()
                        S.add("act", ACTV(rs.t, ps[bank][:, :], AF.Relu), reads=pbs(bank), writes=[rs.b])
                        S.add("pool", TT(uT[:, fc, :], rs.t, rs.t, ALU.mult), reads=[rs.b], writes=[b_uT[fc]])
                for fg in range(8):
                    ws = Wdnr.next()
                    S.add("sp", DMA(ws.t, Wdv[:, fg * 4:(fg + 1) * 4, :]), reads=[b_Wd], writes=[ws.b], dma=True, dmabuf=ws.b)
                    for ti in range(4):
                        for half in range(2):
                            bank = ti * 2 + half
                            for fj in range(4):
                                fc = fg * 4 + fj
                                S.add("pe", MM(ps[bank][:, :], uT[:, fc, ti * 128:(ti + 1) * 128], ws.t[:, fj, half * 512:(half + 1) * 512],
                                               fc == 0, fc == 31),
                                      reads=[ws.b, b_uT[fc]], writes=pbs(bank), partial=(fc > 0))
                for ti in range(4):
                    i = tg * 4 + ti
                    hs = h1s[ti]
                    for half in range(2):
                        bank = ti * 2 + half
                        hsl = slice(half * 512, (half + 1) * 512)
                        S.add("dve", TT(hs.t[:, hsl], ps[bank][:, :], hs.t[:, hsl], ALU.add), reads=pbs(bank) + [hs.b], writes=[hs.b])
                    rstd, sbuf_ = rms_stats(hs.t, hs.b)
                    os_ = otr.next()
                    S.add("dve", STT(os_.t, hs.t, rstd, nfw[:, :], ALU.mult, ALU.mult), reads=[hs.b, sbuf_, b_nfw], writes=[os_.b])
                    S.add("act", DMA(out_d[b, i * 128:(i + 1) * 128, :], os_.t), reads=[os_.b, outb[i]], writes=[outb[i]],
                          dma=True, dmabuf=os_.sd)
            S.barrier()

        run_sched(nc, S)
    return nc


_NC_CACHE = {}


def kernel(**inputs):
    x = np.ascontiguousarray(inputs["x"], dtype=np.float32)
    nb = x.shape[0]
    per = nb // NCORES
    if per not in _NC_CACHE:
        _NC_CACHE[per] = build_nc(per)
    nc = _NC_CACHE[per]
    names = ["norm_mix_w", "w_in", "conv_w", "conv_b", "dt_bias", "a_log", "d_skip", "ssd_norm_w", "f_bias",
             "w_out", "norm_mlp_w", "w_up", "w_down", "norm_final_w"]
    shared = {n: np.ascontiguousarray(inputs[n], dtype=np.float32) for n in names}
    in_maps = []
    for c in range(NCORES):
        m = dict(shared)
        m["x"] = np.ascontiguousarray(x[c * per:(c + 1) * per])
        in_maps.append(m)
    res = run_bass_kernel_spmd(nc, in_maps, core_ids=list(range(NCORES)))
    return np.concatenate([r["out"] for r in res.results], axis=0).astype(np.float32)
```

```python
import numpy as np
from contextlib import ExitStack
import concourse.bass as bass
import concourse.mybir as mybir
from concourse.bass_utils import run_bass_kernel_spmd
from concourse.ap import AP

F32 = mybir.dt.float32
BF16 = mybir.dt.bfloat16
AF = mybir.ActivationFunctionType
ALU = mybir.AluOpType

NCORES = 8
D = 1024
T = 2048
NT = T // 128
EIN = 5664
EPS = 1e-5
ENGS = ("pe", "act", "dve", "pool", "sp")


class Buf:
    __slots__ = ("name", "writers", "readers", "dsem", "dcount")

    def __init__(self, name):
        self.name = name
        self.writers = []
        self.readers = []
        self.dsem = None
        self.dcount = 0


class Op:
    __slots__ = ("eng", "fn", "deps", "is_dma", "token", "signal", "dmabuf", "idx", "ndma")


class Sched:
    def __init__(self):
        self.ops = []
        self.bufs = []
        self.last = {e: None for e in ENGS}
        self.dma_since_barrier = []

    def buf(self, name):
        b = Buf(name)
        self.bufs.append(b)
        return b

    def record(self):
        self.rec = []

    def stop(self):
        r, self.rec = self.rec, None
        return r

    @staticmethod
    def merge_lists(A, Bl):
        out = []
        na, nb = len(A), len(Bl)
        ia = ib = 0
        while ia < na or ib < nb:
            if ib >= nb or (ia < na and ia * nb <= ib * na):
                out.append(A[ia]); ia += 1
            else:
                out.append(Bl[ib]); ib += 1
        return out

    def replay_merged(self, A, Bl):
        na, nb = len(A), len(Bl)
        ia = ib = 0
        while ia < na or ib < nb:
            if ib >= nb or (ia < na and ia * nb <= ib * na):
                a, kw = A[ia]; ia += 1
            else:
                a, kw = Bl[ib]; ib += 1
            self.add(*a, **kw)

    def add(self, eng, fn, reads=(), writes=(), dma=False, dmabuf=None, ndma=1, same_eng_ok=None,
            partial=False, extra_deps=()):
        if getattr(self, "rec", None) is not None:
            self.rec.append(((eng, fn), dict(reads=list(reads), writes=list(writes), dma=dma, dmabuf=dmabuf, ndma=ndma,
                                             same_eng_ok=same_eng_ok, partial=partial, extra_deps=tuple(extra_deps))))
            return None
        if same_eng_ok is None:
            same_eng_ok = (eng == "pe")
        op = Op()
        op.eng, op.fn, op.is_dma, op.dmabuf, op.ndma = eng, fn, dma, dmabuf, ndma
        op.deps, op.token, op.signal = [], None, False
        op.idx = len(self.ops)
        deps = set(extra_deps)
        for r in reads:
            deps.update(r.writers)
        for w in writes:
            if not partial:
                deps.update(w.writers)
            deps.update(w.readers)
        for r in reads:
            r.readers.append(op.idx)
        for w in writes:
            if partial:
                w.writers.append(op.idx)
            else:
                w.writers = [op.idx]
            w.readers = []
        deps.discard(op.idx)
        for d in sorted(deps):
            dop = self.ops[d]
            if same_eng_ok and dop.eng == eng and not dop.is_dma and not dma:
                continue
            op.deps.append(d)
            dop.signal = True
        self.ops.append(op)
        if fn is not None:
            self.last[eng] = op.idx
        if dma:
            assert dmabuf is not None
            self.dma_since_barrier.append(op.idx)
        return op

    def barrier(self):
        deps = [v for v in self.last.values() if v is not None] + list(self.dma_since_barrier)
        self.dma_since_barrier = []
        for e in ENGS:
            self.add(e, None, extra_deps=deps, same_eng_ok=False)


def run_sched(nc, sched):
    ops = sched.ops
    cnt = {e: 0 for e in ENGS}
    for op in ops:
        if op.is_dma:
            b = op.dmabuf
            b.dcount += 16 * op.ndma
            op.token = ("d", b, b.dcount)
        elif op.signal:
            cnt[op.eng] += 1
            op.token = ("e", op.eng, cnt[op.eng])
    dma_bufs = [b for b in sched.bufs if b.dcount > 0]
    with ExitStack() as es:
        esem = {e: es.enter_context(nc.semaphore(f"s_{e}")) for e in ENGS}
        for b in dma_bufs:
            b.dsem = es.enter_context(nc.semaphore(f"d_{b.name}"))
        block = es.enter_context(nc.Block())
        per_eng = {e: [o for o in ops if o.eng == e] for e in ENGS}

        def emit_engine(e, handle):
            waited = {}
            for op in per_eng[e]:
                need = {}
                for d in op.deps:
                    t = ops[d].token
                    if t[0] == "e":
                        key = ("e", t[1])
                        sem = esem[t[1]]
                    else:
                        key = ("d", id(t[1]))
                        sem = t[1].dsem
                    if need.get(key, (None, 0))[1] < t[2]:
                        need[key] = (sem, t[2])
                for key, (sem, val) in need.items():
                    if waited.get(key, 0) >= val:
                        continue
                    waited[key] = val
                    handle.wait_ge(sem, val)
                if op.fn is None:
                    continue
                ins = op.fn(handle)
                if op.is_dma:
                    if not isinstance(ins, (list, tuple)):
                        ins = [ins]
                    assert len(ins) == op.ndma
                    for i in ins:
                        i.then_inc(op.dmabuf.dsem, 16)
                elif op.signal:
                    ins.then_inc(esem[e], 1)
            last = {}
            for op in per_eng[e]:
                if op.is_dma:
                    last[id(op.dmabuf)] = (op.dmabuf.dsem, op.token[2])
            for key, (sem, val) in last.items():
                if waited.get(("d", key), 0) < val:
                    handle.wait_ge(sem, val)

        @block.sync
        def _(h):
            emit_engine("sp", h)

        @block.tensor
        def _(h):
            emit_engine("pe", h)

        @block.scalar
        def _(h):
            emit_engine("act", h)

        @block.vector
        def _(h):
            emit_engine("dve", h)

        @block.gpsimd
        def _(h):
            emit_engine("pool", h)
    return cnt


class Slot:
    def __init__(self, t, b, sd=None):
        self.t, self.b, self.sd = t, b, sd


class Ring:
    def __init__(self, slots):
        self.slots = slots
        self.i = 0

    def next(self):
        s = self.slots[self.i % len(self.slots)]
        self.i += 1
        return s

    def retarget(self, tensors):
        for s, t in zip(self.slots, tensors):
            s.t = t


def MM(out, lhsT, rhs, start=True, stop=True):
    return lambda h: h.matmul(out, lhsT=lhsT, rhs=rhs, start=start, stop=stop)


def TR(out, in_, ident):
    return lambda h: h.transpose(out=out, in_=in_, identity=ident)


def ACTV(out, in_, func, bias=None, scale=None, accum_out=None):
    kw = {}
    if bias is not None:
        kw["bias"] = bias
    if scale is not None:
        kw["scale"] = scale
    if accum_out is not None:
        kw["accum_out"] = accum_out
    return lambda h: h.activation(out=out, in_=in_, func=func, **kw)


def TT(out, in0, in1, op):
    return lambda h: h.tensor_tensor(out=out, in0=in0, in1=in1, op=op)


def TS(out, in0, s1, s2, op0, op1=None):
    if op1 is None:
        return lambda h: h.tensor_scalar(out=out, in0=in0, scalar1=s1, scalar2=None, op0=op0)
    return lambda h: h.tensor_scalar(out=out, in0=in0, scalar1=s1, scalar2=s2, op0=op0, op1=op1)


def STT(out, in0, scalar, in1, op0, op1):
    return lambda h: h.scalar_tensor_tensor(out=out, in0=in0, scalar=scalar, in1=in1, op0=op0, op1=op1)


def CP(out, in_):
    return lambda h: h.tensor_copy(out=out, in_=in_)


def MS(ap, val):
    return lambda h: h.memset(ap, val)


def DMA(out, in_, slow=False):
    return lambda h: h.dma_start(out=out, in_=in_, allow_slow_non_contiguous=slow)


def RCP(out, in_):
    return lambda h: h.reciprocal(out=out, in_=in_)


def RCPF(out, in_):
    return lambda h: h.reciprocal_approx_fast(out=out, in_=in_)


def SCAN(out, d0, d1, init):
    return lambda h: h.tensor_tensor_scan(out=out, data0=d0, data1=d1, initial=init, op0=ALU.add, op1=ALU.add)


def ASEL(out, in_, pattern, op, fill, base, cm):
    return lambda h: h.affine_select(out=out, in_=in_, pattern=pattern, compare_op=op, fill=fill, base=base, channel_multiplier=cm)


ARENA_BYTES = 100 * 1024


def build_nc(nseq, debug=None):
    nc = bass.Bass("TRN2", target_bir_lowering=False)
    S = Sched()

    def dram_in(name, shape):
        return nc.dram_tensor(name, list(shape), F32, kind="ExternalInput")

    x_h = dram_in("x", [nseq, T, D])
    p_norm_mix = dram_in("norm_mix_w", [1, D])
    p_w_in = dram_in("w_in", [1, D, EIN])
    p_conv_w = dram_in("conv_w", [1, 4, 1536])
    p_conv_b = dram_in("conv_b", [1, 1536])
    p_dt_bias = dram_in("dt_bias", [1, 16])
    p_a_log = dram_in("a_log", [1, 16])
    p_d_skip = dram_in("d_skip", [1, 16])
    p_ssd_norm = dram_in("ssd_norm_w", [1, D])
    p_f_bias = dram_in("f_bias", [1, 16])
    p_w_out = dram_in("w_out", [1, 2048, D])
    p_norm_mlp = dram_in("norm_mlp_w", [1, D])
    p_w_up = dram_in("w_up", [1, D, 4096])
    p_w_down = dram_in("w_down", [1, 4096, D])
    p_norm_final = dram_in("norm_final_w", [D])
    out_h = nc.dram_tensor("out", [nseq, T, D], F32, kind="ExternalOutput")
    x_d = x_h.ap()
    out_d = out_h.ap()
    W1 = nc.dram_tensor("W1s", [D, EIN], BF16, kind="Internal").ap()
    Wo = nc.dram_tensor("Wos", [2048, D], BF16, kind="Internal").ap()
    Wu = nc.dram_tensor("Wus", [D, 4096], BF16, kind="Internal").ap()
    Wd = nc.dram_tensor("Wds", [4096, D], BF16, kind="Internal").ap()
    cs8 = nc.dram_tensor("cs8s", [16, 3, T], BF16, kind="Internal").ap()
    ac6 = nc.dram_tensor("ac6s", [16, 6, T], BF16, kind="Internal").ap()
    dbg_d = None
    if debug:
        dbg_d = {k: nc.dram_tensor("dbg_" + k, list(shp), F32, kind="ExternalOutput").ap() for k, shp in debug.items()}

    def rawap(handle, offset, pat):
        return AP(handle, offset, [list(p) for p in pat])

    with ExitStack() as es:
        def sb(name, shape, dt):
            return es.enter_context(nc.sbuf_tensor(name, list(shape), dt))

        hT = sb("hT", [128, 8, T], BF16)
        yT = sb("yT", [128, 16, T], BF16)
        arena = sb("arena", [128, ARENA_BYTES // 2], BF16)
        ident = sb("ident", [128, 128], BF16)
        onesb = sb("onesb", [128, 128], BF16)
        U1 = sb("U1", [128, 128], BF16)
        U2b = sb("U2b", [128, 128], BF16)
        U2f = sb("U2f", [128, 128], F32)
        mneg = sb("mneg", [128, 128], BF16)
        mneg4 = sb("mneg4", [128, 4, 128], BF16)
        selA = sb("selA", [35, 128], BF16)
        selB = sb("selB", [35, 128], BF16)
        cf = sb("cf", [128, 128], F32)
        wn1 = sb("wn1", [128, 8], F32)
        wn2 = sb("wn2", [128, 8], F32)
        snw = sb("snw", [128, 8], F32)
        dsk = sb("dsk", [128, 8], F32)
        nfw = sb("nfw", [128, D], F32)
        cw = sb("cw", [128, 12, 4], F32)
        cb = sb("cb", [128, 12], F32)
        cbh = sb("cbh", [128, 12], F32)
        dtb_bc = sb("dtb_bc", [128, 16], F32)
        A_bc = sb("A_bc", [128, 16], F32)
        nfb = sb("nfb", [16, 1], F32)
        dtb_p = sb("dtb_p", [16, 1], F32)
        A_p = sb("A_p", [16, 1], F32)
        epsb = sb("epsb", [128, 1], F32)
        Wdtf = sb("Wdtf", [128, 8, 32], BF16)
        stat = sb("stat", [128, 4, 4], F32)
        dt_all = sb("dt_all", [128, 16, 16], F32)
        a_all = sb("a_all", [128, 16, 16], F32)
        ps = [es.enter_context(nc.psum_tensor(f"ps{i}", [128, 512], F32)) for i in range(8)]
        pb = [S.buf(f"pb{i}") for i in range(8)]
        pb4b = S.buf("pb4b")

        def pbs(bank):
            return [pb[4], pb4b] if bank == 4 else [pb[bank]]

        def carve(off, shape, dt):
            esz = 2 if dt == BF16 else 4
            n = int(np.prod(shape))
            assert off % 4 == 0 and off + n * esz <= ARENA_BYTES, (off, shape)
            v = arena[:, off // 2: off // 2 + n * esz // 2]
            if dt != BF16:
                v = v.bitcast(dt)
            if len(shape) == 2:
                v = v.rearrange("p (a b) -> p a b", a=shape[0])
            elif len(shape) == 3:
                v = v.rearrange("p (a b c) -> p a b c", a=shape[0], b=shape[1])
            return v

        B = {}

        def gb(name):
            if name not in B:
                B[name] = S.buf(name)
            return B[name]

        def ring(name, tensors):
            return Ring([Slot(t, gb(f"{name}{i}"), gb(f"{name}{i}_st")) for i, t in enumerate(tensors)])

        hTq = [gb(f"hTq{i}") for i in range(4)]
        yTb = [[gb(f"yT{e}_{q}") for q in range(4)] for e in range(16)]
        outb = [gb(f"outb{i}") for i in range(NT)]

        cst = gb("consts")

        def small_dma(dst, src, nm, slow=False):
            b_ = gb("c_" + nm)
            S.add("sp", DMA(dst, src, slow), writes=[b_], dma=True, dmabuf=b_)
            return b_

        b_wn1 = small_dma(wn1[:], rawap(p_norm_mix, 0, [[1, 128], [128, 8]]), "wn1", True)
        b_wn2 = small_dma(wn2[:], rawap(p_norm_mlp, 0, [[1, 128], [128, 8]]), "wn2", True)
        b_snw = small_dma(snw[:], rawap(p_ssd_norm, 0, [[1, 128], [128, 8]]), "snw", True)
        b_nfw = small_dma(nfw[:], rawap(p_norm_final, 0, [[0, 128], [1, D]]), "nfw")
        b_cw = gb("c_cw")
        for k in range(4):
            S.add("sp", DMA(cw[:, :, k], rawap(p_conv_w, k * 1536, [[1, 128], [128, 12]]), True),
                  writes=[b_cw], dma=True, dmabuf=b_cw, partial=(k > 0))
        b_cb = small_dma(cb[:], rawap(p_conv_b, 0, [[1, 128], [128, 12]]), "cb", True)
        b_cbh = gb("c_cbh")
        S.add("dve", TS(cbh[:], cb[:], 0.5, None, ALU.mult), reads=[b_cb], writes=[b_cbh])
        b_dsk = gb("c_dsk")
        for hh in range(2):
            S.add("sp", DMA(dsk[hh * 64:(hh + 1) * 64, :], rawap(p_d_skip, hh, [[0, 64], [2, 8]]), True),
                  writes=[b_dsk], dma=True, dmabuf=b_dsk, partial=(hh > 0))
        b_dtb = small_dma(dtb_bc[:], rawap(p_dt_bias, 0, [[0, 128], [1, 16]]), "dtb")
        b_A = small_dma(A_bc[:], rawap(p_a_log, 0, [[0, 128], [1, 16]]), "A")
        b_nfb = small_dma(nfb[:], rawap(p_f_bias, 0, [[1, 16], [1, 1]]), "nfb")
        b_dtbp = small_dma(dtb_p[:], rawap(p_dt_bias, 0, [[1, 16], [1, 1]]), "dtbp")
        b_Ap = small_dma(A_p[:], rawap(p_a_log, 0, [[1, 16], [1, 1]]), "Ap")
        S.add("act", ACTV(A_p[:], A_p[:], AF.Exp), reads=[b_Ap], writes=[b_Ap])
        S.add("dve", TS(A_p[:], A_p[:], -1.0, None, ALU.mult), reads=[b_Ap], writes=[b_Ap])
        S.add("act", ACTV(A_bc[:], A_bc[:], AF.Exp), reads=[b_A], writes=[b_A])
        S.add("dve", TS(A_bc[:], A_bc[:], -1.0, None, ALU.mult), reads=[b_A], writes=[b_A])
        S.add("dve", TS(nfb[:], nfb[:], -1.0, None, ALU.mult), reads=[b_nfb], writes=[b_nfb])
        S.add("dve", MS(epsb[:], EPS), writes=[gb("c_eps")])
        S.add("pool", MS(cf[:], 1.0), writes=[cst])
        S.add("dve", CP(onesb[:], cf[:]), reads=[cst], writes=[gb("c_ones")])
        S.add("pool", ASEL(U2f[:], cf[:], [[1, 128]], ALU.is_ge, 0.0, 0, -1), reads=[cst], writes=[gb("c_U2f")])
        S.add("dve", CP(U2b[:], U2f[:]), reads=[gb("c_U2f")], writes=[gb("c_U2b")])
        S.add("pool", ASEL(cf[:], cf[:], [[-1, 128]], ALU.is_gt, 0.0, 0, 1), reads=[cst, gb("c_ones"), gb("c_U2f")], writes=[cst])
        S.add("dve", CP(U1[:], cf[:]), reads=[cst], writes=[gb("c_U1")])
        S.add("dve", TS(mneg[:], cf[:], -30000.0, None, ALU.mult), reads=[cst], writes=[gb("c_mneg")])
        S.add("pool", MS(cf[:], 1.0), reads=[gb("c_U1"), gb("c_mneg")], writes=[cst])
        S.add("pool", ASEL(cf[:], cf[:], [[-1, 128]], ALU.is_equal, 0.0, 0, 1), reads=[cst], writes=[cst])
        S.add("dve", CP(ident[:], cf[:]), reads=[cst], writes=[gb("c_ident")])
        b_mneg4 = gb("c_mneg4")
        S.add("dve", CP(mneg4[:], mneg[:, :].unsqueeze(1).broadcast_to([128, 4, 128])), reads=[gb("c_mneg")], writes=[b_mneg4])
        b_sel = gb("c_sel")
        S.add("pool", MS(selA[:], 0.0), writes=[b_sel])
        S.add("pool", MS(selB[:], 0.0), writes=[b_sel], partial=True)
        S.add("pool", MS(selA[0:3, :], 1.0), reads=[b_sel], writes=[b_sel])
        S.add("pool", MS(selB[32:35, :], 1.0), reads=[b_sel], writes=[b_sel])
        b_eps = gb("c_eps")
        b_ident, b_ones, b_U1, b_U2b, b_U2f, b_mneg = (gb(n) for n in ["c_ident", "c_ones", "c_U1", "c_U2b", "c_U2f", "c_mneg"])

        win = p_w_in.ap()[0]
        segs = []
        for g in range(2):
            base = g * 1280
            segs.append((base, g * 512, 512))
            segs.append((base + 512, 1024 + g * 512, 512))
            segs.append((base + 1024, 2048 + g * 128, 128))
            segs.append((base + 1152, 2304 + g * 128, 128))
        for hp in range(8):
            base = 2560 + hp * 384
            segs.append((base, 2576 + hp * 128, 128))
            segs.append((base + 128, 3600 + hp * 128, 128))
            segs.append((base + 256, 4624 + hp * 128, 128))
        segs.append((5632, 2560, 16))
        segs.append((5648, 5648, 16))
        b_W1g = [gb("W1g0"), gb("W1g1")]
        b_W1hp = [gb(f"W1hp{i}") for i in range(8)]
        b_W1dtf = gb("W1dtf")
        seg_buf = []
        for g in range(2):
            seg_buf += [b_W1g[g]] * 4
        for hp in range(8):
            seg_buf += [b_W1hp[hp]] * 3
        seg_buf += [b_W1dtf, b_W1dtf]
        order_ = [32, 33] + list(range(0, 32))
        seen_ = set()
        for n_ in order_:
            d0, s0, n = segs[n_]
            bb = seg_buf[n_]
            S.add("pool", DMA(W1[:, d0:d0 + n], win[:, s0:s0 + n]), writes=[bb], dma=True, dmabuf=bb, partial=(id(bb) in seen_))
            seen_.add(id(bb))
        b_Wo, b_Wu, b_Wd = gb("Wo"), gb("Wu"), gb("Wd")
        wout = p_w_out.ap()[0]
        wup = p_w_up.ap()[0]
        wdn = p_w_down.ap()[0]
        for i in range(4):
            S.add("pool", DMA(Wo[i * 512:(i + 1) * 512, :], wout[i * 512:(i + 1) * 512, :]), writes=[b_Wo], dma=True, dmabuf=b_Wo,
                  partial=(i > 0))
        for i in range(4):
            S.add("pool", DMA(Wu[i * 256:(i + 1) * 256, :], wup[i * 256:(i + 1) * 256, :]), writes=[b_Wu], dma=True, dmabuf=b_Wu,
                  partial=(i > 0))
        for i in range(4):
            S.add("pool", DMA(Wd[i * 1024:(i + 1) * 1024, :], wdn[i * 1024:(i + 1) * 1024, :]), writes=[b_Wd], dma=True, dmabuf=b_Wd,
                  partial=(i > 0))
        W1v = W1.rearrange("(kc p) e -> p kc e", p=128)
        Wov = Wo.rearrange("(ec p) d -> p ec d", p=128)
        Wuv = Wu.rearrange("(kc p) f -> p kc f", p=128)
        Wdv = Wd.rearrange("(fc p) d -> p fc d", p=128)
        b_Wdtf = gb("Wdtf")
        S.add("sp", DMA(Wdtf[:], W1v[:, :, 5632:5664]), reads=[b_W1dtf], writes=[b_Wdtf], dma=True, dmabuf=b_Wdtf)

        stat_b = [gb(f"stat{i}") for i in range(4)]
        stat_i = [0]
        JUNK = [None]

        def rms_stats(src_ap, src_buf):
            k = stat_i[0] % 4
            stat_i[0] += 1
            sbuf_ = stat_b[k]
            S.add("act", ACTV(JUNK[0], src_ap, AF.Square, accum_out=stat[:, k, 0:1]), reads=[src_buf], writes=[sbuf_])
            S.add("act", ACTV(stat[:, k, 1:2], stat[:, k, 0:1], AF.Ln, bias=epsb[:], scale=1.0 / D), reads=[sbuf_, b_eps], writes=[sbuf_])
            S.add("act", ACTV(stat[:, k, 2:3], stat[:, k, 1:2], AF.Exp, scale=-0.5), reads=[sbuf_], writes=[sbuf_])
            return stat[:, k, 2:3], sbuf_

        def norm_transpose(src_ap, src_buf, i, wn, wn_buf, xn_slot, bank):
            rstd, sbuf_ = rms_stats(src_ap, src_buf)
            S.add("dve", TS(xn_slot.t, src_ap, rstd, None, ALU.mult), reads=[src_buf, sbuf_], writes=[xn_slot.b])
            psT = ps[bank][:].bitcast(BF16)
            for kc in range(8):
                S.add("pe", TR(psT[:, kc * 128:(kc + 1) * 128], xn_slot.t[:, kc * 128:(kc + 1) * 128], ident[:]),
                      reads=[xn_slot.b, b_ident], writes=pbs(bank), partial=(kc > 0))
            S.add("dve", TT(hT[:, :, i * 128:(i + 1) * 128], psT.rearrange("p (k t) -> p k t", k=8),
                            wn.unsqueeze(2).broadcast_to([128, 8, 128]), ALU.mult),
                  reads=pbs(bank) + [wn_buf], writes=[hTq[i // 4]], partial=True)

        def nt_a(src_ap, src_buf, xn_slot):
            rstd, sbuf_ = rms_stats(src_ap, src_buf)
            S.add("dve", TS(xn_slot.t, src_ap, rstd, None, ALU.mult), reads=[src_buf, sbuf_], writes=[xn_slot.b])

        def nt_b(i, wn, wn_buf, xn_slot, bank):
            psT = ps[bank][:].bitcast(BF16)
            for kc in range(8):
                S.add("pe", TR(psT[:, kc * 128:(kc + 1) * 128], xn_slot.t[:, kc * 128:(kc + 1) * 128], ident[:]),
                      reads=[xn_slot.b, b_ident], writes=pbs(bank), partial=(kc > 0))
            S.add("dve", TT(hT[:, :, i * 128:(i + 1) * 128], psT.rearrange("p (k t) -> p k t", k=8),
                            wn.unsqueeze(2).broadcast_to([128, 8, 128]), ALU.mult),
                  reads=pbs(bank) + [wn_buf], writes=[hTq[i // 4]], partial=True)

        def dbg_store(key, src_ap, src_bufs):
            if dbg_d is None or key not in dbg_d:
                return
            b_ = gb("dbgst_" + key)
            S.add("pool", DMA(dbg_d[key], src_ap), reads=list(src_bufs), dma=True, dmabuf=b_)

        for b in range(nseq):
            xr = ring("xr", [carve(i * 4096, [1024], F32) for i in range(3)])
            xnr = ring("xn", [carve(12288 + i * 2048, [1024], BF16) for i in range(2)])
            JUNK[0] = carve(16384, [1024], BF16)
            for i in range(NT):
                xs = xr.next()
                S.add("sp", DMA(xs.t, x_d[b, i * 128:(i + 1) * 128, :]), writes=[xs.b], dma=True, dmabuf=xs.b)
                norm_transpose(xs.t, xs.b, i, wn1[:, :], b_wn1, xnr.next(), i % 2)
            if b == 0:
                dbg_store("hT", hT[:, :, :], hTq)

            OFF0 = 18432
            tmp256 = carve(OFF0, [256], F32)
            fe = carve(OFF0 + 1024, [512], F32)
            csq = [carve(OFF0 + 3072 + i * 2048, [512], F32) for i in range(2)]
            rr = carve(OFF0 + 7168, [512], F32)
            zer = carve(OFF0 + 9216, [512], F32)
            SPq = carve(OFF0 + 11264, [3, 512], BF16)
            b_t256, b_fe, b_rr, b_zer, b_SPq = gb("t256"), gb("fe"), gb("rr"), gb("zer"), gb("SPq")
            b_csq = [gb("csq0"), gb("csq1")]
            b_dt, b_a = gb("dt_all"), gb("a_all")
            b_cs8 = gb("cs8")
            ae = carve(OFF0 + 14336, [512], F32)
            acs = [carve(OFF0 + 16384 + i * 2048, [512], F32) for i in range(2)]
            acm = carve(OFF0 + 20480, [512], F32)
            rr2 = carve(OFF0 + 22528, [512], F32)
            bs4 = carve(OFF0 + 24576, [4], F32)
            SA = carve(OFF0 + 24592, [6, 512], BF16)
            b_ae, b_acm, b_rr2, b_bs4, b_SA = gb("ae"), gb("acm"), gb("rr2"), gb("bs4"), gb("SA")
            b_acs = [gb("acs0"), gb("acs1")]
            b_ac6 = gb("ac6")
            for c in range(NT):
                for kc in range(8):
                    S.add("pe", MM(ps[2][:, c * 16:(c + 1) * 16], hT[:, kc, c * 128:(c + 1) * 128], Wdtf[:, kc, 0:16], kc == 0, kc == 7),
                          reads=[hTq[c // 4], b_Wdtf], writes=[pb[2]], partial=not (c == 0 and kc == 0))
            S.add("dve", TT(tmp256.rearrange("p (c h) -> p c h", c=16), ps[2][:, 0:256].rearrange("p (c h) -> p c h", c=16),
                            dtb_bc[:, :].unsqueeze(1).broadcast_to([128, 16, 16]), ALU.add),
                  reads=[pb[2], b_dtb], writes=[b_t256])
            S.add("act", ACTV(tmp256, tmp256, AF.Exp), reads=[b_t256], writes=[b_t256])
            S.add("act", ACTV(dt_all[:].rearrange("p c h -> p (c h)"), tmp256, AF.Ln, bias=1.0), reads=[b_t256], writes=[b_dt])
            S.add("dve", TT(a_all[:], dt_all[:], A_bc[:, :].unsqueeze(1).broadcast_to([128, 16, 16]), ALU.mult),
                  reads=[b_dt, b_A], writes=[b_a])
            S.add("dve", MS(zer[0:16, :], 0.0), writes=[b_zer])
            for tq in range(4):
                bank = 3 + tq
                for kc in range(8):
                    S.add("pe", MM(ps[bank][0:16, :], Wdtf[:, kc, 16:32], hT[:, kc, tq * 512:(tq + 1) * 512], kc == 0, kc == 7),
                          reads=[hTq[tq], b_Wdtf], writes=pbs(bank), partial=(kc > 0))
                S.add("act", ACTV(fe[0:16, :], ps[bank][0:16, :], AF.Exp, bias=nfb[:, 0:1], scale=-1.0),
                      reads=pbs(bank) + [b_nfb], writes=[b_fe])
                S.add("act", ACTV(fe[0:16, :], fe[0:16, :], AF.Ln, bias=1.0), reads=[b_fe], writes=[b_fe])
                cur, prv = csq[tq % 2], csq[(tq + 1) % 2]
                bcur, bprv = b_csq[tq % 2], b_csq[(tq + 1) % 2]
                if tq == 0:
                    S.add("dve", SCAN(cur[0:16, :], fe[0:16, :], zer[0:16, :], 0.0), reads=[b_fe, b_zer], writes=[bcur])
                else:
                    S.add("dve", SCAN(cur[0:16, :], fe[0:16, :], zer[0:16, :], prv[0:16, 511:512]),
                          reads=[b_fe, b_zer, bprv], writes=[bcur])
                S.add("dve", TS(SPq[0:16, 0, :], cur[0:16, :], 8.0, None, ALU.mult), reads=[bcur], writes=[b_SPq])
                S.add("dve", STT(rr[0:16, :], cur[0:16, :], 8.0, SPq[0:16, 0, :], ALU.mult, ALU.subtract),
                      reads=[bcur, b_SPq], writes=[b_rr])
                S.add("dve", CP(SPq[0:16, 1, :], rr[0:16, :]), reads=[b_rr], writes=[b_SPq], partial=True)
                S.add("dve", TT(rr[0:16, :], rr[0:16, :], SPq[0:16, 1, :], ALU.subtract), reads=[b_rr, b_SPq], writes=[b_rr])
                S.add("dve", CP(SPq[0:16, 2, :], rr[0:16, :]), reads=[b_rr], writes=[b_SPq], partial=True)
                S.add("sp", DMA(cs8[:, :, tq * 512:(tq + 1) * 512], SPq[0:16, :, :]),
                      reads=[b_SPq], writes=[b_cs8], dma=True, dmabuf=gb("SPq_st"), partial=(tq > 0))
                for kc in range(8):
                    S.add("pe", MM(ps[7][0:16, :], Wdtf[:, kc, 0:16], hT[:, kc, tq * 512:(tq + 1) * 512], kc == 0, kc == 7),
                          reads=[hTq[tq], b_Wdtf], writes=[pb[7]], partial=(kc > 0))
                S.add("act", ACTV(ae[0:16, :], ps[7][0:16, :], AF.Exp, bias=dtb_p[:, 0:1]), reads=[pb[7], b_dtbp], writes=[b_ae])
                S.add("act", ACTV(ae[0:16, :], ae[0:16, :], AF.Ln, bias=1.0), reads=[b_ae], writes=[b_ae])
                S.add("dve", TS(ae[0:16, :], ae[0:16, :], A_p[:, 0:1], None, ALU.mult), reads=[b_ae, b_Ap], writes=[b_ae])
                acur, aprv = acs[tq % 2], acs[(tq + 1) % 2]
                bacur, baprv = b_acs[tq % 2], b_acs[(tq + 1) % 2]
                if tq == 0:
                    S.add("dve", SCAN(acur[0:16, :], ae[0:16, :], zer[0:16, :], 0.0), reads=[b_ae, b_zer], writes=[bacur])
                    S.add("dve", MS(bs4[0:16, 0:1], 0.0), writes=[b_bs4])
                else:
                    S.add("dve", SCAN(acur[0:16, :], ae[0:16, :], zer[0:16, :], aprv[0:16, 511:512]),
                          reads=[b_ae, b_zer, baprv], writes=[bacur])
                    S.add("dve", CP(bs4[0:16, 0:1], aprv[0:16, 511:512]), reads=[baprv], writes=[b_bs4])
                S.add("dve", CP(bs4[0:16, 1:4], acur[0:16, :].rearrange("p (c l) -> p c l", c=4)[:, 0:3, 127]),
                      reads=[bacur], writes=[b_bs4], partial=True)
                S.add("dve", TT(acm[0:16, :].rearrange("p (c l) -> p c l", c=4), acur[0:16, :].rearrange("p (c l) -> p c l", c=4),
                                bs4[0:16, 0:4].unsqueeze(2).broadcast_to([16, 4, 128]), ALU.subtract),
                      reads=[bacur, b_bs4], writes=[b_acm])
                S.add("dve", CP(SA[0:16, 0, :], acm[0:16, :]), reads=[b_acm], writes=[b_SA])
                S.add("dve", TT(rr2[0:16, :], acm[0:16, :], SA[0:16, 0, :], ALU.subtract), reads=[b_acm, b_SA], writes=[b_rr2])
                S.add("dve", CP(SA[0:16, 1, :], rr2[0:16, :]), reads=[b_rr2], writes=[b_SA], partial=True)
                S.add("dve", TT(rr2[0:16, :], rr2[0:16, :], SA[0:16, 1, :], ALU.subtract), reads=[b_rr2, b_SA], writes=[b_rr2])
                S.add("dve", CP(SA[0:16, 2, :], rr2[0:16, :]), reads=[b_rr2], writes=[b_SA], partial=True)
                S.add("dve", TS(SA[0:16, 3:6, :], SA[0:16, 0:3, :], -1.0, None, ALU.mult), reads=[b_SA], writes=[b_SA], partial=True)
                S.add("sp", DMA(ac6[:, :, tq * 512:(tq + 1) * 512], SA[0:16, :, :]),
                      reads=[b_SA], writes=[b_ac6], dma=True, dmabuf=gb("SA_st"), partial=(tq > 0))
            S.barrier()

            yTu = yT[:, 8:16, :].rearrange("p a b -> p (a b)")

            def carve2(off, shape, dt):
                esz = 2 if dt == BF16 else 4
                n = int(np.prod(shape))
                assert off % 4 == 0 and off + n * esz <= 32768, (off, shape)
                v = yTu[:, off // 2: off // 2 + n * esz // 2]
                if dt != BF16:
                    v = v.bitcast(dt)
                if len(shape) == 2:
                    v = v.rearrange("p (a b) -> p a b", a=shape[0])
                return v

            O = 18432
            Wg = carve(O, [8, 1280], BF16); O += 20480
            gz_r = [carve(O + i * 4096, [4, 512], BF16) for i in range(2)]; O += 8192
            pexb = [carve(O + i * 1040, [520], BF16) for i in range(2)]; O += 2080
            xbc_r = [carve(O + i * 6144, [6, 512], BF16) for i in range(2)]; O += 12288
            hlb = carve(O, [6, 4], BF16); O += 64
            Dg = carve(O, [24, 128], BF16); O += 6144
            prevF = carve(O, [512], F32); O += 2048
            prevB = carve(O, [512], BF16); O += 1024
            yg = carve(O, [4, 512], F32); O += 8192
            sq = carve(O, [4, 512], BF16); O += 4096
            rstd_bc = carve(O, [512], F32); O += 2048
            lnv_bc = carve(O, [512], F32); O += 2048
            LT_r = [carve(O + i * 2048, [8, 128], BF16) for i in range(2)]; O += 4096
            MT_r = [carve(O + i * 2048, [8, 128], BF16) for i in range(2)]; O += 4096
            tnb = [carve(O + i * 1024, [512], BF16) for i in range(2)]; O += 2048
            hvb = [carve(O + i * 1024, [512], BF16) for i in range(2)]; O += 2048
            b_tnb, b_hvb = [gb("tnb0"), gb("tnb1")], [gb("hvb0"), gb("hvb1")]
            Dd = carve(O, [8, 128], BF16); O += 2048
            dsp = carve(O, [16], BF16); O += 32
            dsr = carve(O, [8], F32); O += 32
            assert O <= ARENA_BYTES, O
            O2 = 0
            Ebc_r = [carve2(O2 + i * 4096, [8, 128], F32) for i in range(2)]; O2 += 8192
            CpT_r = [carve2(O2 + i * 2048, [8, 128], BF16) for i in range(2)]; O2 += 4096
            xdt_r = [carve2(O2 + i * 1024, [512], BF16) for i in range(2)]; O2 += 2048
            xdtD_r = [carve2(O2 + i * 1024, [512], BF16) for i in range(2)]; O2 += 2048
            Btok_r = [carve2(O2 + i * 256, [128], BF16) for i in range(2)]; O2 += 512
            CBm_r = [carve2(O2 + i * 256, [128], BF16) for i in range(2)]; O2 += 512
            Rr = carve2(O2, [8, 512], BF16); O2 += 8192
            assert O2 <= 32768
            b_Rr = gb("Rr")
            b_Wg, b_hl, b_Dg = gb("Wg"), gb("hl"), gb("Dg")
            b_gz = [gb("gz0"), gb("gz1")]
            b_xbc = [gb("xbcT0"), gb("xbcT1")]
            b_pex = [gb("pex0"), gb("pex1")]
            b_prevF, b_prevB, b_yg, b_sq, b_rstd, b_lnv = (gb(n) for n in ["prevF", "prevB", "yg", "sq", "rstd_bc", "lnv_bc"])

            def r2(nm):
                return [gb(nm + "0"), gb(nm + "1")]
            b_LT, b_MT, b_Ebc, b_CpT, b_xdt, b_xdtD, b_Btok, b_CBm = (
                r2(n) for n in ["LT", "MT", "Ebc", "CpT", "xdt", "xdtD", "Btok", "CBm"])
            pexi = 0
            tni = 0
            pbk = 0
            h8 = "p (h q) -> p h q"
            for g in range(2):
                S.add("sp", DMA(Wg, W1v[:, :, g * 1280:(g + 1) * 1280]), reads=[b_W1g[g]], writes=[b_Wg], dma=True, dmabuf=b_Wg)
                S.add("dve", MS(hlb, 0.0), writes=[b_hl])
                S.add("dve", MS(prevF, 0.0), writes=[b_prevF])
                S.add("dve", MS(prevB, 0.0), writes=[b_prevB])
                for ci in range(6):
                    cc = (g * 4 + ci) if ci < 4 else (8 + g if ci == 4 else 10 + g)
                    for k in range(4):
                        S.add("dve", TS(Dg[:, ci * 4 + k, :], ident[:, :], cw[:, cc, k:k + 1], None, ALU.mult),
                              reads=[b_ident, b_cw], writes=[b_Dg], partial=not (ci == 0 and k == 0))
                b_dsp, b_Dd = gb("dsp"), gb("Dd")
                S.add("dve", CP(dsp[:, 0:4], dsk[:, g * 4:(g + 1) * 4]), reads=[b_dsk], writes=[b_dsp])
                S.add("dve", TT(dsr[:, 0:4], dsk[:, g * 4:(g + 1) * 4], dsp[:, 0:4], ALU.subtract), reads=[b_dsk, b_dsp], writes=[gb("dsr")])
                S.add("dve", CP(dsp[:, 4:8], dsr[:, 0:4]), reads=[gb("dsr")], writes=[b_dsp], partial=True)
                for ec in range(4):
                    for j in range(2):
                        S.add("dve", TS(Dd[:, ec * 2 + j, :], ident[:, :], dsp[:, j * 4 + ec:j * 4 + ec + 1], None, ALU.mult),
                              reads=[b_ident, b_dsp], writes=[b_Dd], partial=not (ec == 0 and j == 0))

                def emit_inproj(tq, lo=0, hi=10):
                    nonlocal pexi, pbk, tni
                    q2 = tq % 2
                    tsl = slice(tq * 512, (tq + 1) * 512)
                    xbcT, gz_t = xbc_r[q2], gz_r[q2]
                    order = [(4 + j, j) for j in range(6)] + [(j, None) for j in range(4)]
                    deferred = []
                    for blk, ci in order[lo:hi]:
                        bank = pbk % 3
                        pbk += 1
                        for kc in range(8):
                            S.add("pe", MM(ps[bank][:, :], Wg[:, kc, blk * 128:(blk + 1) * 128], hT[:, kc, tsl], kc == 0, kc == 7),
                                  reads=[b_Wg, hTq[tq]], writes=[pb[bank]], partial=(kc > 0))
                        ti = tni % 2
                        tni += 1
                        if ci is None:
                            S.add("act", ACTV(tnb[ti], ps[bank][:, :], AF.Tanh, scale=0.5), reads=[pb[bank]], writes=[b_tnb[ti]])
                            S.add("act", ACTV(hvb[ti], ps[bank][:, :], AF.Copy, scale=0.5), reads=[pb[bank]], writes=[b_hvb[ti]])
                            for f_ in deferred:
                                f_()
                            deferred = [lambda ti=ti, blk=blk: S.add("dve", STT(gz_t[:, blk, :], tnb[ti], 1.0, hvb[ti], ALU.add, ALU.mult),
                                                                   reads=[b_tnb[ti], b_hvb[ti]], writes=[b_gz[q2]], partial=(blk > 0))]
                            continue
                        cc = (g * 4 + ci) if ci < 4 else (8 + g if ci == 4 else 10 + g)
                        pi = pexi % 2
                        pexi += 1
                        px, bpx = pexb[pi], b_pex[pi]
                        S.add("dve", CP(px[:, 0:3], hlb[:, ci, 0:3]), reads=[b_hl], writes=[bpx])
                        S.add("act", ACTV(px[:, 3:515], ps[bank][:, :], AF.Copy), reads=[pb[bank]], writes=[bpx], partial=True)
                        bank2 = pbk % 3
                        pbk += 1
                        for k in range(4):
                            S.add("pe", MM(ps[bank2][:, :], Dg[:, ci * 4 + k, :], px[:, k:k + 512], k == 0, k == 3),
                                  reads=[bpx, b_Dg], writes=[pb[bank2]], partial=(k > 0))
                        S.add("act", ACTV(tnb[ti], ps[bank2][:, :], AF.Tanh, bias=cbh[:, cc:cc + 1], scale=0.5),
                              reads=[pb[bank2], b_cbh], writes=[b_tnb[ti]])
                        S.add("act", ACTV(hvb[ti], ps[bank2][:, :], AF.Identity, bias=cbh[:, cc:cc + 1], scale=0.5),
                              reads=[pb[bank2], b_cbh], writes=[b_hvb[ti]])
                        for f_ in deferred:
                            f_()
                        deferred = [
                            lambda px=px, bpx=bpx, ci=ci: S.add("dve", CP(hlb[:, ci, 0:3], px[:, 512:515]), reads=[bpx], writes=[b_hl], partial=True),
                            lambda ti=ti, ci=ci: S.add("dve", STT(xbcT[:, ci, :], tnb[ti], 1.0, hvb[ti], ALU.add, ALU.mult),
                                                       reads=[b_tnb[ti], b_hvb[ti]], writes=[b_xbc[q2]], partial=(ci > 0))]
                    for f_ in deferred:
                        f_()

                def emit_rows(tq):
                    tsl = slice(tq * 512, (tq + 1) * 512)
                    import os as _os3
                    if _os3.environ.get("KDBG_NOROWDMA") == "1":
                        return
                    if tq == 0:
                        S.add("pool", MS(Rr[0:35, :, :], 0.0), writes=[b_Rr])
                    S.add("sp", DMA(Rr[0:3, :, :], ac6[g * 8:(g + 1) * 8, 0:3, tsl].rearrange("h k t -> k h t")),
                          reads=[b_ac6], writes=[b_Rr], dma=True, dmabuf=b_Rr)
                    S.add("sp", DMA(Rr[32:35, :, :], ac6[g * 8:(g + 1) * 8, 3:6, tsl].rearrange("h k t -> k h t")),
                          reads=[b_ac6], writes=[b_Rr], dma=True, dmabuf=b_Rr)

                def emit_S1(c):
                    tq, cl, k = c // 4, c % 4, c % 2
                    q2 = tq % 2
                    xbcT = xbc_r[q2]
                    csl = slice(cl * 128, (cl + 1) * 128)
                    dt_c = dt_all[:, c, g * 8:(g + 1) * 8]
                    LT, MT, Ebc, CpT = LT_r[k], MT_r[k], Ebc_r[k], CpT_r[k]
                    xdt, xdtD, Btok, CBm = xdt_r[k], xdtD_r[k], Btok_r[k], CBm_r[k]
                    if cl == 0:
                        emit_rows(tq)
                    for hh in range(2):
                        rv = Rr[0:35, hh * 4:(hh + 1) * 4, csl]
                        p3v = ps[3][:, :].rearrange("p (a b) -> p a b", a=4)
                        p4v = ps[4][:, :].rearrange("p (a b) -> p a b", a=4)
                        S.add("pe", MM(p3v, selA[0:35, :], rv, True, False), reads=[b_Rr, b_sel], writes=[pb[3]])
                        S.add("pe", MM(p4v, selA[0:35, :], rv, True, True), reads=[b_Rr, b_sel], writes=pbs(4))
                        for h4 in range(4):
                            hd = hh * 4 + h4
                            osl = slice(h4 * 128, (h4 + 1) * 128)
                            S.add("pe", MM(ps[3][:, osl], Rr[0:35, hd, csl], selB[0:35, :], False, False),
                                  reads=[b_Rr, b_sel], writes=[pb[3]], partial=True)
                        S.add("pe", MM(p3v, ident[:, :], mneg4[:, :, :], False, True), reads=[b_ident, b_mneg4], writes=[pb[3]], partial=True)
                        S.add("act", ACTV(LT[:, hh * 4:(hh + 1) * 4, :].rearrange("p a b -> p (a b)"), ps[3][:, :], AF.Exp),
                              reads=[pb[3]], writes=[b_LT[k]], partial=(hh > 0))
                        S.add("act", ACTV(Ebc[:, hh * 4:(hh + 1) * 4, :].rearrange("p a b -> p (a b)"), ps[4][:, :], AF.Exp),
                              reads=pbs(4), writes=[b_Ebc[k]], partial=(hh > 0))
                    psT = ps[5][:].bitcast(BF16)
                    for xi in range(5):
                        S.add("pe", TR(psT[:, xi * 128:(xi + 1) * 128], xbcT[:, xi, csl], ident[:]),
                              reads=[b_xbc[q2], b_ident], writes=[pb[5]], partial=(xi > 0))
                    S.add("pe", MM(ps[5][:, 384:512], xbcT[:, 4, csl], xbcT[:, 5, csl]), reads=[b_xbc[q2]], writes=[pb[5]], partial=True)
                    S.add("dve", TT(xdt.rearrange(h8, h=8), psT[:, 0:512].rearrange(h8, h=8),
                                    dt_c.unsqueeze(2).broadcast_to([128, 8, 64]), ALU.mult),
                          reads=[pb[5], b_dt], writes=[b_xdt[k]])
                    S.add("dve", CP(Btok, psT[:, 512:640]), reads=[pb[5]], writes=[b_Btok[k]])
                    S.add("dve", TT(CBm, ps[5][:, 384:512], U2f[:, :], ALU.mult), reads=[pb[5], b_U2f], writes=[b_CBm[k]])
                    S.add("dve", TT(xdtD.rearrange(h8, h=8), xdt.rearrange(h8, h=8),
                                    LT[:, :, 127:128].broadcast_to([128, 8, 64]), ALU.mult),
                          reads=[b_xdt[k], b_LT[k]], writes=[b_xdtD[k]])
                    S.add("dve", TT(MT, LT, CBm.unsqueeze(1).broadcast_to([128, 8, 128]), ALU.mult),
                          reads=[b_LT[k], b_CBm[k]], writes=[b_MT[k]])
                    S.add("dve", TT(CpT, Ebc, xbcT[:, 5, csl].unsqueeze(1).broadcast_to([128, 8, 128]), ALU.mult),
                          reads=[b_Ebc[k], b_xbc[q2]], writes=[b_CpT[k]])

                def emit_S2(c):
                    tq, cl, k = c // 4, c % 4, c % 2
                    q2 = tq % 2
                    xbcT, gz_t = xbc_r[q2], gz_r[q2]
                    csl = slice(cl * 128, (cl + 1) * 128)
                    MT, Ebc, CpT = MT_r[k], Ebc_r[k], CpT_r[k]
                    xdt, xdtD, Btok = xdt_r[k], xdtD_r[k], Btok_r[k]
                    S.add("pe", MM(ps[7][:, :], Btok, xdtD), reads=[b_Btok[k], b_xdtD[k]], writes=[pb[7]])
                    for pr in range(4):
                        psl = slice(pr * 128, (pr + 1) * 128)
                        for j in range(2):
                            S.add("pe", MM(ps[6][:, psl], Dd[:, pr * 2 + j, :], xbcT[:, pr, csl], j == 0, False),
                                  reads=[b_xbc[q2], b_Dd], writes=[pb[6]], partial=not (pr == 0 and j == 0))
                        for hx in range(2):
                            hd, r0 = pr * 2 + hx, hx * 64
                            S.add("pe", MM(ps[6][r0:r0 + 64, psl], xdt[:, hd * 64:(hd + 1) * 64], MT[:, hd, :], False, False),
                                  reads=[b_xdt[k], b_MT[k]], writes=[pb[6]], partial=True)
                            S.add("pe", MM(ps[6][r0:r0 + 64, psl], prevB[:, hd * 64:(hd + 1) * 64], CpT[:, hd, :], False, True),
                                  reads=[b_prevB, b_CpT[k]], writes=[pb[6]], partial=True)
                    S.add("dve", TT(prevF.rearrange(h8, h=8), prevF.rearrange(h8, h=8),
                                    Ebc[:, :, 127:128].broadcast_to([128, 8, 64]), ALU.mult),
                          reads=[b_prevF, b_Ebc[k]], writes=[b_prevF])
                    S.add("dve", TT(prevF, prevF, ps[7][:, :], ALU.add), reads=[b_prevF, pb[7]], writes=[b_prevF])
                    S.add("pool", CP(prevB, prevF), reads=[b_prevF], writes=[b_prevB])
                    S.add("dve", TT(yg[:, :, csl], ps[6][:, :].rearrange("p (a b) -> p a b", a=4), gz_t[:, :, csl], ALU.mult),
                          reads=[pb[6], b_gz[q2]], writes=[b_yg], partial=True)

                def emit_norm(tq):
                    tsl = slice(tq * 512, (tq + 1) * 512)
                    for ec in range(4):
                        S.add("act", ACTV(sq[:, ec, :], yg[:, ec, :], AF.Square), reads=[b_yg], writes=[b_sq], partial=(ec > 0))
                    for ec in range(4):
                        S.add("pe", MM(ps[7][:, :], onesb[:, :], sq[:, ec, :], ec == 0, ec == 3),
                              reads=[b_sq, b_ones], writes=[pb[7]], partial=(ec > 0))
                    S.add("act", ACTV(lnv_bc, ps[7][:, :], AF.Ln, bias=epsb[:], scale=1.0 / 512), reads=[pb[7], b_eps], writes=[b_lnv])
                    S.add("act", ACTV(rstd_bc, lnv_bc, AF.Exp, scale=-0.5), reads=[b_lnv], writes=[b_rstd])
                    for ec in range(4):
                        e_ = g * 4 + ec
                        S.add("dve", STT(yT[:, e_, tsl], yg[:, ec, :], snw[:, e_:e_ + 1], rstd_bc, ALU.mult, ALU.mult),
                              reads=[b_yg, b_rstd, b_snw], writes=[yTb[e_][tq]])

                import os as _os
                _nointer = _os.environ.get("KDBG_NOINTER") == "1"
                emit_inproj(0)
                pieces = [(0, 3), (3, 6), (6, 8), (8, 10)]
                for c in range(NT + 1):
                    S.record()
                    if c < NT:
                        emit_S1(c)
                    strA1 = S.stop()
                    S.record()
                    if c >= 1:
                        emit_S2(c - 1)
                        if (c - 1) % 4 == 3:
                            emit_norm((c - 1) // 4)
                    strA2 = S.stop()
                    strA = strA1 + strA2
                    S.record()
                    if c < NT and c // 4 + 1 < 4:
                        lo_, hi_ = pieces[c % 4]
                        emit_inproj(c // 4 + 1, lo_, hi_)
                    strB = S.stop()
                    if c % 4 == 0:
                        S.replay_merged(strA, [])
                        S.replay_merged([], strB)
                    else:
                        S.replay_merged(strA, strB)
            if b == 0:
                dbg_store("yssd", yT[:, 0:8, :], [yTb[e][q] for e in range(8) for q in range(4)])
            S.barrier()

            O = 0
            Whp = [carve(O + i * 6144, [8, 384], BF16) for i in range(2)]; O += 12288
            Qa = [[carve(O + (s * 2 + hd) * 4096, [T], BF16) for hd in range(2)] for s in range(2)]; O += 16384
            Ka = [[carve(O + (s * 2 + hd) * 4096, [T], BF16) for hd in range(2)] for s in range(2)]; O += 16384
            Va = [carve(O + s * 8192, [16, 2, 128], BF16) for s in range(2)]
            Va3 = [carve(O + s * 8192, [32, 128], BF16) for s in range(2)]; O += 16384
            PT = [carve(O + i * 1024, [512], BF16) for i in range(3)]; O += 3072
            rec = [carve(O + i * 2048, [512], F32) for i in range(2)]; O += 4096
            WoutT = carve(O, [16, 1024], BF16); O += 32768
            assert O <= ARENA_BYTES
            b_Whp = [gb("Whp0"), gb("Whp1")]
            b_Qa = [[gb(f"Qa{s}{hd}") for hd in range(2)] for s in range(2)]
            b_Ka = [[gb(f"Ka{s}{hd}") for hd in range(2)] for s in range(2)]
            b_Va = [gb("Va0"), gb("Va1")]
            b_PT = [gb(f"PT{i}") for i in range(3)]
            b_rec = [gb("rec0"), gb("rec1")]
            b_WoutT = gb("WoutT")
            S.add("sp", DMA(WoutT, Wov), reads=[b_Wo], writes=[b_WoutT], dma=True, dmabuf=b_WoutT)
            pti = 0
            poi = 0
            def c_inproj(hp):
                s = hp % 2
                S.add("sp", DMA(Whp[s], W1v[:, :, 2560 + hp * 384: 2560 + (hp + 1) * 384]),
                      reads=[b_W1hp[hp]], writes=[b_Whp[s]], dma=True, dmabuf=b_Whp[s])
                for hd in range(2):
                    head = 2 * hp + hd
                    S.add("pool", MS(Ka[s][hd][64:70, :], -1.0), writes=[b_Ka[s][hd]])
                    S.add("pool", MS(Qa[s][hd][64:70, :], 1.0), writes=[b_Qa[s][hd]])
                    S.add("sp", DMA(Ka[s][hd][64:67, :], cs8[head:head + 1, :, :]),
                          reads=[b_cs8], writes=[b_Ka[s][hd]], dma=True, dmabuf=b_Ka[s][hd])
                    S.add("sp", DMA(Qa[s][hd][67:70, :], cs8[head:head + 1, :, :]),
                          reads=[b_cs8], writes=[b_Qa[s][hd]], dma=True, dmabuf=b_Qa[s][hd])
                S.add("pool", MS(Va3[s][:, :, 64:128], 1.0), writes=[b_Va[s]])
                nb = 0
                for which, dst, bdst in ((0, Qa[s], b_Qa[s]), (1, Ka[s], b_Ka[s])):
                    for tq in range(4):
                        bank = nb % 2
                        nb += 1
                        tsl = slice(tq * 512, (tq + 1) * 512)
                        for kc in range(8):
                            S.add("pe", MM(ps[bank][:, :], Whp[s][:, kc, which * 128:(which + 1) * 128], hT[:, kc, tsl], kc == 0, kc == 7),
                                  reads=[b_Whp[s], hTq[tq]], writes=[pb[bank]], partial=(kc > 0))
                        for hd in range(2):
                            S.add("dve", CP(dst[hd][0:64, tsl], ps[bank][hd * 64:(hd + 1) * 64, :]),
                                  reads=[pb[bank]], writes=[bdst[hd]], partial=True)
                vT = yT[:, 8 + hp, :]
                for tq in range(4):
                    bank = nb % 2
                    nb += 1
                    tsl = slice(tq * 512, (tq + 1) * 512)
                    for kc in range(8):
                        S.add("pe", MM(ps[bank][:, :], Whp[s][:, kc, 256:384], hT[:, kc, tsl], kc == 0, kc == 7),
                              reads=[b_Whp[s], hTq[tq]], writes=[pb[bank]], partial=(kc > 0))
                    S.add("dve", CP(vT[:, tsl], ps[bank][:, :]), reads=[pb[bank]], writes=[yTb[8 + hp][tq]])
                psTv = ps[2][:].bitcast(BF16)
                for i0 in range(0, NT, 8):
                    for ii in range(8):
                        i = i0 + ii
                        S.add("pe", TR(psTv[:, ii * 128:(ii + 1) * 128], vT[:, i * 128:(i + 1) * 128], ident[:]),
                              reads=[yTb[8 + hp][i // 4], b_ident], writes=[pb[2]], partial=(ii > 0))
                    for hd in range(2):
                        S.add("dve", CP(Va[s][:, i0:i0 + 8, hd, 0:64],
                                        psTv.rearrange("p (a c d) -> p a c d", a=8, c=2)[:, :, hd, :]),
                              reads=[pb[2]], writes=[b_Va[s]], partial=True)

            def c_attn(hp):
                nonlocal pti, poi
                s = hp % 2
                steps = [(hd, J, kb) for hd in range(2) for J in range(4) for kb in range(4 * J + 4)]
                infos = {}
                LOOK = 2
                for n_ in range(len(steps) + LOOK):
                    if n_ < len(steps):
                        hd, J, kb = steps[n_]
                        pi = pti % 3
                        pti += 1
                        bank = 3 + pi
                        r = kb - 4 * J
                        c0 = max(r, 0) * 128
                        K_ = Ka[s][hd][0:70, kb * 128:(kb + 1) * 128]
                        Qt = Qa[s][hd]
                        rd = [b_Ka[s][hd], b_Qa[s][hd]]
                        if r < 0:
                            S.add("pe", MM(ps[bank][:, 0:512], K_, Qt[0:70, J * 512:(J + 1) * 512]), reads=rd, writes=pbs(bank))
                        else:
                            q0 = J * 512 + c0
                            S.add("pe", MM(ps[bank][:, c0:c0 + 128], K_, Qt[0:70, q0:q0 + 128], True, False), reads=rd, writes=pbs(bank))
                            S.add("pe", MM(ps[bank][:, c0:c0 + 128], ident[:, :], mneg[:, :], False, True),
                                  reads=[b_ident, b_mneg], writes=pbs(bank), partial=True)
                            if c0 + 128 < 512:
                                S.add("pe", MM(ps[bank][:, c0 + 128:512], K_, Qt[0:70, q0 + 128:J * 512 + 512]),
                                      reads=rd, writes=pbs(bank), partial=True)
                        S.add("act", ACTV(PT[pi][:, c0:512], ps[bank][:, c0:512], AF.Exp, scale=0.125), reads=pbs(bank), writes=[b_PT[pi]])
                        infos[n_] = (pi, c0)
                    if n_ >= LOOK:
                        hd, J, kb = steps[n_ - LOOK]
                        pi, c0 = infos[n_ - LOOK]
                        if kb == 0:
                            poi += 1
                        ob = 6 + (poi % 2)
                        last = (kb == 4 * J + 3)
                        S.add("pe", MM(ps[ob][:, c0:512], Va[s][:, kb, hd, :], PT[pi][:, c0:512], kb == 0, last),
                              reads=[b_Va[s], b_PT[pi]], writes=[pb[ob]], partial=(kb > 0))
                        if last:
                            ri = poi % 2
                            r0 = hd * 64
                            if (J % 2) == 1:
                                S.add("act", ACTV(rec[ri][64:128, :], ps[ob][64:128, :], AF.Ln), reads=[pb[ob]], writes=[b_rec[ri]])
                                S.add("act", ACTV(rec[ri][64:128, :], rec[ri][64:128, :], AF.Exp, scale=-1.0), reads=[b_rec[ri]], writes=[b_rec[ri]])
                            else:
                                S.add("dve", RCP(rec[ri][64:128, :], ps[ob][64:128, :]), reads=[pb[ob]], writes=[b_rec[ri]])
                            S.add("dve", TT(yT[r0:r0 + 64, 8 + hp, J * 512:(J + 1) * 512], ps[ob][0:64, :], rec[ri][64:128, :], ALU.mult),
                                  reads=[pb[ob], b_rec[ri]], writes=[yTb[8 + hp][J]], partial=True)

            c_inproj(0)
            for hp in range(8):
                S.record()
                c_attn(hp)
                strA = S.stop()
                S.record()
                if hp < 7:
                    c_inproj(hp + 1)
                strB = S.stop()
                S.replay_merged(strA, strB)
            if b == 0:
                dbg_store("yatt", yT[:, 8:16, :], [yTb[e][q] for e in range(8, 16) for q in range(4)])
            S.barrier()

            xr = ring("xr", [carve(i * 4096, [1024], F32) for i in range(3)])
            xnr = ring("xn", [carve(12288 + i * 2048, [1024], BF16) for i in range(2)])
            JUNK[0] = carve(16384, [1024], BF16)
            h1r = ring("h1t", [carve(18432 + i * 4096, [1024], F32) for i in range(3)])
            pend_nt = None
            for i in range(NT):
                xs = xr.next()
                S.add("sp", DMA(xs.t, x_d[b, i * 128:(i + 1) * 128, :]), writes=[xs.b], dma=True, dmabuf=xs.b)
                hs = h1r.next()
                if pend_nt is not None:
                    nt_a(pend_nt[0], pend_nt[1], pend_nt[3])
                for half in range(2):
                    bank = (i % 2) * 2 + half
                    hsl = slice(half * 512, (half + 1) * 512)
                    for ec in range(16):
                        S.add("pe", MM(ps[bank][:, :], yT[:, ec, i * 128:(i + 1) * 128], WoutT[:, ec, hsl], ec == 0, ec == 15),
                              reads=[yTb[ec][i // 4], b_WoutT], writes=[pb[bank]], partial=(ec > 0))
                    S.add("dve", TT(hs.t[:, hsl], ps[bank][:, :], xs.t[:, hsl], ALU.add),
                          reads=[pb[bank], xs.b], writes=[hs.b], partial=(half > 0))
                if pend_nt is not None:
                    nt_b(pend_nt[2], wn2[:, :], b_wn2, pend_nt[3], pend_nt[4])
                S.add("act", DMA(out_d[b, i * 128:(i + 1) * 128, :], hs.t), reads=[hs.b], writes=[outb[i]], dma=True, dmabuf=hs.sd)
                pend_nt = (hs.t, hs.b, i, xnr.next(), 4 + (i % 2))
            nt_a(pend_nt[0], pend_nt[1], pend_nt[3])
            nt_b(pend_nt[2], wn2[:, :], b_wn2, pend_nt[3], pend_nt[4])
            S.barrier()

            O = 0
            Wupr = ring("Wup", [carve(O + i * 8192, [8, 512], BF16) for i in range(2)]); O += 16384
            Wdnr = ring("Wdn", [carve(O + i * 8192, [4, 1024], BF16) for i in range(3)]); O += 24576
            uT = carve(O, [32, 512], BF16); O += 32768
            h1l = ring("h1l", [carve(O + i * 4096, [1024], F32) for i in range(4)]); O += 16384
            otr = ring("ot", [carve(O + i * 4096, [1024], F32) for i in range(2)]); O += 8192
            rtr = ring("rt", [carve(O + i * 1024, [512], BF16) for i in range(2)]); O += 2048
            JUNK[0] = carve(O, [1024], BF16); O += 2048
            assert O <= ARENA_BYTES
            b_uT = [gb(f"uT{i}") for i in range(32)]
            for tg in range(4):
                tsl = slice(tg * 512, (tg + 1) * 512)
                nb = 0
                h1s = []
                for ti in range(4):
                    i = tg * 4 + ti
                    hs = h1l.next()
                    h1s.append(hs)
                    S.add("act", DMA(hs.t, out_d[b, i * 128:(i + 1) * 128, :]), reads=[outb[i]], writes=[hs.b], dma=True, dmabuf=hs.b)
                for fg in range(8):
                    ws = Wupr.next()
                    S.add("sp", DMA(ws.t, Wuv[:, :, fg * 512:(fg + 1) * 512]), reads=[b_Wu], writes=[ws.b], dma=True, dmabuf=ws.b)
                    for fj in range(4):
                        fc = fg * 4 + fj
                        bank = nb % 8
                        nb += 1
                        for kc in range(8):
                            S.add("pe", MM(ps[bank][:, :], ws.t[:, kc, fj * 128:(fj + 1) * 128], hT[:, kc, tsl], kc == 0, kc == 7),
                                  reads=[ws.b, hTq[tg]], writes=pbs(bank), partial=(kc > 0))
                        rs = rtr.next()
                        S.add("act", ACTV(rs.t, ps[bank][:, :], AF.Relu), reads=pbs(bank), writes=[rs.b])
                        S.add("pool", TT(uT[:, fc, :], rs.t, rs.t, ALU.mult), reads=[rs.b], writes=[b_uT[fc]])
                for fg in range(8):
                    ws = Wdnr.next()
                    S.add("sp", DMA(ws.t, Wdv[:, fg * 4:(fg + 1) * 4, :]), reads=[b_Wd], writes=[ws.b], dma=True, dmabuf=ws.b)
                    for ti in range(4):
                        for half in range(2):
                            bank = ti * 2 + half
                            for fj in range(4):
                                fc = fg * 4 + fj
                                S.add("pe", MM(ps[bank][:, :], uT[:, fc, ti * 128:(ti + 1) * 128], ws.t[:, fj, half * 512:(half + 1) * 512],
                                               fc == 0, fc == 31),
                                      reads=[ws.b, b_uT[fc]], writes=pbs(bank), partial=(fc > 0))
                for ti in range(4):
                    i = tg * 4 + ti
                    hs = h1s[ti]
                    for half in range(2):
                        bank = ti * 2 + half
                        hsl = slice(half * 512, (half + 1) * 512)
                        S.add("dve", TT(hs.t[:, hsl], ps[bank][:, :], hs.t[:, hsl], ALU.add), reads=pbs(bank) + [hs.b], writes=[hs.b])
                    rstd, sbuf_ = rms_stats(hs.t, hs.b)
                    os_ = otr.next()
                    S.add("dve", STT(os_.t, hs.t, rstd, nfw[:, :], ALU.mult, ALU.mult), reads=[hs.b, sbuf_, b_nfw], writes=[os_.b])
                    S.add("act", DMA(out_d[b, i * 128:(i + 1) * 128, :], os_.t), reads=[os_.b, outb[i]], writes=[outb[i]],
                          dma=True, dmabuf=os_.sd)
            S.barrier()

        run_sched(nc, S)
    return nc


_NC_CACHE = {}


def kernel(**inputs):
    x = np.ascontiguousarray(inputs["x"], dtype=np.float32)
    nb = x.shape[0]
    per = nb // NCORES
    if per not in _NC_CACHE:
        _NC_CACHE[per] = build_nc(per)
    nc = _NC_CACHE[per]
    names = ["norm_mix_w", "w_in", "conv_w", "conv_b", "dt_bias", "a_log", "d_skip", "ssd_norm_w", "f_bias",
             "w_out", "norm_mlp_w", "w_up", "w_down", "norm_final_w"]
    shared = {n: np.ascontiguousarray(inputs[n], dtype=np.float32) for n in names}
    in_maps = []
    for c in range(NCORES):
        m = dict(shared)
        m["x"] = np.ascontiguousarray(x[c * per:(c + 1) * per])
        in_maps.append(m)
    res = run_bass_kernel_spmd(nc, in_maps, core_ids=list(range(NCORES)))
    return np.concatenate([r["out"] for r in res.results], axis=0).astype(np.float32)
```
